# Optimizing a Trainium2 kernel written in Bass

```python
import jax, jax.numpy as jnp
from jax import lax
import numpy as np

D_MODEL = 1024
BATCH = 16
SEQ = 256
DEPTH = 4
DEC_BATCH = 2
DEC_SEQ = 1024
PAST_LEN = 512

GRID_W = 64
D_BRANCH = 512
D_POOL = D_BRANCH
POOL_WINDOWS = (2, 4, 8, 16)
N_POOL_GROUPS = 4
POOL_GROUP = D_POOL // N_POOL_GROUPS
D_SGU = D_BRANCH
N_SGU_GROUPS = 4
SGU_GROUP = D_SGU // N_SGU_GROUPS
CHUNK = 128
D_MLSTM = D_BRANCH
N_MLSTM_HEADS = 4
HEAD_DIM = D_MLSTM // N_MLSTM_HEADS
MLSTM_CHUNK = 64
N_BRANCH = 3
D_FF = 4 * D_MODEL
EPS = 1e-6
IN_WIDTHS = (D_POOL, D_SGU, D_SGU, D_MLSTM, D_MLSTM, D_MLSTM, D_MLSTM, 4 * N_MLSTM_HEADS)
D_IN = D_POOL + 2 * D_SGU + 4 * D_MLSTM + 4 * N_MLSTM_HEADS + N_BRANCH * D_MODEL

kernel_name = "hybrid_pool_sgu_mlstm_diffusion_step"


def rmsnorm(x, g):
    x32 = x.astype(jnp.float32)
    y = x32 * lax.rsqrt(jnp.mean(x32 * x32, axis=-1, keepdims=True) + EPS)
    return (y * g.astype(jnp.float32)).astype(x.dtype)


def centred_mean(x, w, axis):
    n = x.shape[axis]
    s = jnp.cumsum(x.astype(jnp.float32), axis=axis)
    pad = [(0, 0)] * x.ndim
    pad[axis] = (1, 0)
    s = jnp.pad(s, pad)
    t = jnp.arange(n)
    lo = jnp.clip(t - w // 2, 0, n)
    hi = jnp.clip(t + (w - w // 2), 0, n)
    total = jnp.take(s, hi, axis=axis) - jnp.take(s, lo, axis=axis)
    shape = [1] * x.ndim
    shape[axis] = n
    cnt = (hi - lo).astype(jnp.float32).reshape(shape)
    return total / cnt


def pool_branch(xp, w_pool, pool_scale, grid):
    B, T, _ = xp.shape
    groups = jnp.split(xp, N_POOL_GROUPS, axis=-1)
    outs = []
    for g, win in zip(groups, POOL_WINDOWS):
        if grid:
            rows = T // GRID_W
            gg = g.reshape(B, rows, GRID_W, POOL_GROUP)
            pooled = centred_mean(centred_mean(gg, win, 2), win, 1).reshape(B, T, POOL_GROUP)
        else:
            pooled = centred_mean(g, win, 1)
        outs.append(pooled - g.astype(jnp.float32))
    d = jnp.stack(outs, axis=2).astype(xp.dtype)
    y = jnp.einsum('btgc,gcd->btgd', d, w_pool).reshape(B, T, D_POOL)
    return y * pool_scale


def sgu_branch(u, v, g_sgu, w_sgu, b_sgu):
    B, T, _ = u.shape
    vn = rmsnorm(v, g_sgu).reshape(B, T // CHUNK, CHUNK, N_SGU_GROUPS, SGU_GROUP)
    mixed = jnp.einsum('gpq,bnqgc->bnpgc', w_sgu, vn) + jnp.swapaxes(b_sgu, 0, 1)[None, None, :, :, None]
    return u * mixed.reshape(B, T, D_SGU)


def mlstm_scan(q, k, v, i_pre, logf, C0, n0, m0):
    B, T, H, Dh = q.shape
    L = MLSTM_CHUNK
    nc = T // L

    def to_chunks(a):
        return jnp.moveaxis(a.reshape(B, nc, L, *a.shape[2:]), 1, 0)

    xs = (to_chunks(q), to_chunks(k), to_chunks(v), to_chunks(i_pre), to_chunks(logf))
    causal = jnp.tril(jnp.ones((L, L), dtype=bool))

    def step(carry, xc):
        C, n, m = carry
        qc, kc, vc, ic, fc = xc
        b = jnp.cumsum(fc, axis=1)
        dmat = b[:, :, None, :] - b[:, None, :, :] + ic[:, None, :, :]
        dmat = jnp.where(causal[None, :, :, None], dmat, -jnp.inf)
        inter = b + m[:, None, :]
        m_t = jnp.maximum(inter, jnp.max(dmat, axis=2))
        w_intra = jnp.exp(dmat - m_t[:, :, None, :])
        w_inter = jnp.exp(inter - m_t)
        s = jnp.einsum('bthd,bshd->btsh', qc, kc) * w_intra
        num = jnp.einsum('btsh,bshd->bthd', s, vc) + w_inter[..., None] * jnp.einsum('bthd,bhde->bthe', qc, C)
        den = jnp.sum(s, axis=2) + w_inter * jnp.einsum('bthd,bhd->bth', qc, n)
        h = num / jnp.maximum(jnp.abs(den), jnp.exp(-m_t))[..., None]
        bL = b[:, -1, :]
        m_new = m_t[:, -1, :]
        w_state = jnp.exp(bL[:, None, :] - b + ic - m_new[:, None, :])
        decay = jnp.exp(bL + m - m_new)
        C_new = decay[..., None, None] * C + jnp.einsum('bsh,bshd,bshe->bhde', w_state, kc, vc)
        n_new = decay[..., None] * n + jnp.einsum('bsh,bshd->bhd', w_state, kc)
        return (C_new, n_new, m_new), h

    (C, n, m), hs = lax.scan(step, (C0, n0, m0), xs)
    h = jnp.moveaxis(hs, 0, 1).reshape(B, T, H, Dh)
    return h, C, n, m


def mlstm_branch(q, k, v, o_pre, gate_pre, b_gates, g_mlstm, C0, n0, m0):
    B, T, _ = q.shape
    f32 = jnp.float32

    def heads(a):
        return a.astype(f32).reshape(B, T, N_MLSTM_HEADS, HEAD_DIM)

    qh, kh, vh = heads(q), heads(k) * (HEAD_DIM ** -0.5), heads(v)
    gp = gate_pre.astype(f32) + b_gates.astype(f32)
    i_f, i_b, f_f, f_b = jnp.split(gp, 4, axis=-1)
    logf_f = jax.nn.log_sigmoid(f_f)
    logf_b = jax.nn.log_sigmoid(f_b)
    C0, n0, m0 = C0.astype(f32), n0.astype(f32), m0.astype(f32)
    h_f, Cf, nf, mf = mlstm_scan(qh, kh, vh, i_f, logf_f, C0[:, 0], n0[:, 0], m0[:, 0])

    def flip(a):
        return jnp.flip(a, axis=1)

    h_b, Cb, nb, mb = mlstm_scan(flip(qh), flip(kh), flip(vh), flip(i_b), flip(logf_b), C0[:, 1], n0[:, 1], m0[:, 1])
    h = h_f + flip(h_b)
    h = h * lax.rsqrt(jnp.mean(h * h, axis=-1, keepdims=True) + EPS) * g_mlstm.astype(f32).reshape(N_MLSTM_HEADS, HEAD_DIM)
    y = jax.nn.sigmoid(o_pre.astype(f32)) * h.reshape(B, T, D_MLSTM)
    C_out = jnp.stack([Cf, Cb], axis=1)
    n_out = jnp.stack([nf, nb], axis=1)
    m_out = jnp.stack([mf, mb], axis=1)
    return y.astype(q.dtype), C_out, n_out, m_out


def trunk_layer(x, cond, grid, C0, n0, m0, w_ada, b_ada, g_norm1, g_norm2, w_in, b_gates, w_pool, pool_scale,
                g_sgu, w_sgu, b_sgu, g_mlstm, w_branch, w_out, w_ff1, w_ff2):
    B, T, _ = x.shape
    mod = (jax.nn.silu(cond) @ w_ada + b_ada)[:, None, :]
    sh1, sc1, gt1, sh2, sc2, gt2 = jnp.split(mod, 6, axis=-1)
    u = rmsnorm(x, g_norm1) * (1 + sc1) + sh1
    proj = u @ w_in
    points = np.cumsum(IN_WIDTHS).tolist()
    xp, su, sv, q, k, v, o_pre, gate_pre, br_pre = jnp.split(proj, points, axis=-1)
    y_a = pool_branch(xp, w_pool, pool_scale, grid)
    y_b = sgu_branch(su, sv, g_sgu, w_sgu, b_sgu)
    y_c, C_out, n_out, m_out = mlstm_branch(q, k, v, o_pre, gate_pre, b_gates, g_mlstm, C0, n0, m0)
    ys = jnp.stack([y_a.astype(x.dtype), y_b.astype(x.dtype), y_c.astype(x.dtype)], axis=2)
    branches = jnp.einsum('btrc,rcd->btrd', ys, w_branch)
    gates = jax.nn.sigmoid(br_pre).reshape(B, T, N_BRANCH, D_MODEL)
    merged = jnp.sum(gates * branches, axis=2)
    x = x + gt1 * (merged @ w_out)
    u2 = rmsnorm(x, g_norm2) * (1 + sc2) + sh2
    x = x + gt2 * (jnp.square(jax.nn.relu(u2 @ w_ff1)) @ w_ff2)
    return x, C_out, n_out, m_out


def setup_inputs(seed: int = 0) -> dict:
    key = jax.random.key(seed)
    ks = jax.random.split(key, 25)
    f32 = jnp.float32
    H, Dh, D = N_MLSTM_HEADS, HEAD_DIM, D_MODEL

    def nrm(k, shape, s):
        return jax.random.normal(k, shape, f32) * s

    f_bias = 3.0 + 3.0 * jnp.linspace(0.0, 1.0, H, dtype=f32)
    b_i = nrm(ks[12], (DEPTH, 2 * H), 0.1)
    b_f = jnp.tile(f_bias, 2)[None, :] + nrm(ks[13], (DEPTH, 2 * H), 0.1)
    return {
        "x_prompt": nrm(ks[0], (BATCH, SEQ, D), 1.0),
        "x_sample": nrm(ks[1], (DEC_BATCH, DEC_SEQ, D), 1.0),
        "state_C": nrm(ks[2], (DEC_BATCH, DEPTH, 2, H, Dh, Dh), 0.1),
        "state_n": nrm(ks[3], (DEC_BATCH, DEPTH, 2, H, Dh), 0.1),
        "state_m": 1.0 + nrm(ks[4], (DEC_BATCH, DEPTH, 2, H), 0.5),
        "c": nrm(ks[5], (DEC_BATCH, D), 1.0),
        "c_ctx": nrm(ks[6], (D,), 1.0),
        "w_ada": nrm(ks[7], (DEPTH, D, 6 * D), 0.5 * D ** -0.5),
        "b_ada": nrm(ks[8], (DEPTH, 6 * D), 0.01),
        "g_norm1": 1.0 + nrm(ks[9], (DEPTH, D), 0.05),
        "g_norm2": 1.0 + nrm(ks[10], (DEPTH, D), 0.05),
        "w_in": nrm(ks[11], (DEPTH, D, D_IN), D ** -0.5),
        "b_gates": jnp.concatenate([b_i, b_f], axis=-1),
        "w_pool": nrm(ks[14], (DEPTH, N_POOL_GROUPS, POOL_GROUP, POOL_GROUP), POOL_GROUP ** -0.5),
        "pool_scale": 1.0 + nrm(ks[15], (DEPTH, D_POOL), 0.1),
        "g_sgu": 1.0 + nrm(ks[16], (DEPTH, D_SGU), 0.05),
        "w_sgu": nrm(ks[17], (DEPTH, N_SGU_GROUPS, CHUNK, CHUNK), CHUNK ** -0.5),
        "b_sgu": 1.0 + nrm(ks[18], (DEPTH, N_SGU_GROUPS, CHUNK), 0.1),
        "g_mlstm": 1.0 + nrm(ks[19], (DEPTH, D_MLSTM), 0.05),
        "w_branch": nrm(ks[20], (DEPTH, N_BRANCH, D_BRANCH, D), D_BRANCH ** -0.5),
        "w_out": nrm(ks[21], (DEPTH, D, D), D ** -0.5),
        "w_ff1": nrm(ks[22], (DEPTH, D, D_FF), D ** -0.5),
        "w_ff2": nrm(ks[23], (DEPTH, D_FF, D), D_FF ** -0.5),
        "g_final": 1.0 + nrm(ks[24], (D,), 0.05),
    }


def reference(x_prompt, x_sample, state_C, state_n, state_m, c, c_ctx, w_ada, b_ada, g_norm1, g_norm2, w_in,
              b_gates, w_pool, pool_scale, g_sgu, w_sgu, b_sgu, g_mlstm, w_branch, w_out, w_ff1, w_ff2, g_final):
    B = x_prompt.shape[0]
    H, Dh = N_MLSTM_HEADS, HEAD_DIM
    ctx_cond = c_ctx[None, :]
    C_zero = jnp.zeros((B, 2, H, Dh, Dh), jnp.float32)
    n_zero = jnp.zeros((B, 2, H, Dh), jnp.float32)
    m_zero = jnp.zeros((B, 2, H), jnp.float32)
    xc, xs = x_prompt, x_sample
    new_C, new_n, new_m = [], [], []
    for l in range(DEPTH):
        lp = (w_ada[l], b_ada[l], g_norm1[l], g_norm2[l], w_in[l], b_gates[l], w_pool[l], pool_scale[l],
              g_sgu[l], w_sgu[l], b_sgu[l], g_mlstm[l], w_branch[l], w_out[l], w_ff1[l], w_ff2[l])
        xc, C_l, n_l, m_l = trunk_layer(xc, ctx_cond, False, C_zero, n_zero, m_zero, *lp)
        new_C.append(C_l)
        new_n.append(n_l)
        new_m.append(m_l)
        xs, _, _, _ = trunk_layer(xs, c, True, state_C[:, l], state_n[:, l], state_m[:, l], *lp)
    y_prompt = rmsnorm(xc, g_final)
    y_sample = rmsnorm(xs, g_final)
    new_C = jnp.stack(new_C, axis=1)
    new_n = jnp.stack(new_n, axis=1)
    new_m = jnp.stack(new_m, axis=1)
    return (y_prompt, y_sample, new_C, new_n, new_m)
```

```python
import os
from contextlib import ExitStack

import numpy as np
import ml_dtypes
import concourse.bass as bass
import concourse.mybir as mybir
from concourse.bass_utils import run_bass_kernel_spmd

F32 = mybir.dt.float32
BF16 = mybir.dt.bfloat16
ALU = mybir.AluOpType
AF = mybir.ActivationFunctionType
AX = mybir.AxisListType

D = 1024
T = 1024
DEPTH = 4
DIN = 6672
DFF = 4096
NT = 8
KC = 8
EPS = 1e-6
SEM_CAP = 24000
NSLOT = 4
C_XP, C_SU, C_SV, C_Q, C_K, C_V, C_O, C_G, C_BR = 0, 512, 1024, 1536, 2048, 2560, 3072, 3584, 3600


class Buf:
    __slots__ = ("name", "last_write", "reads", "aliases")

    def __init__(self, name):
        self.name = name
        self.last_write = None
        self.reads = []
        self.aliases = []


def alias(*bufs):
    for a in bufs:
        for b in bufs:
            if a is not b and b not in a.aliases:
                a.aliases.append(b)


class Prog:
    ENGS = ("pe", "act", "dve", "pool", "sp")

    def __init__(self, nc, ctx):
        self.nc = nc
        self.ctx = ctx
        self.streams = {e: [] for e in self.ENGS}
        self.cur_sem = {}
        self.cur_cnt = {}
        self.epoch = {}
        for e in self.ENGS:
            self._new_epoch(e, first=True)
        self.waited = {e: {} for e in self.ENGS}
        self.n_inst = 0
        self._dma_sems = {}
        self._dma_rr = {}

    def _new_sem(self, name):
        return self.ctx.enter_context(self.nc.semaphore(name))

    def _new_epoch(self, e, first=False):
        self.epoch[e] = 0 if first else self.epoch[e] + 1
        self.cur_sem[e] = self._new_sem(f"s_{e}_{self.epoch[e]}")
        self.cur_cnt[e] = 0

    def _emit_waits(self, e, deps):
        need = {}
        w = self.waited[e]
        for t in deps:
            if t is None:
                continue
            sem, val, teng, tep = t
            key = id(sem)
            if w.get(key, 0) >= val:
                continue
            if teng is not None and w.get(("ep", teng), -1) > tep:
                continue
            if key not in need or need[key][1] < val:
                need[key] = (sem, val, teng, tep)
        for key, (sem, val, teng, tep) in need.items():
            self.streams[e].append(("wait", sem, val))
            w[key] = val
            if teng is not None:
                w[("ep", teng)] = max(w.get(("ep", teng), -1), tep)

    def _deps(self, reads, writes, extra):
        deps = list(extra)
        for b in reads:
            deps.append(b.last_write)
        for b in writes:
            deps.append(b.last_write)
            deps.extend(b.reads)
            for a in b.aliases:
                deps.append(a.last_write)
                deps.extend(a.reads)
        return deps

    def _commit(self, tok, reads, writes):
        for b in reads:
            b.reads.append(tok)
        for b in writes:
            b.last_write = tok
            b.reads = []

    def op(self, e, fn, reads=(), writes=(), extra=()):
        deps = self._deps(reads, writes, extra)
        if e == "pe":
            deps = [t for t in deps if t is not None and t[2] != "pe"]
        self._emit_waits(e, deps)
        if self.cur_cnt[e] >= SEM_CAP:
            self._new_epoch(e)
        self.cur_cnt[e] += 1
        tok = (self.cur_sem[e], self.cur_cnt[e], e, self.epoch[e])
        self.streams[e].append(("op", fn, self.cur_sem[e]))
        self.n_inst += 1
        self._commit(tok, reads, writes)
        return tok

    def _get_dma_sem(self, q):
        pool = self._dma_sems.setdefault(q, [])
        rr = self._dma_rr.setdefault(q, 0)
        if len(pool) < 20:
            s = [self._new_sem(f"dma_{q}{len(pool)}"), 0]
            pool.append(s)
            return s
        s = pool[rr % len(pool)]
        self._dma_rr[q] = rr + 1
        key = id(s[0])
        if self.waited[q].get(key, 0) < s[1]:
            self.streams[q].append(("wait", s[0], s[1]))
            self.waited[q][key] = s[1]
        return s

    def dma(self, q, out, in_, reads=(), writes=(), extra=()):
        deps = self._deps(reads, writes, extra)
        self._emit_waits(q, deps)
        s = self._get_dma_sem(q)
        s[1] += 16
        tok = (s[0], s[1], None, 0)
        self.streams[q].append(("dma", out, in_, s[0]))
        self.n_inst += 1
        self._commit(tok, reads, writes)
        return tok

    def wait_all(self, e, toks):
        self._emit_waits(e, toks)

    def emit(self, block):
        streams = self.streams

        def run(eng, lst):
            for it in lst:
                if it[0] == "wait":
                    eng.wait_ge(it[1], it[2])
                elif it[0] == "op":
                    it[1](eng).then_inc(it[2], 1)
                else:
                    eng.dma_start(out=it[1], in_=it[2]).then_inc(it[3], 16)

        @block.tensor
        def _(eng):
            run(eng, streams["pe"])

        @block.scalar
        def _(eng):
            run(eng, streams["act"])

        @block.vector
        def _(eng):
            run(eng, streams["dve"])

        @block.gpsimd
        def _(eng):
            run(eng, streams["pool"])

        @block.sync
        def _(eng):
            run(eng, streams["sp"])


def build_program(depth=DEPTH, dbg=()):
    nc = bass.Bass("TRN2", target_bir_lowering=False)

    def din(name, shape, dt=F32):
        return nc.dram_tensor(name, list(shape), dt, kind="ExternalInput").ap()

    def dout(name, shape, dt=F32):
        return nc.dram_tensor(name, list(shape), dt, kind="ExternalOutput").ap()

    x_in = din("x", [T, D])
    cond_in = din("cond", [128, KC])
    keep_in = din("keep", [128, 4])
    minit_in = din("minit", [128, DEPTH * 4 * 8])
    cinit_in = din("cinit", [DEPTH, 4, 128, 8 * 129])
    poolm_in = din("poolm", [4, T, T], BF16)
    masks_in = din("masks", [128, 2 * 128])
    ident_in = din("ident", [128, 128])
    w_ada = din("w_ada", [DEPTH, D, 6 * D])
    bada_in = din("bada", [128, DEPTH * 48])
    gn_in = din("gn", [128, DEPTH * 2 * KC])
    w_in = din("w_in", [DEPTH, D, DIN])
    b_gates = din("b_gates", [DEPTH, 16])
    w_pool = din("w_pool", [DEPTH, 4, 128, 128])
    pscale_in = din("pscale", [128, DEPTH * 4])
    gsgu_in = din("gsgu", [128, DEPTH * 4])
    w_sguT = din("w_sguT", [DEPTH, 128, 4 * 128])
    b_sgu = din("b_sgu", [DEPTH, 4 * 128])
    g_mlstm = din("g_mlstm", [DEPTH, 512])
    w_branch = din("w_branch", [DEPTH, 3, 512, D])
    w_out = din("w_out", [DEPTH, D, D])
    w_ff1 = din("w_ff1", [DEPTH, D, DFF])
    w_ff2 = din("w_ff2", [DEPTH, DFF, D])
    gfin_in = din("gfin", [128, KC])

    y_out = dout("y", [T, D])
    c_out = dout("c_out", [DEPTH, 4, 2, 128, 4 * 129])
    m_out = dout("m_out", [DEPTH, 4, 8])
    dbg_toks = []

    with ExitStack() as ctx:
        P = Prog(nc, ctx)

        def sb(name, shape, dt=F32):
            return ctx.enter_context(nc.sbuf_tensor("s_" + name, list(shape), dt))

        x_fm = sb("x_fm", [128, KC, T])
        u_fm = sb("u_fm", [128, KC, T], BF16)
        wslot = [sb(f"wslot{i}", [128, 4096], BF16) for i in range(NSLOT)]
        arena = sb("arena", [128, 6 * 4096], BF16)
        v_tm = sb("v_tm", [128, NT, 512], BF16)
        vw_ext = sb("vw_ext", [128, NT, 8, 130], BF16)
        p0t = sb("p0t", [128, NT, 512], BF16)
        ident = sb("ident", [128, 128])
        ident_bf = sb("ident_bf", [128, 128], BF16)
        masks = sb("masks", [128, 2, 128])
        ones_mean_bf = sb("ones_mean_bf", [128, 128], BF16)
        ones_f = sb("ones_f", [128, 128])
        cond_sb = sb("cond_sb", [128, KC])
        scond = sb("scond", [128, KC], BF16)
        keep = sb("keep", [128, 4])
        minit = sb("minit", [128, DEPTH, 4, 8])
        gn = sb("gn", [128, DEPTH, 2, KC])
        pscale = sb("pscale", [128, DEPTH, 4])
        gsgu = sb("gsgu", [128, DEPTH, 4])
        gfin = sb("gfin", [128, KC])
        bada = sb("bada", [128, DEPTH, 48])
        modfm = sb("modfm", [128, DEPTH, 48])
        AB = sb("AB", [128, DEPTH, 2, KC])
        wg = sb("wg", [128, KC, 16], BF16)
        bg_rep = sb("bg_rep", [128, 16])
        wpool = sb("wpool", [128, 4, 128], BF16)
        wsgu = sb("wsgu", [128, 4, 128], BF16)
        bsgu_rep = sb("bsgu_rep", [128, 4, 128])
        gml_rep = sb("gml_rep", [128, 512])
        ftile = [sb(f"ftile{i}", [128, 512]) for i in range(3)]
        modrow = ftile[0]
        tmp_b = [sb(f"tmpb{i}", [128, 512], BF16) for i in range(2)]
        ssq = sb("ssq", [128, NT])
        rsv = sb("rsv", [128, NT])
        gp = sb("gp", [128, NT, 16])
        gt0 = sb("gt0", [128, NT, 8])
        G_tm = sb("G_tm", [128, NT, 16])
        b_tm = sb("b_tm", [128, NT, 8])
        Q2 = sb("Q2", [128, 2])
        Dg = sb("Dg", [128, 128])
        Rmax = sb("Rmax", [128, NT, 16])
        Rsum = sb("Rsum", [128, NT, 16])
        mprev = sb("mprev", [128, 2, 9, 4])
        Mlast = sb("Mlast", [128, 2, 8, 4])
        decay = sb("decay", [128, 2, 8, 4])
        wst = sb("wst", [128, NT, 8])
        clampt = sb("clampt", [128, NT, 8])
        Cst = sb("Cst", [128, 8, 129])
        Cd = sb("Cd", [128, 8, 129])
        Cd_bf = [sb(f"Cd_bf{i}", [128, 8, 130], BF16) for i in range(2)]
        Cfin = sb("Cfin", [128, 8, 129])
        mstage = sb("mstage", [128, 4, 8])
        cin = [sb(f"cin{i}", [128, 4, 129]) for i in range(2)]
        dn = sb("dn", [128, 8])
        rd = sb("rd", [128, 8])
        hss2 = sb("hss2", [128, 2, 4])
        yc_tm = tmp_b[0]

        psum = [ctx.enter_context(nc.psum_tensor(f"psum{i}", [128, 1024], F32)) for i in range(4)]
        block = ctx.enter_context(nc.Block())

        def aview(i, shape_str, **kw):
            return arena[:, i * 4096:(i + 1) * 4096].rearrange(shape_str, **kw)

        xp_tm = aview(0, "p (t c) -> p t c", t=NT)
        vn_tm = aview(1, "p (t c) -> p t c", t=NT)
        su_fm = aview(2, "p (g t) -> p g t", g=4)
        q_fm = aview(3, "p (g t) -> p g t", g=4)
        kf_fm = aview(4, "p (g t) -> p g t", g=4)
        o_tm = aview(4, "p (t c) -> p t c", t=NT)
        p0b = aview(4, "p (t c) -> p t c", t=NT)
        k_tm = aview(5, "p (t c) -> p t c", t=NT)
        a01_f32 = arena[:, 0:8192].bitcast(F32)
        h_tm = a01_f32.rearrange("p (t h e) -> p t h e", t=NT, h=4)
        acc_fm = a01_f32.rearrange("p (c t) -> p c t", c=4)
        io_tm = [a01_f32[:, i * 1024:(i + 1) * 1024] for i in range(2)]
        hff = [arena[:, 8192 + i * 8192: 8192 + (i + 1) * 8192].rearrange("p (c t) -> p c t", c=8) for i in range(2)]
        yc_fm = q_fm
        yb_fm = su_fm
        y_a = v_tm[:].rearrange("p t c -> p (t c)").rearrange("p (g t) -> p g t", g=4)
        p0t_flat = p0t[:].rearrange("p t c -> p (t c)")
        d_fm = p0t_flat.rearrange("p (g t) -> p g t", g=4)
        merged = vw_ext[:].rearrange("p t g c -> p (t g c)")[:, 0:8192].rearrange("p (c t) -> p c t", c=8)

        B = {}

        def bf(name):
            if name not in B:
                B[name] = Buf(name)
            return B[name]

        bx = [[bf(f"x{c}_{h}") for h in range(2)] for c in range(KC)]
        bu = [bf(f"u{h}") for h in range(2)]
        bws = [bf(f"ws{i}") for i in range(NSLOT)]
        bps = [bf(f"ps{i}") for i in range(8)]
        bft = [bf(f"ftile{i}") for i in range(3)]
        b_xp, b_vn, b_su, b_q, b_kf, b_kt, b_o = (bf(n) for n in ("xp", "vn", "su", "q", "kf", "kt", "o"))
        b_h = bf("h")
        bacc = [[bf(f"acc{n}_{h}") for h in range(2)] for n in range(4)]
        bio = [bf("io0"), bf("io1")]
        bhff = [bf("hff0"), bf("hff1")]
        alias(b_h, b_xp, b_vn)
        for n in range(4):
            for h in range(2):
                alias(bacc[n][h], b_h)
                alias(bacc[n][h], b_xp)
                alias(bacc[n][h], b_vn)
        for i in range(2):
            alias(bio[i], b_xp)
            alias(bio[i], b_h)
            for n in range(4):
                for h in range(2):
                    alias(bio[i], bacc[n][h])
        alias(bhff[0], b_su, b_q)
        alias(bhff[1], b_kf, b_kt, b_o)
        b_p0t, b_mrg, b_d = bf("p0t"), bf("mrg"), bf("d")
        alias(b_p0t, b_d)
        b_p0b = bf("p0b")
        alias(b_p0b, b_kf)
        alias(b_p0b, b_o)
        alias(b_p0b, bhff[1])
        alias(b_mrg, bf("vw"))
        b_v, b_ya = bf("v"), bf("y_a")
        alias(b_v, b_ya)

        def psv(i):
            return psum[i // 2][:, (i % 2) * 512:(i % 2 + 1) * 512]

        ps_rr = [0]

        ps_reserved = set()

        def ps_next():
            while True:
                i = ps_rr[0] % 8
                ps_rr[0] += 1
                if i not in ps_reserved:
                    return i

        chain_q = []

        def drain(n):
            for _ in range(min(n, len(chain_q))):
                chain_q.pop(0)()

        ws_rr = [0]

        def load_w(view, kc, n, q="pool"):
            i = ws_rr[0] % NSLOT
            ws_rr[0] += 1
            dst = wslot[i][:, 0:kc * n].rearrange("p (k n) -> p k n", k=kc)
            P.dma(q, dst, view, writes=[bws[i]])
            return dst, bws[i]

        def dump(name, ap, shape, reads, dt=F32):
            if name in dbg:
                o = dout("dbg_" + name, shape, dt)
                dbg_toks.append(P.dma("sp", o, ap, reads=reads))

        out_toks = []
        evac_rr = [0]

        def copy_evac(dst, src, reads, writes, scale=None):
            evac_rr[0] += 1
            if evac_rr[0] % 2 == 0:
                if scale is None:
                    P.op("act", lambda e: e.activation(out=dst, in_=src, func=AF.Copy), reads=reads, writes=writes)
                else:
                    P.op("act", lambda e: e.activation(out=dst, in_=src, func=AF.Copy, scale=scale), reads=reads, writes=writes)
            else:
                if scale is None:
                    P.op("dve", lambda e: e.tensor_copy(out=dst, in_=src), reads=reads, writes=writes)
                else:
                    P.op("dve", lambda e: e.tensor_scalar(out=dst, in0=src, scalar1=scale, scalar2=None, op0=ALU.mult),
                         reads=reads, writes=writes)

        P.dma("sp", ident[:], ident_in, writes=[bf("ident")])
        P.dma("sp", masks[:].rearrange("p a b -> p (a b)"), masks_in, writes=[bf("masks")])
        P.dma("sp", cond_sb[:], cond_in, writes=[bf("cond")])
        P.dma("sp", keep[:], keep_in, writes=[bf("keep")])
        P.dma("sp", minit[:].rearrange("p a b c -> p (a b c)"), minit_in, writes=[bf("minit")])
        P.dma("sp", gn[:].rearrange("p a b c -> p (a b c)"), gn_in, writes=[bf("gn")])
        P.dma("sp", bada[:].rearrange("p a b -> p (a b)"), bada_in, writes=[bf("bada")])
        P.dma("sp", pscale[:].rearrange("p a b -> p (a b)"), pscale_in, writes=[bf("pscale")])
        P.dma("sp", gsgu[:].rearrange("p a b -> p (a b)"), gsgu_in, writes=[bf("gsgu")])
        P.dma("sp", gfin[:], gfin_in, writes=[bf("gfin")])
        P.op("dve", lambda e: e.memset(ones_mean_bf[:], 1.0 / D), writes=[bf("ones_mean_bf")])
        P.op("dve", lambda e: e.memset(ones_f[:], 1.0), writes=[bf("ones_f")])
        P.op("dve", lambda e: e.tensor_copy(out=ident_bf[:], in_=ident[:]), reads=[bf("ident")], writes=[bf("ident_bf")])
        P.op("dve", lambda e: e.memset(mprev[:], 0.0), writes=[bf("mchain")])
        P.op("dve", lambda e: e.memset(Cst[:], 0.0), writes=[bf("Cst0"), bf("Cst1")])
        P.op("dve", lambda e: e.memset(Cfin[:], 0.0), writes=[bf("Cfin0"), bf("Cfin1")])
        P.op("dve", lambda e: e.memset(mstage[:], 0.0), writes=[bf("mstage")])
        for i in range(2):
            P.op("dve", lambda e, i=i: e.memset(Cd_bf[i][:], 0.0), writes=[bf(f"Cdbf{i}_0"), bf(f"Cdbf{i}_1")])
        P.op("dve", lambda e: e.memset(vw_ext[:], 0.0), writes=[bf("vw")])
        P.op("act", lambda e: e.activation(out=scond[:], in_=cond_sb[:], func=AF.Silu), reads=[bf("cond")], writes=[bf("scond")])

        for t in range(NT):
            it = io_tm[t % 2]
            bi = bio[t % 2]
            P.dma("sp", it, x_in[t * 128:(t + 1) * 128, :], writes=[bi])
            for cg in range(2):
                pi = ps_next()
                for c4 in range(4):
                    c = cg * 4 + c4
                    P.op("pe", lambda e, pi=pi, c4=c4, c=c, it=it: e.matmul(
                        psv(pi)[:, c4 * 128:(c4 + 1) * 128], lhsT=it[:, c * 128:(c + 1) * 128], rhs=ident[:],
                        start=True, stop=True), reads=[bi, bf("ident")], writes=[bps[pi]])
                copy_evac(x_fm[:, cg * 4:(cg + 1) * 4, t * 128:(t + 1) * 128], psv(pi).rearrange("p (c t) -> p c t", c=4),
                          [bps[pi]], [bx[c][t // 4] for c in range(cg * 4, cg * 4 + 4)])

        bmod = [[bf(f"mod{l}_{w}") for w in range(2)] for l in range(DEPTH)]
        bAB = [[bf(f"AB{l}_{w}") for w in range(2)] for l in range(DEPTH)]

        def ada_mm(l, ng, pcols, pbufs):
            wv, wb_ = load_w(w_ada[l, :, ng * 512:(ng + 1) * 512].rearrange("(k p) n -> p k n", p=128), KC, 512)
            for n4 in range(4):
                for kc in range(KC):
                    P.op("pe", lambda e, n4=n4, kc=kc: e.matmul(
                        pcols[:, n4:n4 + 1], lhsT=wv[:, kc, n4 * 128:(n4 + 1) * 128], rhs=scond[:, kc:kc + 1],
                        start=(kc == 0), stop=(kc == KC - 1)), reads=[wb_, bf("scond")], writes=pbufs)

        def ada_evac(l, ng, pcols, pbufs):
            piece = 0 if ng < 4 else 1
            P.op("dve", lambda e: e.tensor_tensor(out=modfm[:, l, ng * 4:(ng + 1) * 4], in0=pcols, in1=bada[:, l, ng * 4:(ng + 1) * 4], op=ALU.add),
                 reads=list(pbufs) + [bf("bada")], writes=[bmod[l][piece]])
            if ng == 3 or ng == 9:
                which = 0 if ng == 3 else 1
                off = 8 if which == 0 else 32
                P.op("dve", lambda e: e.scalar_tensor_tensor(
                    out=AB[:, l, which, :], in0=modfm[:, l, off:off + 8], scalar=1.0, in1=gn[:, l, which, :], op0=ALU.add, op1=ALU.mult),
                    reads=[bmod[l][piece], bf("gn")], writes=[bAB[l][which]])

        for ng in range(4):
            pi = ps_next()
            ada_mm(0, ng, psv(pi)[:, 0:4], [bps[pi]])
            ada_evac(0, ng, psv(pi)[:, 0:4], [bps[pi]])
        dump("xfm", x_fm[:].rearrange("p c t -> p (c t)"), [128, KC * T], [b for bb in bx for b in bb])

        def ms_rstd(h):
            hs = slice(h * 512, (h + 1) * 512)
            pi = ps_next()
            for c in range(KC):
                sq, bsq = tmp_b[c % 2], bf(f"tmpb{c % 2}")
                P.op("act", lambda e, sq=sq, c=c: e.activation(out=sq[:], in_=x_fm[:, c, hs], func=AF.Square),
                     reads=[bx[c][h]], writes=[bsq])
                P.op("pe", lambda e, sq=sq, c=c: e.matmul(psv(pi), lhsT=ones_mean_bf[:], rhs=sq[:], start=(c == 0), stop=(c == KC - 1)),
                     reads=[bsq, bf("ones_mean_bf")], writes=[bps[pi]])
            P.op("act", lambda e: e.activation(out=ftile[2][:], in_=psv(pi), func=AF.Ln, bias=EPS), reads=[bps[pi]], writes=[bft[2]])
            P.op("act", lambda e: e.activation(out=ftile[2][:], in_=ftile[2][:], func=AF.Exp, scale=-0.5), reads=[bft[2]], writes=[bft[2]])

        def rmsnorm_to_u(l, which):
            shoff = 0 if which == 0 else 24
            for h in range(2):
                hs = slice(h * 512, (h + 1) * 512)
                ms_rstd(h)
                for c in range(KC):
                    tf, btf = ftile[c % 2], bft[c % 2]
                    P.op("dve", lambda e, tf=tf, c=c, hs=hs: e.tensor_tensor(out=tf[:], in0=x_fm[:, c, hs], in1=ftile[2][:], op=ALU.mult),
                         reads=[bx[c][h], bft[2]], writes=[btf])
                    P.op("act", lambda e, tf=tf, c=c, hs=hs: e.activation(
                        out=u_fm[:, c, hs], in_=tf[:], func=AF.Identity,
                        scale=AB[:, l, which, c:c + 1], bias=modfm[:, l, shoff + c:shoff + c + 1]),
                        reads=[btf, bAB[l][which], bmod[l][which]], writes=[bu[h]])

        def proj_fm(wv, wb_, ncols, evac):
            for nchk in range(ncols // 128):
                for h in range(2):
                    pi = ps_next()
                    for kc in range(KC):
                        P.op("pe", lambda e, pi=pi, kc=kc, nchk=nchk, h=h: e.matmul(
                            psv(pi), lhsT=wv[:, kc, nchk * 128:(nchk + 1) * 128], rhs=u_fm[:, kc, h * 512:(h + 1) * 512],
                            start=(kc == 0), stop=(kc == KC - 1)), reads=[wb_, bu[h]], writes=[bps[pi]])
                    evac(pi, nchk, h)
                    drain(2)

        def proj_tm(wv, wb_, ncols, evac):
            for t in range(NT):
                pi = ps_next()
                for kc in range(KC):
                    P.op("pe", lambda e, pi=pi, kc=kc, t=t: e.matmul(
                        psv(pi)[:, 0:ncols], lhsT=u_fm[:, kc, t * 128:(t + 1) * 128], rhs=wv[:, kc, 0:ncols],
                        start=(kc == 0), stop=(kc == KC - 1)), reads=[wb_, bu[t // 4]], writes=[bps[pi]])
                evac(pi, t)
                drain(2)

        def w_in_unit(l, col, n=512):
            return load_w(w_in[l, :, col:col + n].rearrange("(k p) n -> p k n", p=128), KC, n)

        ksc = 128 ** -0.5

        def layer(l):
            P.dma("pool", wg[:], w_in[l, :, C_G:C_G + 16].rearrange("(k p) n -> p k n", p=128), writes=[bf("wg")])
            P.dma("pool", wpool[:], w_pool[l].rearrange("g c d -> c g d"), writes=[bf("wpool")])
            P.dma("pool", wsgu[:].rearrange("p g q -> p (g q)"), w_sguT[l], writes=[bf("wsgu")])
            P.dma("sp", bg_rep[:], b_gates[l, :].partition_broadcast(128), writes=[bf("bg_rep")])
            P.dma("sp", bsgu_rep[:].rearrange("p g q -> p (g q)"), b_sgu[l, :].partition_broadcast(128), writes=[bf("bsgu_rep")])
            P.dma("sp", gml_rep[:], g_mlstm[l, :].partition_broadcast(128), writes=[bf("gml_rep")])

            rmsnorm_to_u(l, 0)
            if l == 0:
                dump("u0", u_fm[:].rearrange("p c t -> p (c t)"), [128, KC * T], bu, BF16)

            wv, wb_ = w_in_unit(l, C_XP)
            proj_tm(wv, wb_, 512, lambda pi, t: copy_evac(xp_tm[:, t, :], psv(pi), [bps[pi]], [b_xp]))
            pi = ps_next()
            for t in range(NT):
                for kc in range(KC):
                    P.op("pe", lambda e, pi=pi, kc=kc, t=t: e.matmul(
                        psv(pi)[:, t * 16:(t + 1) * 16], lhsT=u_fm[:, kc, t * 128:(t + 1) * 128], rhs=wg[:, kc, :],
                        start=(kc == 0), stop=(kc == KC - 1)), reads=[bf("wg"), bu[t // 4]], writes=[bps[pi]])
            P.op("dve", lambda e, pi=pi: e.tensor_tensor(
                out=gp[:], in0=psv(pi)[:, 0:128].rearrange("p (t g) -> p t g", t=NT),
                in1=bg_rep[:, :].unsqueeze(1).to_broadcast([128, NT, 16]), op=ALU.add),
                reads=[bps[pi], bf("bg_rep")], writes=[bf("gp")])
            if l == 0:
                dump("gp", gp[:].rearrange("p t g -> p (t g)"), [128, 128], [bf("gp")])

            fg = gp[:, :, 8:16]
            P.op("dve", lambda e: e.scalar_tensor_tensor(out=gt0[:], in0=fg, scalar=-1.0, in1=fg, op0=ALU.mult, op1=ALU.max),
                 reads=[bf("gp")], writes=[bf("gt0")])
            P.op("act", lambda e: e.activation(out=gt0[:], in_=gt0[:], func=AF.Exp, scale=-1.0), reads=[bf("gt0")], writes=[bf("gt0")])
            P.op("act", lambda e: e.activation(out=gt0[:], in_=gt0[:], func=AF.Ln, bias=1.0), reads=[bf("gt0")], writes=[bf("gt0")])
            P.op("dve", lambda e: e.scalar_tensor_tensor(out=G_tm[:, :, 8:16], in0=fg, scalar=0.0, in1=gt0[:], op0=ALU.min, op1=ALU.subtract),
                 reads=[bf("gp"), bf("gt0")], writes=[bf("G_tm")])
            pi = ps_next()
            for t in range(NT):
                for dr in range(2):
                    P.op("pe", lambda e, pi=pi, t=t, dr=dr: e.matmul(
                        psv(pi)[:, t * 8 + dr * 4: t * 8 + dr * 4 + 4], lhsT=masks[:, dr, :], rhs=G_tm[:, t, 8 + dr * 4: 12 + dr * 4],
                        start=True, stop=True), reads=[bf("masks"), bf("G_tm")], writes=[bps[pi]])
            P.op("dve", lambda e, pi=pi: e.tensor_copy(out=b_tm[:], in_=psv(pi)[:, 0:64].rearrange("p (t g) -> p t g", t=NT)),
                 reads=[bps[pi]], writes=[bf("b_tm")])
            P.op("dve", lambda e: e.tensor_tensor(out=G_tm[:, :, 0:8], in0=gp[:, :, 0:8], in1=b_tm[:], op=ALU.subtract),
                 reads=[bf("gp"), bf("b_tm")], writes=[bf("G_tm")])
            pi = ps_next()
            P.op("pe", lambda e, pi=pi: e.matmul(psv(pi)[:, 0:128], lhsT=G_tm[:].rearrange("p t g -> p (t g)"), rhs=ident[:],
                                                 start=True, stop=True), reads=[bf("G_tm"), bf("ident")], writes=[bps[pi]])
            P.op("dve", lambda e, pi=pi: e.tensor_reduce(out=Q2[:, 0:1], in_=psv(pi)[:, 0:128], axis=AX.X, op=ALU.max),
                 reads=[bps[pi]], writes=[bf("Q2")])
            P.op("dve", lambda e, pi=pi: e.tensor_reduce(out=Q2[:, 1:2], in_=psv(pi)[:, 0:128], axis=AX.X, op=ALU.add),
                 reads=[bps[pi]], writes=[bf("Q2")])
            for qi, R in ((0, Rmax), (1, Rsum)):
                P.op("dve", lambda e, qi=qi: e.tensor_scalar(out=Dg[:], in0=ident[:], scalar1=Q2[:, qi:qi + 1], scalar2=None, op0=ALU.mult),
                     reads=[bf("ident"), bf("Q2")], writes=[bf("Dg")])
                pi = ps_next()
                P.op("pe", lambda e, pi=pi: e.matmul(psv(pi)[:, 0:128], lhsT=ones_f[:], rhs=Dg[:], start=True, stop=True),
                     reads=[bf("ones_f"), bf("Dg")], writes=[bps[pi]])
                P.op("dve", lambda e, pi=pi, R=R: e.tensor_copy(out=R[:].rearrange("p t g -> p (t g)"), in_=psv(pi)[:, 0:128]),
                     reads=[bps[pi]], writes=[bf("R")])
            bm = bf("mchain")

            def ch_reset(dr, pidx, slot, ds):
                return lambda: P.op("dve", lambda e: e.scalar_tensor_tensor(
                    out=mprev[:, dr, pidx, :], in0=mprev[:, dr, pidx, :], scalar=keep[:, slot:slot + 1],
                    in1=minit[:, l, slot, ds], op0=ALU.mult, op1=ALU.add),
                    reads=[bf("keep"), bf("minit"), bm], writes=[bm])

            def ch_max(dr, pidx, c, ds):
                return lambda: P.op("dve", lambda e: e.tensor_tensor(
                    out=Mlast[:, dr, c, :], in0=mprev[:, dr, pidx, :], in1=Rmax[:, c, ds], op=ALU.max),
                    reads=[bf("R"), bm], writes=[bm])

            def ch_add(dr, nidx, c, dr4):
                return lambda: P.op("dve", lambda e: e.tensor_tensor(
                    out=mprev[:, dr, nidx, :], in0=Mlast[:, dr, c, :], in1=Rsum[:, c, 8 + dr4:12 + dr4], op=ALU.add),
                    reads=[bf("R"), bm], writes=[bm])

            def ch_stage(dr, nidx, slot):
                return lambda: P.op("dve", lambda e: e.tensor_copy(
                    out=mstage[:, slot, dr * 4:dr * 4 + 4], in_=mprev[:, dr, nidx, :]), reads=[bm], writes=[bf("mstage")])

            for j in range(NT):
                for dr in range(2):
                    c = j if dr == 0 else NT - 1 - j
                    pidx = c if dr == 0 else c + 1
                    nidx = c + 1 if dr == 0 else c
                    ds = slice(dr * 4, dr * 4 + 4)
                    if j % 2 == 0:
                        chain_q.append(ch_reset(dr, pidx, j // 2, ds))
                    chain_q.append(ch_max(dr, pidx, c, ds))
                    chain_q.append(ch_add(dr, nidx, c, dr * 4))
                    if j % 2 == 1:
                        chain_q.append(ch_stage(dr, nidx, j // 2))
            wv, wb_ = w_in_unit(l, C_SU)
            proj_fm(wv, wb_, 512, lambda pi, n, h: copy_evac(su_fm[:, n, h * 512:(h + 1) * 512], psv(pi), [bps[pi]], [b_su]))
            wv, wb_ = w_in_unit(l, C_SV)

            def sv_evac(pi, t):
                P.op("act", lambda e: e.activation(out=ftile[0][:], in_=psv(pi), func=AF.Square, accum_out=ssq[:, t:t + 1]),
                     reads=[bps[pi]], writes=[bft[0], bf("ssq")])
                P.op("act", lambda e: e.activation(out=rsv[:, t:t + 1], in_=ssq[:, t:t + 1], func=AF.Ln, scale=1.0 / 512, bias=EPS),
                     reads=[bf("ssq")], writes=[bf("rsv")])
                P.op("act", lambda e: e.activation(out=rsv[:, t:t + 1], in_=rsv[:, t:t + 1], func=AF.Exp, scale=-0.5),
                     reads=[bf("rsv")], writes=[bf("rsv")])
                P.op("dve", lambda e: e.tensor_scalar(out=vn_tm[:, t, :], in0=psv(pi), scalar1=rsv[:, t:t + 1], scalar2=None, op0=ALU.mult),
                     reads=[bps[pi], bf("rsv")], writes=[b_vn])
            proj_tm(wv, wb_, 512, sv_evac)
            wv, wb_ = w_in_unit(l, C_Q)
            proj_fm(wv, wb_, 512, lambda pi, n, h: copy_evac(q_fm[:, n, h * 512:(h + 1) * 512], psv(pi), [bps[pi]], [b_q]))
            wv, wb_ = w_in_unit(l, C_K)
            proj_fm(wv, wb_, 512, lambda pi, n, h: copy_evac(kf_fm[:, n, h * 512:(h + 1) * 512], psv(pi), [bps[pi]], [b_kf], scale=ksc))
            for t in range(NT):
                pi = ps_next()
                for hd in range(4):
                    P.op("pe", lambda e, pi=pi, hd=hd, t=t: e.matmul(
                        psv(pi)[:, hd * 128:(hd + 1) * 128], lhsT=kf_fm[:, hd, t * 128:(t + 1) * 128], rhs=ident_bf[:],
                        start=True, stop=True), reads=[b_kf, bf("ident_bf")], writes=[bps[pi]])
                copy_evac(k_tm[:, t, :], psv(pi), [bps[pi]], [b_kt])
                drain(2)
            wv, wb_ = w_in_unit(l, C_V)
            proj_tm(wv, wb_, 512, lambda pi, t: copy_evac(v_tm[:, t, :], psv(pi), [bps[pi]], [b_v]))
            drain(len(chain_q))
            out_toks.append(P.dma("sp", m_out[l:l + 1].rearrange("a s g -> a (s g)"), mstage[0:1].rearrange("p s g -> p (s g)"),
                                  reads=[bf("mstage")]))
            P.op("dve", lambda e: e.tensor_tensor(out=decay[:, 0, :, :], in0=mprev[:, 0, 0:NT, :], in1=Mlast[:, 0, :, :], op=ALU.subtract),
                 reads=[bm], writes=[bf("decay")])
            P.op("dve", lambda e: e.tensor_tensor(out=decay[:, 1, :, :], in0=mprev[:, 1, 1:NT + 1, :], in1=Mlast[:, 1, :, :], op=ALU.subtract),
                 reads=[bm], writes=[bf("decay")])
            P.op("act", lambda e: e.activation(out=decay[:], in_=decay[:], func=AF.Exp), reads=[bf("decay")], writes=[bf("decay")])
            for dr in range(2):
                ds = slice(dr * 4, dr * 4 + 4)
                P.op("dve", lambda e, dr=dr, ds=ds: e.tensor_tensor(
                    out=wst[:, :, ds], in0=G_tm[:, :, ds], in1=Mlast[:, dr, :, :], op=ALU.subtract),
                    reads=[bf("G_tm"), bm], writes=[bf("wst")])
                P.op("dve", lambda e, dr=dr, ds=ds: e.scalar_tensor_tensor(
                    out=clampt[:, :, ds], in0=b_tm[:, :, ds], scalar=-1.0, in1=Mlast[:, dr, :, :],
                    op0=ALU.mult, op1=ALU.subtract), reads=[bf("b_tm"), bm], writes=[bf("clampt")])
            P.op("act", lambda e: e.activation(out=wst[:], in_=wst[:], func=AF.Exp), reads=[bf("wst")], writes=[bf("wst")])
            P.op("act", lambda e: e.activation(out=clampt[:], in_=clampt[:], func=AF.Exp), reads=[bf("clampt")], writes=[bf("clampt")])
            if l == 0:
                dump("wst", wst[:].rearrange("p t g -> p (t g)"), [128, 64], [bf("wst")])
                dump("clampt", clampt[:].rearrange("p t g -> p (t g)"), [128, 64], [bf("clampt")])
                dump("decay", decay[:].rearrange("p a c g -> p (a c g)"), [128, 64], [bf("decay")])
            for dr in range(2):
                P.op("dve", lambda e, dr=dr: e.tensor_tensor(
                    out=vw_ext[:, :, dr * 4:dr * 4 + 4, 0:128], in0=v_tm[:].rearrange("p t (h e) -> p t h e", h=4),
                    in1=wst[:, :, dr * 4:dr * 4 + 4].unsqueeze(3).to_broadcast([128, NT, 4, 128]), op=ALU.mult),
                    reads=[b_v, bf("wst")], writes=[bf("vw")])
            P.op("dve", lambda e: e.tensor_copy(out=vw_ext[:, :, :, 128], in_=wst[:]), reads=[bf("wst")], writes=[bf("vw")])

            def sgu_tile(t):
                pi = ps_next()
                for g in range(4):
                    P.op("pe", lambda e, g=g: e.matmul(
                        psv(pi)[:, g * 128:(g + 1) * 128], lhsT=vn_tm[:, t, g * 128:(g + 1) * 128], rhs=wsgu[:, g, :],
                        start=True, stop=True), reads=[b_vn, bf("wsgu")], writes=[bps[pi]])
                tf, btf = ftile[t % 2], bft[t % 2]
                for g in range(4):
                    P.op("dve", lambda e, g=g: e.scalar_tensor_tensor(
                        out=tf[:, g * 128:(g + 1) * 128], in0=psv(pi)[:, g * 128:(g + 1) * 128], scalar=gsgu[:, l, g:g + 1],
                        in1=bsgu_rep[:, g, :], op0=ALU.mult, op1=ALU.add),
                        reads=[bps[pi], bf("gsgu"), bf("bsgu_rep")], writes=[btf])
                P.op("dve", lambda e: e.tensor_tensor(
                    out=yb_fm[:, :, t * 128:(t + 1) * 128], in0=su_fm[:, :, t * 128:(t + 1) * 128],
                    in1=tf[:].rearrange("p (g q) -> p g q", g=4), op=ALU.mult),
                    reads=[btf, b_su], writes=[b_su])

            for t in range(NT):
                sgu_tile(t)
            if l == 0:
                dump("y_b", yb_fm, [128, 4, T], [b_su], BF16)
            ada_q = [(l, ng) for ng in range(4, 12)] + ([(l + 1, ng) for ng in range(4)] if l + 1 < depth else [])
            ada_g = ada_q[:4]
            ada_l = ada_q[4:]
            for g in range(4):
                for h in range(2):
                    mv, mb = load_w(poolm_in[g, :, h * 512:(h + 1) * 512].rearrange("(k p) n -> p k n", p=128), KC, 512, q="sp")
                    pi = ps_next()
                    for sc in range(NT):
                        P.op("pe", lambda e, pi=pi, sc=sc, g=g, mv=mv: e.matmul(
                            psv(pi), lhsT=xp_tm[:, sc, g * 128:(g + 1) * 128], rhs=mv[:, sc, :],
                            start=(sc == 0), stop=(sc == NT - 1)), reads=[b_xp, mb], writes=[bps[pi]])
                    copy_evac(d_fm[:, g, h * 512:(h + 1) * 512], psv(pi), [bps[pi]], [b_d])
            for g in range(4):
                for h in range(2):
                    pi = ps_next()
                    P.op("pe", lambda e, pi=pi, g=g, h=h: e.matmul(
                        psv(pi), lhsT=wpool[:, g, :], rhs=d_fm[:, g, h * 512:(h + 1) * 512], start=True, stop=True),
                        reads=[bf("wpool"), b_d], writes=[bps[pi]])
                    copy_evac(y_a[:, g, h * 512:(h + 1) * 512], psv(pi), [bps[pi], bf("pscale")], [b_ya],
                              scale=pscale[:, l, g:g + 1])
            if l == 0:
                dump("y_a", y_a, [128, 4, T], [b_ya], BF16)


            st_banks = []
            for t in range(NT):
                pi = ps_next()
                st_banks.append(pi)
                for hd in range(4):
                    P.op("pe", lambda e, pi=pi, hd=hd, t=t: e.matmul(
                        psv(pi)[:, hd * 128:(hd + 1) * 128], lhsT=kf_fm[:, hd, t * 128:(t + 1) * 128], rhs=q_fm[:, hd, t * 128:(t + 1) * 128],
                        start=True, stop=True), reads=[b_kf, b_q], writes=[bps[pi]])
                P.op("dve", lambda e, pi=pi, t=t: e.tensor_tensor(
                    out=p0t[:, t, :].rearrange("p (h s) -> p h s", h=4), in0=psv(pi).rearrange("p (h s) -> p h s", h=4),
                    in1=masks[:, 0, :].unsqueeze(1).to_broadcast([128, 4, 128]), op=ALU.mult),
                    reads=[bps[pi], bf("masks")], writes=[b_p0t])
            for t in range(NT):
                pi = st_banks[t]
                P.op("dve", lambda e, pi=pi, t=t: e.tensor_tensor(
                    out=p0b[:, t, :].rearrange("p (h s) -> p h s", h=4), in0=psv(pi).rearrange("p (h s) -> p h s", h=4),
                    in1=masks[:, 1, :].unsqueeze(1).to_broadcast([128, 4, 128]), op=ALU.mult),
                    reads=[bps[pi], bf("masks")], writes=[b_p0b])
            pmat = [(p0t, b_p0t), (p0b, b_p0b)]

            bCs = [bf("Cst0"), bf("Cst1")]
            bCf = [bf("Cfin0"), bf("Cfin1")]
            bCd = [bf("Cd0"), bf("Cd1")]
            bhh = [[bf(f"h{t}_{hd}") for hd in range(4)] for t in range(NT)]
            for t in range(NT):
                for hd in range(4):
                    alias(bhh[t][hd], b_h)
            h_open = P.op("dve", lambda e: e.memset(dn[:, 0:1], 0.0), writes=[b_h, bf("dn0")])

            def cin_load(slot, dr):
                P.dma("sp", cin[dr][:], cinit_in[l, slot].rearrange("p (a b) -> p a b", a=8)[:, dr * 4:dr * 4 + 4, :],
                      writes=[bf(f"cin{dr}")])

            for dr in range(2):
                cin_load(0, dr)

            def prep(j, dr):
                c = j if dr == 0 else NT - 1 - j
                ds = slice(dr * 4, dr * 4 + 4)
                if j % 2 == 0:
                    slot = j // 2
                    ci, bci = cin[dr], bf(f"cin{dr}")
                    P.op("dve", lambda e: e.scalar_tensor_tensor(
                        out=Cst[:, ds, :], in0=Cfin[:, ds, :], scalar=keep[:, slot:slot + 1], in1=ci[:],
                        op0=ALU.mult, op1=ALU.add), reads=[bf("keep"), bci, bCf[dr]], writes=[bCs[dr]])
                    if slot + 1 < 4:
                        cin_load(slot + 1, dr)
                P.op("dve", lambda e: e.tensor_tensor(
                    out=Cd[:, ds, :], in0=Cst[:, ds, :], in1=decay[:, dr, c, :].unsqueeze(2).to_broadcast([128, 4, 129]), op=ALU.mult),
                    reads=[bCs[dr], bf("decay")], writes=[bCd[dr]])
                cb = Cd_bf[j % 2]
                P.op("act", lambda e: e.activation(out=cb[:, ds, 0:129], in_=Cd[:, ds, :], func=AF.Copy),
                     reads=[bCd[dr]], writes=[bf(f"Cdbf{j % 2}_{dr}")])

            for dr in range(2):
                prep(0, dr)
            ada_cols = psum[3][:, 960:964]
            ada_prev = [None]
            for j in range(NT):
                tiles = [j, NT - 1 - j]
                for dr in range(2):
                    t_ = tiles[dr]
                    upv = psum[2 + dr][:].rearrange("p (h c) -> p h c", h=4)
                    bup = [bps[4 + dr * 2], bps[5 + dr * 2]]
                    for hd in range(4):
                        P.op("pe", lambda e, hd=hd, t_=t_, dr=dr, upv=upv: e.matmul(
                            upv[:, hd, 0:129], lhsT=k_tm[:, t_, hd * 128:(hd + 1) * 128], rhs=vw_ext[:, t_, dr * 4 + hd, 0:129],
                            start=True, stop=True), reads=[b_kt, bf("vw")], writes=[bup[hd // 2]])
                for dr in range(2):
                    t_ = tiles[dr]
                    ndv = psum[dr][:].rearrange("p (h c) -> p h c", h=4)
                    bnd = [bps[dr * 2], bps[dr * 2 + 1]]
                    pm, bpm = pmat[dr]
                    cb = Cd_bf[j % 2]
                    for hd in range(4):
                        P.op("pe", lambda e, hd=hd, t_=t_, dr=dr, ndv=ndv, pm=pm: e.matmul(
                            ndv[:, hd, 0:129], lhsT=pm[:, t_, hd * 128:(hd + 1) * 128], rhs=vw_ext[:, t_, dr * 4 + hd, 0:129],
                            start=True, stop=False), reads=[bpm, bf("vw")], writes=[bnd[hd // 2]])
                        P.op("pe", lambda e, hd=hd, t_=t_, dr=dr, ndv=ndv, cb=cb: e.matmul(
                            ndv[:, hd, 0:129], lhsT=q_fm[:, hd, t_ * 128:(t_ + 1) * 128], rhs=cb[:, dr * 4 + hd, 0:129],
                            start=False, stop=True), reads=[b_q, bf(f"Cdbf{j % 2}_{dr}")], writes=[bnd[hd // 2]])
                for dr in range(2):
                    ds = slice(dr * 4, dr * 4 + 4)
                    upv = psum[2 + dr][:].rearrange("p (h c) -> p h c", h=4)
                    bup = [bps[4 + dr * 2], bps[5 + dr * 2]]
                    last = (j % 2 == 1)
                    dst, bdst = (Cfin, bCf[dr]) if last else (Cst, bCs[dr])
                    P.op("dve", lambda e, ds=ds, upv=upv, dst=dst: e.tensor_tensor(
                        out=dst[:, ds, :], in0=Cd[:, ds, :], in1=upv[:, :, 0:129], op=ALU.add),
                        reads=bup + [bCd[dr]], writes=[bdst])
                    if last:
                        slot = j // 2
                        out_toks.append(P.dma("sp", c_out[l, slot, dr].rearrange("p (h c) -> p h c", h=4), Cfin[:, ds, :], reads=[bdst]))
                if ada_l or ada_prev[0] is not None:
                    if ada_prev[0] is not None:
                        ada_evac(ada_prev[0][0], ada_prev[0][1], ada_cols, [bps[7]])
                        ada_prev[0] = None
                    if ada_l:
                        al, ang = ada_l.pop(0)
                        ada_mm(al, ang, ada_cols, [bps[7]])
                        ada_prev[0] = (al, ang)
                if j + 1 < NT:
                    for dr in range(2):
                        prep(j + 1, dr)
                for dr in range(2):
                    t_ = tiles[dr]
                    ds = slice(dr * 4, dr * 4 + 4)
                    ndv = psum[dr][:].rearrange("p (h c) -> p h c", h=4)
                    bnd = [bps[dr * 2], bps[dr * 2 + 1]]
                    bdn, brd = bf(f"dn{dr}"), bf(f"rd{dr}")
                    P.op("dve", lambda e, ds=ds, t_=t_, ndv=ndv: e.tensor_tensor(
                        out=dn[:, ds], in0=ndv[:, :, 128], in1=clampt[:, t_, ds], op=ALU.max),
                        reads=bnd + [bf("clampt")], writes=[bdn])
                    P.op("dve", lambda e, ds=ds, ndv=ndv: e.scalar_tensor_tensor(
                        out=dn[:, ds], in0=ndv[:, :, 128], scalar=-1.0, in1=dn[:, ds], op0=ALU.mult, op1=ALU.max),
                        reads=bnd + [bdn], writes=[bdn])
                    P.op("dve", lambda e, ds=ds: e.reciprocal(out=rd[:, ds], in_=dn[:, ds]), reads=[bdn], writes=[brd])
                    if j < NT // 2:
                        P.op("dve", lambda e, ds=ds, t_=t_, ndv=ndv: e.tensor_tensor(
                            out=h_tm[:, t_, :, :], in0=ndv[:, :, 0:128],
                            in1=rd[:, ds].unsqueeze(2).to_broadcast([128, 4, 128]), op=ALU.mult),
                            reads=bnd + [brd], writes=bhh[t_], extra=[h_open])
                    else:
                        for hd in range(4):
                            P.op("dve", lambda e, hd=hd, dr=dr, t_=t_, ndv=ndv: e.scalar_tensor_tensor(
                                out=h_tm[:, t_, hd, :], in0=ndv[:, hd, 0:128], scalar=rd[:, dr * 4 + hd:dr * 4 + hd + 1],
                                in1=h_tm[:, t_, hd, :], op0=ALU.mult, op1=ALU.add),
                                reads=[bnd[hd // 2], brd, bhh[t_][hd]], writes=[bhh[t_][hd]])
            if ada_prev[0] is not None:
                ada_evac(ada_prev[0][0], ada_prev[0][1], ada_cols, [bps[7]])
            P.op("dve", lambda e: e.memset(dn[:, 0:1], 0.0), reads=[bx_ for a_ in bhh for bx_ in a_], writes=[b_h, bf("dn0")])
            if l == 0:
                dump("h", h_tm.rearrange("p t h e -> p (t h e)"), [128, NT * 512], [b_h])

            wv, wb_ = w_in_unit(l, C_O)
            proj_tm(wv, wb_, 512, lambda pi, t: P.op("act", lambda e: e.activation(out=o_tm[:, t, :], in_=psv(pi), func=AF.Sigmoid),
                                                     reads=[bps[pi]], writes=[b_o]))

            junk = ftile[0]
            bjunk = [bf(f"junk{hd}") for hd in range(4)]
            for hd in range(4):
                alias(bjunk[hd], bft[0])
            pi_ada = ps_next()
            ps_reserved.add(pi_ada)
            for t in range(NT):
                hsq, bhsq = ftile[1 + t % 2], bft[1 + t % 2]
                hss, bhss = hss2[:, t % 2, :], bf(f"hss{t % 2}")
                yct, byct = tmp_b[t % 2], bf(f"tmpb{t % 2}")
                for hd in range(4):
                    P.op("act", lambda e, t=t, hd=hd, hss=hss: e.activation(
                        out=junk[:, hd * 128:(hd + 1) * 128], in_=h_tm[:, t, hd, :], func=AF.Square, accum_out=hss[:, hd:hd + 1]),
                        reads=[b_h], writes=[bjunk[hd], bf(f"hss{t % 2}_{hd}")], extra=[bhss.last_write] + list(bhss.reads))
                P.op("act", lambda e, hss=hss: e.activation(out=hss, in_=hss, func=AF.Ln, scale=1.0 / 128, bias=EPS),
                     reads=[bf(f"hss{t % 2}_{hd}") for hd in range(4)], writes=[bhss])
                P.op("act", lambda e, hss=hss: e.activation(out=hss, in_=hss, func=AF.Exp, scale=-0.5), reads=[bhss], writes=[bhss])
                P.op("dve", lambda e, t=t, hsq=hsq, hss=hss: e.tensor_tensor(
                    out=hsq[:].rearrange("p (h e) -> p h e", h=4), in0=h_tm[:, t, :, :],
                    in1=hss.unsqueeze(2).to_broadcast([128, 4, 128]), op=ALU.mult),
                    reads=[b_h, bhss], writes=[bhsq])
                P.op("dve", lambda e, hsq=hsq: e.tensor_tensor(out=hsq[:], in0=hsq[:], in1=gml_rep[:], op=ALU.mult),
                     reads=[bhsq, bf("gml_rep")], writes=[bhsq])
                P.op("dve", lambda e, t=t, hsq=hsq, yct=yct: e.tensor_tensor(out=yct[:], in0=hsq[:], in1=o_tm[:, t, :], op=ALU.mult),
                     reads=[bhsq, b_o], writes=[byct])
                pi = ps_next()
                for hd in range(4):
                    P.op("pe", lambda e, pi=pi, hd=hd, yct=yct: e.matmul(
                        psv(pi)[:, hd * 128:(hd + 1) * 128], lhsT=yct[:, hd * 128:(hd + 1) * 128], rhs=ident_bf[:], start=True, stop=True),
                        reads=[byct, bf("ident_bf")], writes=[bps[pi]])
                copy_evac(yc_fm[:, :, t * 128:(t + 1) * 128], psv(pi).rearrange("p (h s) -> p h s", h=4), [bps[pi]], [b_q])
                if t % 2 == 1 and t // 2 < len(ada_g):
                    u = t // 2
                    ada_mm(ada_g[u][0], ada_g[u][1], psv(pi_ada)[:, 4 * u:4 * u + 4], [bps[pi_ada]])
            for u, (al, ang) in enumerate(ada_g):
                ada_evac(al, ang, psv(pi_ada)[:, 4 * u:4 * u + 4], [bps[pi_ada]])
            ps_reserved.discard(pi_ada)
            if l == 0:
                dump("y_c", yc_fm, [128, 4, T], [b_q], BF16)

            ybr = [(y_a, b_ya), (yb_fm, b_su), (yc_fm, b_q)]
            for cg in range(2):
                for r in range(3):
                    wbv, wbb = w_in_unit(l, C_BR + r * D + cg * 512)
                    wrv, wrb = load_w(w_branch[l, r, :, cg * 512:(cg + 1) * 512].rearrange("(k p) n -> p k n", p=128), 4, 512)
                    yv, yb_ = ybr[r]
                    for n4 in range(4):
                        for h in range(2):
                            hs = slice(h * 512, (h + 1) * 512)
                            pg = ps_next()
                            for kc in range(KC):
                                P.op("pe", lambda e, pg=pg, kc=kc, n4=n4, hs=hs, wbv=wbv: e.matmul(
                                    psv(pg), lhsT=wbv[:, kc, n4 * 128:(n4 + 1) * 128], rhs=u_fm[:, kc, hs],
                                    start=(kc == 0), stop=(kc == KC - 1)), reads=[wbb, bu[h]], writes=[bps[pg]])
                            pb = ps_next()
                            for kc in range(4):
                                P.op("pe", lambda e, pb=pb, kc=kc, n4=n4, hs=hs, wrv=wrv, yv=yv: e.matmul(
                                    psv(pb), lhsT=wrv[:, kc, n4 * 128:(n4 + 1) * 128], rhs=yv[:, kc, hs],
                                    start=(kc == 0), stop=(kc == 3)), reads=[wrb, yb_], writes=[bps[pb]])
                            sg, bsg = ftile[(n4 * 2 + h) % 3], bft[(n4 * 2 + h) % 3]
                            P.op("act", lambda e, pg=pg, sg=sg: e.activation(out=sg[:], in_=psv(pg), func=AF.Sigmoid),
                                 reads=[bps[pg]], writes=[bsg])
                            ba = bacc[n4][h]
                            if r == 0:
                                P.op("dve", lambda e, pb=pb, sg=sg, n4=n4, hs=hs: e.tensor_tensor(
                                    out=acc_fm[:, n4, hs], in0=psv(pb), in1=sg[:], op=ALU.mult),
                                    reads=[bps[pb], bsg], writes=[ba])
                            else:
                                P.op("dve", lambda e, pb=pb, sg=sg: e.tensor_tensor(out=sg[:], in0=psv(pb), in1=sg[:], op=ALU.mult),
                                     reads=[bps[pb], bsg], writes=[bsg])
                                if r == 1:
                                    P.op("dve", lambda e, sg=sg, n4=n4, hs=hs: e.tensor_tensor(
                                        out=acc_fm[:, n4, hs], in0=acc_fm[:, n4, hs], in1=sg[:], op=ALU.add),
                                        reads=[bsg, ba], writes=[ba])
                                else:
                                    P.op("dve", lambda e, sg=sg, n4=n4, hs=hs, cg=cg: e.tensor_tensor(
                                        out=merged[:, cg * 4 + n4, hs], in0=acc_fm[:, n4, hs], in1=sg[:], op=ALU.add),
                                        reads=[bsg, ba], writes=[b_mrg])
            if l == 0:
                dump("merged", merged, [128, 8, T], [b_mrg], BF16)

            for cg in range(2):
                wv, wb_ = load_w(w_out[l, :, cg * 512:(cg + 1) * 512].rearrange("(k p) n -> p k n", p=128), KC, 512)
                for n4 in range(4):
                    n = cg * 4 + n4
                    for h in range(2):
                        hs = slice(h * 512, (h + 1) * 512)
                        pi = ps_next()
                        for kc in range(KC):
                            P.op("pe", lambda e, pi=pi, kc=kc, n4=n4, hs=hs, wv=wv: e.matmul(
                                psv(pi), lhsT=wv[:, kc, n4 * 128:(n4 + 1) * 128], rhs=merged[:, kc, hs],
                                start=(kc == 0), stop=(kc == KC - 1)), reads=[wb_, b_mrg], writes=[bps[pi]])
                        P.op("dve", lambda e, pi=pi, n=n, hs=hs: e.scalar_tensor_tensor(
                            out=x_fm[:, n, hs], in0=psv(pi), scalar=modfm[:, l, 16 + n:17 + n], in1=x_fm[:, n, hs],
                            op0=ALU.mult, op1=ALU.add), reads=[bps[pi], bmod[l][1], bx[n][h]], writes=[bx[n][h]])
            if l == 0:
                dump("xmid", x_fm[:].rearrange("p c t -> p (c t)"), [128, KC * T], [b for bb in bx for b in bb])

            rmsnorm_to_u(l, 1)
            for kg in range(4):
                hb, bhb = hff[kg % 2], bhff[kg % 2]
                for cg in range(2):
                    wv, wb_ = load_w(w_ff1[l, :, kg * 1024 + cg * 512: kg * 1024 + (cg + 1) * 512].rearrange("(k p) n -> p k n", p=128), KC, 512)
                    for n4 in range(4):
                        for h in range(2):
                            hs = slice(h * 512, (h + 1) * 512)
                            pi = ps_next()
                            for kc in range(KC):
                                P.op("pe", lambda e, pi=pi, kc=kc, n4=n4, hs=hs, wv=wv: e.matmul(
                                    psv(pi), lhsT=wv[:, kc, n4 * 128:(n4 + 1) * 128], rhs=u_fm[:, kc, hs],
                                    start=(kc == 0), stop=(kc == KC - 1)), reads=[wb_, bu[h]], writes=[bps[pi]])
                            tb, btb = tmp_b[(n4 * 2 + h) % 2], bf(f"tmpb{(n4 * 2 + h) % 2}")
                            P.op("act", lambda e, pi=pi, tb=tb: e.activation(out=tb[:], in_=psv(pi), func=AF.Relu),
                                 reads=[bps[pi]], writes=[btb])
                            P.op("dve", lambda e, tb=tb, hb=hb, cg=cg, n4=n4, hs=hs: e.tensor_tensor(
                                out=hb[:, cg * 4 + n4, hs], in0=tb[:], in1=tb[:], op=ALU.mult), reads=[btb], writes=[bhb])
                for cg in range(2):
                    wv, wb_ = load_w(w_ff2[l, kg * 1024:(kg + 1) * 1024, cg * 512:(cg + 1) * 512].rearrange("(k p) n -> p k n", p=128), KC, 512)
                    for n4 in range(4):
                        n = cg * 4 + n4
                        for h in range(2):
                            hs = slice(h * 512, (h + 1) * 512)
                            pi = ps_next()
                            for kc in range(KC):
                                P.op("pe", lambda e, pi=pi, kc=kc, n4=n4, hs=hs, wv=wv, hb=hb: e.matmul(
                                    psv(pi), lhsT=wv[:, kc, n4 * 128:(n4 + 1) * 128], rhs=hb[:, kc, hs],
                                    start=(kc == 0), stop=(kc == KC - 1)), reads=[wb_, bhb], writes=[bps[pi]])
                            P.op("dve", lambda e, pi=pi, n=n, hs=hs: e.scalar_tensor_tensor(
                                out=x_fm[:, n, hs], in0=psv(pi), scalar=modfm[:, l, 40 + n:41 + n], in1=x_fm[:, n, hs],
                                op0=ALU.mult, op1=ALU.add), reads=[bps[pi], bmod[l][1], bx[n][h]], writes=[bx[n][h]])
            dump(f"x{l}", x_fm[:].rearrange("p c t -> p (c t)"), [128, KC * T], [b for bb in bx for b in bb])

        for l in range(depth):
            layer(l)

        for h in range(2):
            hs = slice(h * 512, (h + 1) * 512)
            ms_rstd(h)
            for c in range(KC):
                P.op("dve", lambda e, c=c, hs=hs: e.scalar_tensor_tensor(
                    out=x_fm[:, c, hs], in0=x_fm[:, c, hs], scalar=gfin[:, c:c + 1], in1=ftile[2][:], op0=ALU.mult, op1=ALU.mult),
                    reads=[bx[c][h], bft[2], bf("gfin")], writes=[bx[c][h]])
        for t in range(NT):
            ot, bo = io_tm[t % 2], bio[t % 2]
            h = t // 4
            for cg in range(2):
                pi = ps_next()
                for c4 in range(4):
                    c = cg * 4 + c4
                    P.op("pe", lambda e, pi=pi, c4=c4, c=c, t=t: e.matmul(
                        psv(pi)[:, c4 * 128:(c4 + 1) * 128], lhsT=x_fm[:, c, t * 128:(t + 1) * 128], rhs=ident[:], start=True, stop=True),
                        reads=[bx[c][h], bf("ident")], writes=[bps[pi]])
                copy_evac(ot[:, cg * 512:(cg + 1) * 512], psv(pi), [bps[pi]], [bo])
            out_toks.append(P.dma("sp", y_out[t * 128:(t + 1) * 128, :], ot, reads=[bo]))

        P.wait_all("sp", out_toks + dbg_toks)
        P.emit(block)
        n_inst = P.n_inst
    return nc, n_inst


def _centred_weights(n, w):
    t = np.arange(n)
    lo = np.clip(t - w // 2, 0, n)
    hi = np.clip(t + (w - w // 2), 0, n)
    m = np.zeros((n, n), np.float64)
    for i in range(n):
        m[lo[i]:hi[i], i] = 1.0 / (hi[i] - lo[i])
    return m


def _pool_mats(grid):
    out = np.zeros((4, T, T), np.float32)
    for g, w in enumerate((2, 4, 8, 16)):
        if grid:
            mc = _centred_weights(64, w)
            mr = _centred_weights(16, w)
            m = np.kron(mr, mc)
        else:
            m1 = _centred_weights(256, w)
            m = np.zeros((T, T))
            for k in range(4):
                m[k * 256:(k + 1) * 256, k * 256:(k + 1) * 256] = m1
        out[g] = (m - np.eye(T)).astype(np.float32)
    return out.astype(ml_dtypes.bfloat16)


_CONST = {}


def _consts():
    if not _CONST:
        s = np.arange(128)
        mf = (s[:, None] <= s[None, :]).astype(np.float32)
        mb = (s[:, None] >= s[None, :]).astype(np.float32)
        _CONST["masks"] = np.ascontiguousarray(np.stack([mf, mb], axis=1).reshape(128, 256))
        _CONST["ident"] = np.eye(128, dtype=np.float32)
        _CONST["pool_grid"] = _pool_mats(True)
        _CONST["pool_seq"] = _pool_mats(False)
    return _CONST


_PROG = {}


def kernel(x_prompt, x_sample, state_C, state_n, state_m, c, c_ctx, w_ada, b_ada, g_norm1, g_norm2, w_in,
           b_gates, w_pool, pool_scale, g_sgu, w_sgu, b_sgu, g_mlstm, w_branch, w_out, w_ff1, w_ff2, g_final,
           _depth=DEPTH, _dbg=()):
    f = np.float32
    A = lambda a: np.ascontiguousarray(np.asarray(a, dtype=f))
    x_prompt, x_sample = A(x_prompt), A(x_sample)
    state_C, state_n, state_m = A(state_C), A(state_n), A(state_m)
    cst = _consts()

    def fm(vec, nchunk):
        return np.ascontiguousarray(np.asarray(vec, f).reshape(nchunk, 128).T)

    shared = {
        "masks": cst["masks"], "ident": cst["ident"],
        "w_ada": A(w_ada),
        "bada": np.ascontiguousarray(A(b_ada).reshape(DEPTH, 48, 128).transpose(2, 0, 1).reshape(128, DEPTH * 48)),
        "gn": np.ascontiguousarray(np.stack([A(g_norm1).reshape(DEPTH, KC, 128), A(g_norm2).reshape(DEPTH, KC, 128)], axis=1)
                                   .transpose(3, 0, 1, 2).reshape(128, DEPTH * 2 * KC)),
        "w_in": A(w_in), "b_gates": A(b_gates), "w_pool": A(w_pool),
        "pscale": np.ascontiguousarray(A(pool_scale).reshape(DEPTH, 4, 128).transpose(2, 0, 1).reshape(128, DEPTH * 4)),
        "gsgu": np.ascontiguousarray(A(g_sgu).reshape(DEPTH, 4, 128).transpose(2, 0, 1).reshape(128, DEPTH * 4)),
        "w_sguT": np.ascontiguousarray(A(w_sgu).transpose(0, 3, 1, 2).reshape(DEPTH, 128, 512)),
        "b_sgu": A(b_sgu).reshape(DEPTH, 512), "g_mlstm": A(g_mlstm),
        "w_branch": A(w_branch), "w_out": A(w_out), "w_ff1": A(w_ff1), "w_ff2": A(w_ff2),
        "gfin": fm(g_final, KC),
    }
    zeros_c = np.zeros((DEPTH, 4, 128, 8 * 129), f)
    zeros_m = np.zeros((128, DEPTH * 4 * 8), f)
    in_maps = []
    for core in range(8):
        m = dict(shared)
        if core < 2:
            b = core
            m["x"] = x_sample[b]
            m["cond"] = fm(A(c)[b], KC)
            keep = np.zeros((128, 4), f)
            keep[:, 1:] = 1.0
            ci = np.zeros((DEPTH, 4, 128, 8, 129), f)
            ci[:, 0, :, :, 0:128] = state_C[b].transpose(0, 3, 1, 2, 4).reshape(DEPTH, 128, 8, 128)
            ci[:, 0, :, :, 128] = state_n[b].transpose(0, 3, 1, 2).reshape(DEPTH, 128, 8)
            mi = np.zeros((DEPTH, 4, 8), f)
            mi[:, 0, :] = state_m[b].reshape(DEPTH, 8)
            m["keep"] = keep
            m["cinit"] = ci.reshape(DEPTH, 4, 128, 8 * 129)
            m["minit"] = np.ascontiguousarray(np.broadcast_to(mi.reshape(1, -1), (128, DEPTH * 32)))
            m["poolm"] = cst["pool_grid"]
        else:
            g = min(core, 5) - 2
            m["x"] = np.ascontiguousarray(x_prompt[4 * g:4 * g + 4].reshape(T, D))
            m["cond"] = fm(c_ctx, KC)
            m["keep"] = np.zeros((128, 4), f)
            m["cinit"] = zeros_c
            m["minit"] = zeros_m
            m["poolm"] = cst["pool_seq"]
        in_maps.append(m)

    key = (_depth, tuple(_dbg))
    if key not in _PROG:
        _PROG[key] = build_program(_depth, _dbg)
    nc, n_inst = _PROG[key]
    res = run_bass_kernel_spmd(nc, in_maps, core_ids=list(range(8)))
    R = res.results

    y_sample = np.stack([R[0]["y"], R[1]["y"]], axis=0).astype(f)
    y_prompt = np.concatenate([R[cidx]["y"].reshape(4, 256, D) for cidx in range(2, 6)], axis=0).astype(f)
    B = x_prompt.shape[0]
    new_C = np.zeros((B, DEPTH, 2, 4, 128, 128), f)
    new_n = np.zeros((B, DEPTH, 2, 4, 128), f)
    new_m = np.zeros((B, DEPTH, 2, 4), f)
    for cidx in range(2, 6):
        co = R[cidx]["c_out"].reshape(DEPTH, 4, 2, 128, 4, 129)
        mo = R[cidx]["m_out"].reshape(DEPTH, 4, 2, 4)
        for slot in range(4):
            for dr in range(2):
                s = 4 * (cidx - 2) + (slot if dr == 0 else 3 - slot)
                new_C[s, :, dr] = co[:, slot, dr, :, :, 0:128].transpose(0, 2, 1, 3)
                new_n[s, :, dr] = co[:, slot, dr, :, :, 128].transpose(0, 2, 1)
                new_m[s, :, dr] = mo[:, slot, dr]
    if _dbg:
        kernel.dbg = [{k: v for k, v in r.items() if k.startswith("dbg_")} for r in R]
    return (y_prompt, y_sample, new_C, new_n, new_m)
```

```python
import os
from contextlib import ExitStack

import numpy as np
import ml_dtypes
import concourse.bass as bass
import concourse.mybir as mybir
from concourse.bass_utils import run_bass_kernel_spmd

F32 = mybir.dt.float32
BF16 = mybir.dt.bfloat16
ALU = mybir.AluOpType
AF = mybir.ActivationFunctionType
AX = mybir.AxisListType

D = 1024
T = 1024
DEPTH = 4
DIN = 6672
DFF = 4096
NT = 8
KC = 8
EPS = 1e-6
SEM_CAP = 24000
NSLOT = 4
C_XP, C_SU, C_SV, C_Q, C_K, C_V, C_O, C_G, C_BR = 0, 512, 1024, 1536, 2048, 2560, 3072, 3584, 3600


class Buf:
    __slots__ = ("name", "last_write", "reads", "aliases")

    def __init__(self, name):
        self.name = name
        self.last_write = None
        self.reads = []
        self.aliases = []


def alias(*bufs):
    for a in bufs:
        for b in bufs:
            if a is not b and b not in a.aliases:
                a.aliases.append(b)


class Prog:
    ENGS = ("pe", "act", "dve", "pool", "sp")

    def __init__(self, nc, ctx):
        self.nc = nc
        self.ctx = ctx
        self.streams = {e: [] for e in self.ENGS}
        self.cur_sem = {}
        self.cur_cnt = {}
        self.epoch = {}
        for e in self.ENGS:
            self._new_epoch(e, first=True)
        self.waited = {e: {} for e in self.ENGS}
        self.n_inst = 0
        self._dma_sems = {}
        self._dma_rr = {}

    def _new_sem(self, name):
        return self.ctx.enter_context(self.nc.semaphore(name))

    def _new_epoch(self, e, first=False):
        self.epoch[e] = 0 if first else self.epoch[e] + 1
        self.cur_sem[e] = self._new_sem(f"s_{e}_{self.epoch[e]}")
        self.cur_cnt[e] = 0

    def _emit_waits(self, e, deps):
        need = {}
        w = self.waited[e]
        for t in deps:
            if t is None:
                continue
            sem, val, teng, tep = t
            key = id(sem)
            if w.get(key, 0) >= val:
                continue
            if teng is not None and w.get(("ep", teng), -1) > tep:
                continue
            if key not in need or need[key][1] < val:
                need[key] = (sem, val, teng, tep)
        for key, (sem, val, teng, tep) in need.items():
            self.streams[e].append(("wait", sem, val))
            w[key] = val
            if teng is not None:
                w[("ep", teng)] = max(w.get(("ep", teng), -1), tep)

    def _deps(self, reads, writes, extra):
        deps = list(extra)
        for b in reads:
            deps.append(b.last_write)
        for b in writes:
            deps.append(b.last_write)
            deps.extend(b.reads)
            for a in b.aliases:
                deps.append(a.last_write)
                deps.extend(a.reads)
        return deps

    def _commit(self, tok, reads, writes):
        for b in reads:
            b.reads.append(tok)
        for b in writes:
            b.last_write = tok
            b.reads = []

    def op(self, e, fn, reads=(), writes=(), extra=()):
        deps = self._deps(reads, writes, extra)
        if e == "pe":
            deps = [t for t in deps if t is not None and t[2] != "pe"]
        self._emit_waits(e, deps)
        if self.cur_cnt[e] >= SEM_CAP:
            self._new_epoch(e)
        self.cur_cnt[e] += 1
        tok = (self.cur_sem[e], self.cur_cnt[e], e, self.epoch[e])
        self.streams[e].append(("op", fn, self.cur_sem[e]))
        self.n_inst += 1
        self._commit(tok, reads, writes)
        return tok

    def _get_dma_sem(self, q):
        pool = self._dma_sems.setdefault(q, [])
        rr = self._dma_rr.setdefault(q, 0)
        if len(pool) < 20:
            s = [self._new_sem(f"dma_{q}{len(pool)}"), 0]
            pool.append(s)
            return s
        s = pool[rr % len(pool)]
        self._dma_rr[q] = rr + 1
        key = id(s[0])
        if self.waited[q].get(key, 0) < s[1]:
            self.streams[q].append(("wait", s[0], s[1]))
            self.waited[q][key] = s[1]
        return s

    def dma(self, q, out, in_, reads=(), writes=(), extra=()):
        deps = self._deps(reads, writes, extra)
        self._emit_waits(q, deps)
        s = self._get_dma_sem(q)
        s[1] += 16
        tok = (s[0], s[1], None, 0)
        self.streams[q].append(("dma", out, in_, s[0]))
        self.n_inst += 1
        self._commit(tok, reads, writes)
        return tok

    def wait_all(self, e, toks):
        self._emit_waits(e, toks)

    def emit(self, block):
        streams = self.streams

        def run(eng, lst):
            for it in lst:
                if it[0] == "wait":
                    eng.wait_ge(it[1], it[2])
                elif it[0] == "op":
                    it[1](eng).then_inc(it[2], 1)
                else:
                    eng.dma_start(out=it[1], in_=it[2]).then_inc(it[3], 16)

        @block.tensor
        def _(eng):
            run(eng, streams["pe"])

        @block.scalar
        def _(eng):
            run(eng, streams["act"])

        @block.vector
        def _(eng):
            run(eng, streams["dve"])

        @block.gpsimd
        def _(eng):
            run(eng, streams["pool"])

        @block.sync
        def _(eng):
            run(eng, streams["sp"])


def build_program(depth=DEPTH, dbg=()):
    nc = bass.Bass("TRN2", target_bir_lowering=False)

    def din(name, shape, dt=F32):
        return nc.dram_tensor(name, list(shape), dt, kind="ExternalInput").ap()

    def dout(name, shape, dt=F32):
        return nc.dram_tensor(name, list(shape), dt, kind="ExternalOutput").ap()

    x_in = din("x", [T, D])
    cond_in = din("cond", [128, KC])
    keep_in = din("keep", [128, 4])
    minit_in = din("minit", [128, DEPTH * 4 * 8])
    cinit_in = din("cinit", [DEPTH, 4, 128, 8 * 129])
    poolm_in = din("poolm", [4, T, T], BF16)
    masks_in = din("masks", [128, 2 * 128])
    ident_in = din("ident", [128, 128])
    w_ada = din("w_ada", [DEPTH, D, 6 * D])
    bada_in = din("bada", [128, DEPTH * 48])
    gn_in = din("gn", [128, DEPTH * 2 * KC])
    w_in = din("w_in", [DEPTH, D, DIN])
    b_gates = din("b_gates", [DEPTH, 16])
    w_pool = din("w_pool", [DEPTH, 4, 128, 128])
    pscale_in = din("pscale", [128, DEPTH * 4])
    gsgu_in = din("gsgu", [128, DEPTH * 4])
    w_sguT = din("w_sguT", [DEPTH, 128, 4 * 128])
    b_sgu = din("b_sgu", [DEPTH, 4 * 128])
    g_mlstm = din("g_mlstm", [DEPTH, 512])
    w_branch = din("w_branch", [DEPTH, 3, 512, D])
    w_out = din("w_out", [DEPTH, D, D])
    w_ff1 = din("w_ff1", [DEPTH, D, DFF])
    w_ff2 = din("w_ff2", [DEPTH, DFF, D])
    gfin_in = din("gfin", [128, KC])

    y_out = dout("y", [T, D])
    c_out = dout("c_out", [DEPTH, 4, 2, 128, 4 * 129])
    m_out = dout("m_out", [DEPTH, 4, 8])
    dbg_toks = []

    with ExitStack() as ctx:
        P = Prog(nc, ctx)

        def sb(name, shape, dt=F32):
            return ctx.enter_context(nc.sbuf_tensor("s_" + name, list(shape), dt))

        x_fm = sb("x_fm", [128, KC, T])
        u_fm = sb("u_fm", [128, KC, T], BF16)
        wslot = [sb(f"wslot{i}", [128, 4096], BF16) for i in range(NSLOT)]
        arena = sb("arena", [128, 6 * 4096], BF16)
        v_tm = sb("v_tm", [128, NT, 512], BF16)
        vw_ext = sb("vw_ext", [128, NT, 8, 130], BF16)
        p0t = sb("p0t", [128, NT, 512], BF16)
        ident = sb("ident", [128, 128])
        ident_bf = sb("ident_bf", [128, 128], BF16)
        masks = sb("masks", [128, 2, 128])
        ones_mean_bf = sb("ones_mean_bf", [128, 128], BF16)
        ones_f = sb("ones_f", [128, 128])
        cond_sb = sb("cond_sb", [128, KC])
        scond = sb("scond", [128, KC], BF16)
        keep = sb("keep", [128, 4])
        minit = sb("minit", [128, DEPTH, 4, 8])
        gn = sb("gn", [128, DEPTH, 2, KC])
        pscale = sb("pscale", [128, DEPTH, 4])
        gsgu = sb("gsgu", [128, DEPTH, 4])
        gfin = sb("gfin", [128, KC])
        bada = sb("bada", [128, DEPTH, 48])
        modfm = sb("modfm", [128, DEPTH, 48])
        AB = sb("AB", [128, DEPTH, 2, KC])
        wg = sb("wg", [128, KC, 16], BF16)
        bg_rep = sb("bg_rep", [128, 16])
        wpool = sb("wpool", [128, 4, 128], BF16)
        wsgu = sb("wsgu", [128, 4, 128], BF16)
        bsgu_rep = sb("bsgu_rep", [128, 4, 128])
        gml_rep = sb("gml_rep", [128, 512])
        ftile = [sb(f"ftile{i}", [128, 512]) for i in range(3)]
        modrow = ftile[0]
        tmp_b = [sb(f"tmpb{i}", [128, 512], BF16) for i in range(2)]
        ssq = sb("ssq", [128, NT])
        rsv = sb("rsv", [128, NT])
        gp = sb("gp", [128, NT, 16])
        gt0 = sb("gt0", [128, NT, 8])
        G_tm = sb("G_tm", [128, NT, 16])
        b_tm = sb("b_tm", [128, NT, 8])
        Q2 = sb("Q2", [128, 2])
        Dg = sb("Dg", [128, 128])
        Rmax = sb("Rmax", [128, NT, 16])
        Rsum = sb("Rsum", [128, NT, 16])
        mprev = sb("mprev", [128, 2, 9, 4])
        Mlast = sb("Mlast", [128, 2, 8, 4])
        decay = sb("decay", [128, 2, 8, 4])
        wst = sb("wst", [128, NT, 8])
        clampt = sb("clampt", [128, NT, 8])
        Cst = sb("Cst", [128, 8, 129])
        Cd = sb("Cd", [128, 8, 129])
        Cd_bf = [sb(f"Cd_bf{i}", [128, 8, 130], BF16) for i in range(2)]
        Cfin = sb("Cfin", [128, 8, 129])
        mstage = sb("mstage", [128, 4, 8])
        cin = [sb(f"cin{i}", [128, 4, 129]) for i in range(2)]
        dn = sb("dn", [128, 8])
        rd = sb("rd", [128, 8])
        hss2 = sb("hss2", [128, 2, 4])
        yc_tm = tmp_b[0]

        psum = [ctx.enter_context(nc.psum_tensor(f"psum{i}", [128, 1024], F32)) for i in range(4)]
        block = ctx.enter_context(nc.Block())

        def aview(i, shape_str, **kw):
            return arena[:, i * 4096:(i + 1) * 4096].rearrange(shape_str, **kw)

        xp_tm = aview(0, "p (t c) -> p t c", t=NT)
        vn_tm = aview(1, "p (t c) -> p t c", t=NT)
        su_fm = aview(2, "p (g t) -> p g t", g=4)
        q_fm = aview(3, "p (g t) -> p g t", g=4)
        kf_fm = aview(4, "p (g t) -> p g t", g=4)
        o_tm = aview(4, "p (t c) -> p t c", t=NT)
        p0b = aview(4, "p (t c) -> p t c", t=NT)
        k_tm = aview(5, "p (t c) -> p t c", t=NT)
        a01_f32 = arena[:, 0:8192].bitcast(F32)
        h_tm = a01_f32.rearrange("p (t h e) -> p t h e", t=NT, h=4)
        acc_fm = a01_f32.rearrange("p (c t) -> p c t", c=4)
        io_tm = [a01_f32[:, i * 1024:(i + 1) * 1024] for i in range(2)]
        hff = [arena[:, 8192 + i * 8192: 8192 + (i + 1) * 8192].rearrange("p (c t) -> p c t", c=8) for i in range(2)]
        yc_fm = q_fm
        yb_fm = su_fm
        y_a = v_tm[:].rearrange("p t c -> p (t c)").rearrange("p (g t) -> p g t", g=4)
        p0t_flat = p0t[:].rearrange("p t c -> p (t c)")
        d_fm = p0t_flat.rearrange("p (g t) -> p g t", g=4)
        merged = vw_ext[:].rearrange("p t g c -> p (t g c)")[:, 0:8192].rearrange("p (c t) -> p c t", c=8)

        B = {}

        def bf(name):
            if name not in B:
                B[name] = Buf(name)
            return B[name]

        bx = [[bf(f"x{c}_{h}") for h in range(2)] for c in range(KC)]
        bu = [bf(f"u{h}") for h in range(2)]
        bws = [bf(f"ws{i}") for i in range(NSLOT)]
        bps = [bf(f"ps{i}") for i in range(8)]
        bft = [bf(f"ftile{i}") for i in range(3)]
        b_xp, b_vn, b_su, b_q, b_kf, b_kt, b_o = (bf(n) for n in ("xp", "vn", "su", "q", "kf", "kt", "o"))
        b_h = bf("h")
        bacc = [[bf(f"acc{n}_{h}") for h in range(2)] for n in range(4)]
        bio = [bf("io0"), bf("io1")]
        bhff = [bf("hff0"), bf("hff1")]
        alias(b_h, b_xp, b_vn)
        for n in range(4):
            for h in range(2):
                alias(bacc[n][h], b_h)
                alias(bacc[n][h], b_xp)
                alias(bacc[n][h], b_vn)
        for i in range(2):
            alias(bio[i], b_xp)
            alias(bio[i], b_h)
            for n in range(4):
                for h in range(2):
                    alias(bio[i], bacc[n][h])
        alias(bhff[0], b_su, b_q)
        alias(bhff[1], b_kf, b_kt, b_o)
        b_p0t, b_mrg, b_d = bf("p0t"), bf("mrg"), bf("d")
        alias(b_p0t, b_d)
        b_p0b = bf("p0b")
        alias(b_p0b, b_kf)
        alias(b_p0b, b_o)
        alias(b_p0b, bhff[1])
        alias(b_mrg, bf("vw"))
        b_v, b_ya = bf("v"), bf("y_a")
        alias(b_v, b_ya)

        def psv(i):
            return psum[i // 2][:, (i % 2) * 512:(i % 2 + 1) * 512]

        ps_rr = [0]

        ps_reserved = set()

        def ps_next():
            while True:
                i = ps_rr[0] % 8
                ps_rr[0] += 1
                if i not in ps_reserved:
                    return i

        chain_q = []

        def drain(n):
            for _ in range(min(n, len(chain_q))):
                chain_q.pop(0)()

        ws_rr = [0]

        def load_w(view, kc, n):
            i = ws_rr[0] % NSLOT
            ws_rr[0] += 1
            dst = wslot[i][:, 0:kc * n].rearrange("p (k n) -> p k n", k=kc)
            P.dma("pool", dst, view, writes=[bws[i]])
            return dst, bws[i]

        def dump(name, ap, shape, reads, dt=F32):
            if name in dbg:
                o = dout("dbg_" + name, shape, dt)
                dbg_toks.append(P.dma("sp", o, ap, reads=reads))

        out_toks = []
        evac_rr = [0]

        def copy_evac(dst, src, reads, writes, scale=None):
            evac_rr[0] += 1
            if evac_rr[0] % 2 == 0:
                if scale is None:
                    P.op("act", lambda e: e.activation(out=dst, in_=src, func=AF.Copy), reads=reads, writes=writes)
                else:
                    P.op("act", lambda e: e.activation(out=dst, in_=src, func=AF.Copy, scale=scale), reads=reads, writes=writes)
            else:
                if scale is None:
                    P.op("dve", lambda e: e.tensor_copy(out=dst, in_=src), reads=reads, writes=writes)
                else:
                    P.op("dve", lambda e: e.tensor_scalar(out=dst, in0=src, scalar1=scale, scalar2=None, op0=ALU.mult),
                         reads=reads, writes=writes)

        P.dma("sp", ident[:], ident_in, writes=[bf("ident")])
        P.dma("sp", masks[:].rearrange("p a b -> p (a b)"), masks_in, writes=[bf("masks")])
        P.dma("sp", cond_sb[:], cond_in, writes=[bf("cond")])
        P.dma("sp", keep[:], keep_in, writes=[bf("keep")])
        P.dma("sp", minit[:].rearrange("p a b c -> p (a b c)"), minit_in, writes=[bf("minit")])
        P.dma("sp", gn[:].rearrange("p a b c -> p (a b c)"), gn_in, writes=[bf("gn")])
        P.dma("sp", bada[:].rearrange("p a b -> p (a b)"), bada_in, writes=[bf("bada")])
        P.dma("sp", pscale[:].rearrange("p a b -> p (a b)"), pscale_in, writes=[bf("pscale")])
        P.dma("sp", gsgu[:].rearrange("p a b -> p (a b)"), gsgu_in, writes=[bf("gsgu")])
        P.dma("sp", gfin[:], gfin_in, writes=[bf("gfin")])
        P.op("dve", lambda e: e.memset(ones_mean_bf[:], 1.0 / D), writes=[bf("ones_mean_bf")])
        P.op("dve", lambda e: e.memset(ones_f[:], 1.0), writes=[bf("ones_f")])
        P.op("dve", lambda e: e.tensor_copy(out=ident_bf[:], in_=ident[:]), reads=[bf("ident")], writes=[bf("ident_bf")])
        P.op("dve", lambda e: e.memset(mprev[:], 0.0), writes=[bf("mchain")])
        P.op("dve", lambda e: e.memset(Cst[:], 0.0), writes=[bf("Cst0"), bf("Cst1")])
        P.op("dve", lambda e: e.memset(Cfin[:], 0.0), writes=[bf("Cfin0"), bf("Cfin1")])
        P.op("dve", lambda e: e.memset(mstage[:], 0.0), writes=[bf("mstage")])
        for i in range(2):
            P.op("dve", lambda e, i=i: e.memset(Cd_bf[i][:], 0.0), writes=[bf(f"Cdbf{i}_0"), bf(f"Cdbf{i}_1")])
        P.op("dve", lambda e: e.memset(vw_ext[:], 0.0), writes=[bf("vw")])
        P.op("act", lambda e: e.activation(out=scond[:], in_=cond_sb[:], func=AF.Silu), reads=[bf("cond")], writes=[bf("scond")])

        for t in range(NT):
            it = io_tm[t % 2]
            bi = bio[t % 2]
            P.dma("sp", it, x_in[t * 128:(t + 1) * 128, :], writes=[bi])
            for cg in range(2):
                pi = ps_next()
                for c4 in range(4):
                    c = cg * 4 + c4
                    P.op("pe", lambda e, pi=pi, c4=c4, c=c, it=it: e.matmul(
                        psv(pi)[:, c4 * 128:(c4 + 1) * 128], lhsT=it[:, c * 128:(c + 1) * 128], rhs=ident[:],
                        start=True, stop=True), reads=[bi, bf("ident")], writes=[bps[pi]])
                copy_evac(x_fm[:, cg * 4:(cg + 1) * 4, t * 128:(t + 1) * 128], psv(pi).rearrange("p (c t) -> p c t", c=4),
                          [bps[pi]], [bx[c][t // 4] for c in range(cg * 4, cg * 4 + 4)])

        bmod = [[bf(f"mod{l}_{w}") for w in range(2)] for l in range(DEPTH)]
        bAB = [[bf(f"AB{l}_{w}") for w in range(2)] for l in range(DEPTH)]

        def ada_mm(l, ng, pcols, pbufs):
            wv, wb_ = load_w(w_ada[l, :, ng * 512:(ng + 1) * 512].rearrange("(k p) n -> p k n", p=128), KC, 512)
            for n4 in range(4):
                for kc in range(KC):
                    P.op("pe", lambda e, n4=n4, kc=kc: e.matmul(
                        pcols[:, n4:n4 + 1], lhsT=wv[:, kc, n4 * 128:(n4 + 1) * 128], rhs=scond[:, kc:kc + 1],
                        start=(kc == 0), stop=(kc == KC - 1)), reads=[wb_, bf("scond")], writes=pbufs)

        def ada_evac(l, ng, pcols, pbufs):
            piece = 0 if ng < 4 else 1
            P.op("dve", lambda e: e.tensor_tensor(out=modfm[:, l, ng * 4:(ng + 1) * 4], in0=pcols, in1=bada[:, l, ng * 4:(ng + 1) * 4], op=ALU.add),
                 reads=list(pbufs) + [bf("bada")], writes=[bmod[l][piece]])
            if ng == 3 or ng == 9:
                which = 0 if ng == 3 else 1
                off = 8 if which == 0 else 32
                P.op("dve", lambda e: e.scalar_tensor_tensor(
                    out=AB[:, l, which, :], in0=modfm[:, l, off:off + 8], scalar=1.0, in1=gn[:, l, which, :], op0=ALU.add, op1=ALU.mult),
                    reads=[bmod[l][piece], bf("gn")], writes=[bAB[l][which]])

        for ng in range(4):
            pi = ps_next()
            ada_mm(0, ng, psv(pi)[:, 0:4], [bps[pi]])
            ada_evac(0, ng, psv(pi)[:, 0:4], [bps[pi]])
        dump("xfm", x_fm[:].rearrange("p c t -> p (c t)"), [128, KC * T], [b for bb in bx for b in bb])

        def ms_rstd(h):
            hs = slice(h * 512, (h + 1) * 512)
            pi = ps_next()
            for c in range(KC):
                sq, bsq = tmp_b[c % 2], bf(f"tmpb{c % 2}")
                P.op("act", lambda e, sq=sq, c=c: e.activation(out=sq[:], in_=x_fm[:, c, hs], func=AF.Square),
                     reads=[bx[c][h]], writes=[bsq])
                P.op("pe", lambda e, sq=sq, c=c: e.matmul(psv(pi), lhsT=ones_mean_bf[:], rhs=sq[:], start=(c == 0), stop=(c == KC - 1)),
                     reads=[bsq, bf("ones_mean_bf")], writes=[bps[pi]])
            P.op("act", lambda e: e.activation(out=ftile[2][:], in_=psv(pi), func=AF.Ln, bias=EPS), reads=[bps[pi]], writes=[bft[2]])
            P.op("act", lambda e: e.activation(out=ftile[2][:], in_=ftile[2][:], func=AF.Exp, scale=-0.5), reads=[bft[2]], writes=[bft[2]])

        def rmsnorm_to_u(l, which):
            shoff = 0 if which == 0 else 24
            for h in range(2):
                hs = slice(h * 512, (h + 1) * 512)
                ms_rstd(h)
                for c in range(KC):
                    tf, btf = ftile[c % 2], bft[c % 2]
                    P.op("dve", lambda e, tf=tf, c=c, hs=hs: e.tensor_tensor(out=tf[:], in0=x_fm[:, c, hs], in1=ftile[2][:], op=ALU.mult),
                         reads=[bx[c][h], bft[2]], writes=[btf])
                    P.op("act", lambda e, tf=tf, c=c, hs=hs: e.activation(
                        out=u_fm[:, c, hs], in_=tf[:], func=AF.Identity,
                        scale=AB[:, l, which, c:c + 1], bias=modfm[:, l, shoff + c:shoff + c + 1]),
                        reads=[btf, bAB[l][which], bmod[l][which]], writes=[bu[h]])

        def proj_fm(wv, wb_, ncols, evac):
            for nchk in range(ncols // 128):
                for h in range(2):
                    pi = ps_next()
                    for kc in range(KC):
                        P.op("pe", lambda e, pi=pi, kc=kc, nchk=nchk, h=h: e.matmul(
                            psv(pi), lhsT=wv[:, kc, nchk * 128:(nchk + 1) * 128], rhs=u_fm[:, kc, h * 512:(h + 1) * 512],
                            start=(kc == 0), stop=(kc == KC - 1)), reads=[wb_, bu[h]], writes=[bps[pi]])
                    evac(pi, nchk, h)
                    drain(2)

        def proj_tm(wv, wb_, ncols, evac):
            for t in range(NT):
                pi = ps_next()
                for kc in range(KC):
                    P.op("pe", lambda e, pi=pi, kc=kc, t=t: e.matmul(
                        psv(pi)[:, 0:ncols], lhsT=u_fm[:, kc, t * 128:(t + 1) * 128], rhs=wv[:, kc, 0:ncols],
                        start=(kc == 0), stop=(kc == KC - 1)), reads=[wb_, bu[t // 4]], writes=[bps[pi]])
                evac(pi, t)
                drain(2)

        def w_in_unit(l, col, n=512):
            return load_w(w_in[l, :, col:col + n].rearrange("(k p) n -> p k n", p=128), KC, n)

        ksc = 128 ** -0.5

        def layer(l):
            P.dma("pool", wg[:], w_in[l, :, C_G:C_G + 16].rearrange("(k p) n -> p k n", p=128), writes=[bf("wg")])
            P.dma("pool", wpool[:], w_pool[l].rearrange("g c d -> c g d"), writes=[bf("wpool")])
            P.dma("pool", wsgu[:].rearrange("p g q -> p (g q)"), w_sguT[l], writes=[bf("wsgu")])
            P.dma("sp", bg_rep[:], b_gates[l, :].partition_broadcast(128), writes=[bf("bg_rep")])
            P.dma("sp", bsgu_rep[:].rearrange("p g q -> p (g q)"), b_sgu[l, :].partition_broadcast(128), writes=[bf("bsgu_rep")])
            P.dma("sp", gml_rep[:], g_mlstm[l, :].partition_broadcast(128), writes=[bf("gml_rep")])

            rmsnorm_to_u(l, 0)
            if l == 0:
                dump("u0", u_fm[:].rearrange("p c t -> p (c t)"), [128, KC * T], bu, BF16)

            wv, wb_ = w_in_unit(l, C_XP)
            proj_tm(wv, wb_, 512, lambda pi, t: copy_evac(xp_tm[:, t, :], psv(pi), [bps[pi]], [b_xp]))
            pi = ps_next()
            for t in range(NT):
                for kc in range(KC):
                    P.op("pe", lambda e, pi=pi, kc=kc, t=t: e.matmul(
                        psv(pi)[:, t * 16:(t + 1) * 16], lhsT=u_fm[:, kc, t * 128:(t + 1) * 128], rhs=wg[:, kc, :],
                        start=(kc == 0), stop=(kc == KC - 1)), reads=[bf("wg"), bu[t // 4]], writes=[bps[pi]])
            P.op("dve", lambda e, pi=pi: e.tensor_tensor(
                out=gp[:], in0=psv(pi)[:, 0:128].rearrange("p (t g) -> p t g", t=NT),
                in1=bg_rep[:, :].unsqueeze(1).to_broadcast([128, NT, 16]), op=ALU.add),
                reads=[bps[pi], bf("bg_rep")], writes=[bf("gp")])
            if l == 0:
                dump("gp", gp[:].rearrange("p t g -> p (t g)"), [128, 128], [bf("gp")])

            fg = gp[:, :, 8:16]
            P.op("dve", lambda e: e.scalar_tensor_tensor(out=gt0[:], in0=fg, scalar=-1.0, in1=fg, op0=ALU.mult, op1=ALU.max),
                 reads=[bf("gp")], writes=[bf("gt0")])
            P.op("act", lambda e: e.activation(out=gt0[:], in_=gt0[:], func=AF.Exp, scale=-1.0), reads=[bf("gt0")], writes=[bf("gt0")])
            P.op("act", lambda e: e.activation(out=gt0[:], in_=gt0[:], func=AF.Ln, bias=1.0), reads=[bf("gt0")], writes=[bf("gt0")])
            P.op("dve", lambda e: e.scalar_tensor_tensor(out=G_tm[:, :, 8:16], in0=fg, scalar=0.0, in1=gt0[:], op0=ALU.min, op1=ALU.subtract),
                 reads=[bf("gp"), bf("gt0")], writes=[bf("G_tm")])
            pi = ps_next()
            for t in range(NT):
                for dr in range(2):
                    P.op("pe", lambda e, pi=pi, t=t, dr=dr: e.matmul(
                        psv(pi)[:, t * 8 + dr * 4: t * 8 + dr * 4 + 4], lhsT=masks[:, dr, :], rhs=G_tm[:, t, 8 + dr * 4: 12 + dr * 4],
                        start=True, stop=True), reads=[bf("masks"), bf("G_tm")], writes=[bps[pi]])
            P.op("dve", lambda e, pi=pi: e.tensor_copy(out=b_tm[:], in_=psv(pi)[:, 0:64].rearrange("p (t g) -> p t g", t=NT)),
                 reads=[bps[pi]], writes=[bf("b_tm")])
            P.op("dve", lambda e: e.tensor_tensor(out=G_tm[:, :, 0:8], in0=gp[:, :, 0:8], in1=b_tm[:], op=ALU.subtract),
                 reads=[bf("gp"), bf("b_tm")], writes=[bf("G_tm")])
            pi = ps_next()
            P.op("pe", lambda e, pi=pi: e.matmul(psv(pi)[:, 0:128], lhsT=G_tm[:].rearrange("p t g -> p (t g)"), rhs=ident[:],
                                                 start=True, stop=True), reads=[bf("G_tm"), bf("ident")], writes=[bps[pi]])
            P.op("dve", lambda e, pi=pi: e.tensor_reduce(out=Q2[:, 0:1], in_=psv(pi)[:, 0:128], axis=AX.X, op=ALU.max),
                 reads=[bps[pi]], writes=[bf("Q2")])
            P.op("dve", lambda e, pi=pi: e.tensor_reduce(out=Q2[:, 1:2], in_=psv(pi)[:, 0:128], axis=AX.X, op=ALU.add),
                 reads=[bps[pi]], writes=[bf("Q2")])
            for qi, R in ((0, Rmax), (1, Rsum)):
                P.op("dve", lambda e, qi=qi: e.tensor_scalar(out=Dg[:], in0=ident[:], scalar1=Q2[:, qi:qi + 1], scalar2=None, op0=ALU.mult),
                     reads=[bf("ident"), bf("Q2")], writes=[bf("Dg")])
                pi = ps_next()
                P.op("pe", lambda e, pi=pi: e.matmul(psv(pi)[:, 0:128], lhsT=ones_f[:], rhs=Dg[:], start=True, stop=True),
                     reads=[bf("ones_f"), bf("Dg")], writes=[bps[pi]])
                P.op("dve", lambda e, pi=pi, R=R: e.tensor_copy(out=R[:].rearrange("p t g -> p (t g)"), in_=psv(pi)[:, 0:128]),
                     reads=[bps[pi]], writes=[bf("R")])
            bm = bf("mchain")

            def ch_reset(dr, pidx, slot, ds):
                return lambda: P.op("dve", lambda e: e.scalar_tensor_tensor(
                    out=mprev[:, dr, pidx, :], in0=mprev[:, dr, pidx, :], scalar=keep[:, slot:slot + 1],
                    in1=minit[:, l, slot, ds], op0=ALU.mult, op1=ALU.add),
                    reads=[bf("keep"), bf("minit"), bm], writes=[bm])

            def ch_max(dr, pidx, c, ds):
                return lambda: P.op("dve", lambda e: e.tensor_tensor(
                    out=Mlast[:, dr, c, :], in0=mprev[:, dr, pidx, :], in1=Rmax[:, c, ds], op=ALU.max),
                    reads=[bf("R"), bm], writes=[bm])

            def ch_add(dr, nidx, c, dr4):
                return lambda: P.op("dve", lambda e: e.tensor_tensor(
                    out=mprev[:, dr, nidx, :], in0=Mlast[:, dr, c, :], in1=Rsum[:, c, 8 + dr4:12 + dr4], op=ALU.add),
                    reads=[bf("R"), bm], writes=[bm])

            def ch_stage(dr, nidx, slot):
                return lambda: P.op("dve", lambda e: e.tensor_copy(
                    out=mstage[:, slot, dr * 4:dr * 4 + 4], in_=mprev[:, dr, nidx, :]), reads=[bm], writes=[bf("mstage")])

            for j in range(NT):
                for dr in range(2):
                    c = j if dr == 0 else NT - 1 - j
                    pidx = c if dr == 0 else c + 1
                    nidx = c + 1 if dr == 0 else c
                    ds = slice(dr * 4, dr * 4 + 4)
                    if j % 2 == 0:
                        chain_q.append(ch_reset(dr, pidx, j // 2, ds))
                    chain_q.append(ch_max(dr, pidx, c, ds))
                    chain_q.append(ch_add(dr, nidx, c, dr * 4))
                    if j % 2 == 1:
                        chain_q.append(ch_stage(dr, nidx, j // 2))
            wv, wb_ = w_in_unit(l, C_SU)
            proj_fm(wv, wb_, 512, lambda pi, n, h: copy_evac(su_fm[:, n, h * 512:(h + 1) * 512], psv(pi), [bps[pi]], [b_su]))
            wv, wb_ = w_in_unit(l, C_SV)

            def sv_evac(pi, t):
                P.op("act", lambda e: e.activation(out=ftile[0][:], in_=psv(pi), func=AF.Square, accum_out=ssq[:, t:t + 1]),
                     reads=[bps[pi]], writes=[bft[0], bf("ssq")])
                P.op("act", lambda e: e.activation(out=rsv[:, t:t + 1], in_=ssq[:, t:t + 1], func=AF.Ln, scale=1.0 / 512, bias=EPS),
                     reads=[bf("ssq")], writes=[bf("rsv")])
                P.op("act", lambda e: e.activation(out=rsv[:, t:t + 1], in_=rsv[:, t:t + 1], func=AF.Exp, scale=-0.5),
                     reads=[bf("rsv")], writes=[bf("rsv")])
                P.op("dve", lambda e: e.tensor_scalar(out=vn_tm[:, t, :], in0=psv(pi), scalar1=rsv[:, t:t + 1], scalar2=None, op0=ALU.mult),
                     reads=[bps[pi], bf("rsv")], writes=[b_vn])
            proj_tm(wv, wb_, 512, sv_evac)
            wv, wb_ = w_in_unit(l, C_Q)
            proj_fm(wv, wb_, 512, lambda pi, n, h: copy_evac(q_fm[:, n, h * 512:(h + 1) * 512], psv(pi), [bps[pi]], [b_q]))
            wv, wb_ = w_in_unit(l, C_K)
            proj_fm(wv, wb_, 512, lambda pi, n, h: copy_evac(kf_fm[:, n, h * 512:(h + 1) * 512], psv(pi), [bps[pi]], [b_kf], scale=ksc))
            for t in range(NT):
                pi = ps_next()
                for hd in range(4):
                    P.op("pe", lambda e, pi=pi, hd=hd, t=t: e.matmul(
                        psv(pi)[:, hd * 128:(hd + 1) * 128], lhsT=kf_fm[:, hd, t * 128:(t + 1) * 128], rhs=ident_bf[:],
                        start=True, stop=True), reads=[b_kf, bf("ident_bf")], writes=[bps[pi]])
                copy_evac(k_tm[:, t, :], psv(pi), [bps[pi]], [b_kt])
                drain(2)
            wv, wb_ = w_in_unit(l, C_V)
            proj_tm(wv, wb_, 512, lambda pi, t: copy_evac(v_tm[:, t, :], psv(pi), [bps[pi]], [b_v]))
            drain(len(chain_q))
            out_toks.append(P.dma("sp", m_out[l:l + 1].rearrange("a s g -> a (s g)"), mstage[0:1].rearrange("p s g -> p (s g)"),
                                  reads=[bf("mstage")]))
            P.op("dve", lambda e: e.tensor_tensor(out=decay[:, 0, :, :], in0=mprev[:, 0, 0:NT, :], in1=Mlast[:, 0, :, :], op=ALU.subtract),
                 reads=[bm], writes=[bf("decay")])
            P.op("dve", lambda e: e.tensor_tensor(out=decay[:, 1, :, :], in0=mprev[:, 1, 1:NT + 1, :], in1=Mlast[:, 1, :, :], op=ALU.subtract),
                 reads=[bm], writes=[bf("decay")])
            P.op("act", lambda e: e.activation(out=decay[:], in_=decay[:], func=AF.Exp), reads=[bf("decay")], writes=[bf("decay")])
            for dr in range(2):
                ds = slice(dr * 4, dr * 4 + 4)
                P.op("dve", lambda e, dr=dr, ds=ds: e.tensor_tensor(
                    out=wst[:, :, ds], in0=G_tm[:, :, ds], in1=Mlast[:, dr, :, :], op=ALU.subtract),
                    reads=[bf("G_tm"), bm], writes=[bf("wst")])
                P.op("dve", lambda e, dr=dr, ds=ds: e.scalar_tensor_tensor(
                    out=clampt[:, :, ds], in0=b_tm[:, :, ds], scalar=-1.0, in1=Mlast[:, dr, :, :],
                    op0=ALU.mult, op1=ALU.subtract), reads=[bf("b_tm"), bm], writes=[bf("clampt")])
            P.op("act", lambda e: e.activation(out=wst[:], in_=wst[:], func=AF.Exp), reads=[bf("wst")], writes=[bf("wst")])
            P.op("act", lambda e: e.activation(out=clampt[:], in_=clampt[:], func=AF.Exp), reads=[bf("clampt")], writes=[bf("clampt")])
            if l == 0:
                dump("wst", wst[:].rearrange("p t g -> p (t g)"), [128, 64], [bf("wst")])
                dump("clampt", clampt[:].rearrange("p t g -> p (t g)"), [128, 64], [bf("clampt")])
                dump("decay", decay[:].rearrange("p a c g -> p (a c g)"), [128, 64], [bf("decay")])
            for dr in range(2):
                P.op("dve", lambda e, dr=dr: e.tensor_tensor(
                    out=vw_ext[:, :, dr * 4:dr * 4 + 4, 0:128], in0=v_tm[:].rearrange("p t (h e) -> p t h e", h=4),
                    in1=wst[:, :, dr * 4:dr * 4 + 4].unsqueeze(3).to_broadcast([128, NT, 4, 128]), op=ALU.mult),
                    reads=[b_v, bf("wst")], writes=[bf("vw")])
            P.op("dve", lambda e: e.tensor_copy(out=vw_ext[:, :, :, 128], in_=wst[:]), reads=[bf("wst")], writes=[bf("vw")])

            def sgu_tile(t):
                pi = ps_next()
                for g in range(4):
                    P.op("pe", lambda e, g=g: e.matmul(
                        psv(pi)[:, g * 128:(g + 1) * 128], lhsT=vn_tm[:, t, g * 128:(g + 1) * 128], rhs=wsgu[:, g, :],
                        start=True, stop=True), reads=[b_vn, bf("wsgu")], writes=[bps[pi]])
                tf, btf = ftile[t % 2], bft[t % 2]
                for g in range(4):
                    P.op("dve", lambda e, g=g: e.scalar_tensor_tensor(
                        out=tf[:, g * 128:(g + 1) * 128], in0=psv(pi)[:, g * 128:(g + 1) * 128], scalar=gsgu[:, l, g:g + 1],
                        in1=bsgu_rep[:, g, :], op0=ALU.mult, op1=ALU.add),
                        reads=[bps[pi], bf("gsgu"), bf("bsgu_rep")], writes=[btf])
                P.op("dve", lambda e: e.tensor_tensor(
                    out=yb_fm[:, :, t * 128:(t + 1) * 128], in0=su_fm[:, :, t * 128:(t + 1) * 128],
                    in1=tf[:].rearrange("p (g q) -> p g q", g=4), op=ALU.mult),
                    reads=[btf, b_su], writes=[b_su])

            for t in range(NT):
                sgu_tile(t)
            if l == 0:
                dump("y_b", yb_fm, [128, 4, T], [b_su], BF16)
            ada_q = [(l, ng) for ng in range(4, 12)] + ([(l + 1, ng) for ng in range(4)] if l + 1 < depth else [])
            ada_g = ada_q[:4]
            ada_l = ada_q[4:]
            for g in range(4):
                for h in range(2):
                    mv, mb = load_w(poolm_in[g, :, h * 512:(h + 1) * 512].rearrange("(k p) n -> p k n", p=128), KC, 512)
                    pi = ps_next()
                    for sc in range(NT):
                        P.op("pe", lambda e, pi=pi, sc=sc, g=g, mv=mv: e.matmul(
                            psv(pi), lhsT=xp_tm[:, sc, g * 128:(g + 1) * 128], rhs=mv[:, sc, :],
                            start=(sc == 0), stop=(sc == NT - 1)), reads=[b_xp, mb], writes=[bps[pi]])
                    copy_evac(d_fm[:, g, h * 512:(h + 1) * 512], psv(pi), [bps[pi]], [b_d])
            for g in range(4):
                for h in range(2):
                    pi = ps_next()
                    P.op("pe", lambda e, pi=pi, g=g, h=h: e.matmul(
                        psv(pi), lhsT=wpool[:, g, :], rhs=d_fm[:, g, h * 512:(h + 1) * 512], start=True, stop=True),
                        reads=[bf("wpool"), b_d], writes=[bps[pi]])
                    copy_evac(y_a[:, g, h * 512:(h + 1) * 512], psv(pi), [bps[pi], bf("pscale")], [b_ya],
                              scale=pscale[:, l, g:g + 1])
            if l == 0:
                dump("y_a", y_a, [128, 4, T], [b_ya], BF16)


            st_banks = []
            for t in range(NT):
                pi = ps_next()
                st_banks.append(pi)
                for hd in range(4):
                    P.op("pe", lambda e, pi=pi, hd=hd, t=t: e.matmul(
                        psv(pi)[:, hd * 128:(hd + 1) * 128], lhsT=kf_fm[:, hd, t * 128:(t + 1) * 128], rhs=q_fm[:, hd, t * 128:(t + 1) * 128],
                        start=True, stop=True), reads=[b_kf, b_q], writes=[bps[pi]])
                P.op("dve", lambda e, pi=pi, t=t: e.tensor_tensor(
                    out=p0t[:, t, :].rearrange("p (h s) -> p h s", h=4), in0=psv(pi).rearrange("p (h s) -> p h s", h=4),
                    in1=masks[:, 0, :].unsqueeze(1).to_broadcast([128, 4, 128]), op=ALU.mult),
                    reads=[bps[pi], bf("masks")], writes=[b_p0t])
            for t in range(NT):
                pi = st_banks[t]
                P.op("dve", lambda e, pi=pi, t=t: e.tensor_tensor(
                    out=p0b[:, t, :].rearrange("p (h s) -> p h s", h=4), in0=psv(pi).rearrange("p (h s) -> p h s", h=4),
                    in1=masks[:, 1, :].unsqueeze(1).to_broadcast([128, 4, 128]), op=ALU.mult),
                    reads=[bps[pi], bf("masks")], writes=[b_p0b])
            pmat = [(p0t, b_p0t), (p0b, b_p0b)]

            bCs = [bf("Cst0"), bf("Cst1")]
            bCf = [bf("Cfin0"), bf("Cfin1")]
            bCd = [bf("Cd0"), bf("Cd1")]
            bhh = [[bf(f"h{t}_{hd}") for hd in range(4)] for t in range(NT)]
            for t in range(NT):
                for hd in range(4):
                    alias(bhh[t][hd], b_h)
            h_open = P.op("dve", lambda e: e.memset(dn[:, 0:1], 0.0), writes=[b_h, bf("dn0")])

            def cin_load(slot, dr):
                P.dma("sp", cin[dr][:], cinit_in[l, slot].rearrange("p (a b) -> p a b", a=8)[:, dr * 4:dr * 4 + 4, :],
                      writes=[bf(f"cin{dr}")])

            for dr in range(2):
                cin_load(0, dr)

            def prep(j, dr):
                c = j if dr == 0 else NT - 1 - j
                ds = slice(dr * 4, dr * 4 + 4)
                if j % 2 == 0:
                    slot = j // 2
                    ci, bci = cin[dr], bf(f"cin{dr}")
                    P.op("dve", lambda e: e.scalar_tensor_tensor(
                        out=Cst[:, ds, :], in0=Cfin[:, ds, :], scalar=keep[:, slot:slot + 1], in1=ci[:],
                        op0=ALU.mult, op1=ALU.add), reads=[bf("keep"), bci, bCf[dr]], writes=[bCs[dr]])
                    if slot + 1 < 4:
                        cin_load(slot + 1, dr)
                P.op("dve", lambda e: e.tensor_tensor(
                    out=Cd[:, ds, :], in0=Cst[:, ds, :], in1=decay[:, dr, c, :].unsqueeze(2).to_broadcast([128, 4, 129]), op=ALU.mult),
                    reads=[bCs[dr], bf("decay")], writes=[bCd[dr]])
                cb = Cd_bf[j % 2]
                P.op("act", lambda e: e.activation(out=cb[:, ds, 0:129], in_=Cd[:, ds, :], func=AF.Copy),
                     reads=[bCd[dr]], writes=[bf(f"Cdbf{j % 2}_{dr}")])

            for dr in range(2):
                prep(0, dr)
            ada_cols = psum[3][:, 960:964]
            ada_prev = [None]
            for j in range(NT):
                tiles = [j, NT - 1 - j]
                for dr in range(2):
                    t_ = tiles[dr]
                    upv = psum[2 + dr][:].rearrange("p (h c) -> p h c", h=4)
                    bup = [bps[4 + dr * 2], bps[5 + dr * 2]]
                    for hd in range(4):
                        P.op("pe", lambda e, hd=hd, t_=t_, dr=dr, upv=upv: e.matmul(
                            upv[:, hd, 0:129], lhsT=k_tm[:, t_, hd * 128:(hd + 1) * 128], rhs=vw_ext[:, t_, dr * 4 + hd, 0:129],
                            start=True, stop=True), reads=[b_kt, bf("vw")], writes=[bup[hd // 2]])
                for dr in range(2):
                    t_ = tiles[dr]
                    ndv = psum[dr][:].rearrange("p (h c) -> p h c", h=4)
                    bnd = [bps[dr * 2], bps[dr * 2 + 1]]
                    pm, bpm = pmat[dr]
                    cb = Cd_bf[j % 2]
                    for hd in range(4):
                        P.op("pe", lambda e, hd=hd, t_=t_, dr=dr, ndv=ndv, pm=pm: e.matmul(
                            ndv[:, hd, 0:129], lhsT=pm[:, t_, hd * 128:(hd + 1) * 128], rhs=vw_ext[:, t_, dr * 4 + hd, 0:129],
                            start=True, stop=False), reads=[bpm, bf("vw")], writes=[bnd[hd // 2]])
                        P.op("pe", lambda e, hd=hd, t_=t_, dr=dr, ndv=ndv, cb=cb: e.matmul(
                            ndv[:, hd, 0:129], lhsT=q_fm[:, hd, t_ * 128:(t_ + 1) * 128], rhs=cb[:, dr * 4 + hd, 0:129],
                            start=False, stop=True), reads=[b_q, bf(f"Cdbf{j % 2}_{dr}")], writes=[bnd[hd // 2]])
                for dr in range(2):
                    ds = slice(dr * 4, dr * 4 + 4)
                    upv = psum[2 + dr][:].rearrange("p (h c) -> p h c", h=4)
                    bup = [bps[4 + dr * 2], bps[5 + dr * 2]]
                    last = (j % 2 == 1)
                    dst, bdst = (Cfin, bCf[dr]) if last else (Cst, bCs[dr])
                    P.op("dve", lambda e, ds=ds, upv=upv, dst=dst: e.tensor_tensor(
                        out=dst[:, ds, :], in0=Cd[:, ds, :], in1=upv[:, :, 0:129], op=ALU.add),
                        reads=bup + [bCd[dr]], writes=[bdst])
                    if last:
                        slot = j // 2
                        out_toks.append(P.dma("sp", c_out[l, slot, dr].rearrange("p (h c) -> p h c", h=4), Cfin[:, ds, :], reads=[bdst]))
                if ada_l or ada_prev[0] is not None:
                    if ada_prev[0] is not None:
                        ada_evac(ada_prev[0][0], ada_prev[0][1], ada_cols, [bps[7]])
                        ada_prev[0] = None
                    if ada_l:
                        al, ang = ada_l.pop(0)
                        ada_mm(al, ang, ada_cols, [bps[7]])
                        ada_prev[0] = (al, ang)
                if j + 1 < NT:
                    for dr in range(2):
                        prep(j + 1, dr)
                for dr in range(2):
                    t_ = tiles[dr]
                    ds = slice(dr * 4, dr * 4 + 4)
                    ndv = psum[dr][:].rearrange("p (h c) -> p h c", h=4)
                    bnd = [bps[dr * 2], bps[dr * 2 + 1]]
                    bdn, brd = bf(f"dn{dr}"), bf(f"rd{dr}")
                    P.op("dve", lambda e, ds=ds, t_=t_, ndv=ndv: e.tensor_tensor(
                        out=dn[:, ds], in0=ndv[:, :, 128], in1=clampt[:, t_, ds], op=ALU.max),
                        reads=bnd + [bf("clampt")], writes=[bdn])
                    P.op("dve", lambda e, ds=ds, ndv=ndv: e.scalar_tensor_tensor(
                        out=dn[:, ds], in0=ndv[:, :, 128], scalar=-1.0, in1=dn[:, ds], op0=ALU.mult, op1=ALU.max),
                        reads=bnd + [bdn], writes=[bdn])
                    P.op("dve", lambda e, ds=ds: e.reciprocal(out=rd[:, ds], in_=dn[:, ds]), reads=[bdn], writes=[brd])
                    if j < NT // 2:
                        P.op("dve", lambda e, ds=ds, t_=t_, ndv=ndv: e.tensor_tensor(
                            out=h_tm[:, t_, :, :], in0=ndv[:, :, 0:128],
                            in1=rd[:, ds].unsqueeze(2).to_broadcast([128, 4, 128]), op=ALU.mult),
                            reads=bnd + [brd], writes=bhh[t_], extra=[h_open])
                    else:
                        for hd in range(4):
                            P.op("dve", lambda e, hd=hd, dr=dr, t_=t_, ndv=ndv: e.scalar_tensor_tensor(
                                out=h_tm[:, t_, hd, :], in0=ndv[:, hd, 0:128], scalar=rd[:, dr * 4 + hd:dr * 4 + hd + 1],
                                in1=h_tm[:, t_, hd, :], op0=ALU.mult, op1=ALU.add),
                                reads=[bnd[hd // 2], brd, bhh[t_][hd]], writes=[bhh[t_][hd]])
            if ada_prev[0] is not None:
                ada_evac(ada_prev[0][0], ada_prev[0][1], ada_cols, [bps[7]])
            P.op("dve", lambda e: e.memset(dn[:, 0:1], 0.0), reads=[bx_ for a_ in bhh for bx_ in a_], writes=[b_h, bf("dn0")])
            if l == 0:
                dump("h", h_tm.rearrange("p t h e -> p (t h e)"), [128, NT * 512], [b_h])

            wv, wb_ = w_in_unit(l, C_O)
            proj_tm(wv, wb_, 512, lambda pi, t: P.op("act", lambda e: e.activation(out=o_tm[:, t, :], in_=psv(pi), func=AF.Sigmoid),
                                                     reads=[bps[pi]], writes=[b_o]))

            junk = ftile[0]
            bjunk = [bf(f"junk{hd}") for hd in range(4)]
            for hd in range(4):
                alias(bjunk[hd], bft[0])
            pi_ada = ps_next()
            ps_reserved.add(pi_ada)
            for t in range(NT):
                hsq, bhsq = ftile[1 + t % 2], bft[1 + t % 2]
                hss, bhss = hss2[:, t % 2, :], bf(f"hss{t % 2}")
                yct, byct = tmp_b[t % 2], bf(f"tmpb{t % 2}")
                for hd in range(4):
                    P.op("act", lambda e, t=t, hd=hd, hss=hss: e.activation(
                        out=junk[:, hd * 128:(hd + 1) * 128], in_=h_tm[:, t, hd, :], func=AF.Square, accum_out=hss[:, hd:hd + 1]),
                        reads=[b_h], writes=[bjunk[hd], bf(f"hss{t % 2}_{hd}")], extra=[bhss.last_write] + list(bhss.reads))
                P.op("act", lambda e, hss=hss: e.activation(out=hss, in_=hss, func=AF.Ln, scale=1.0 / 128, bias=EPS),
                     reads=[bf(f"hss{t % 2}_{hd}") for hd in range(4)], writes=[bhss])
                P.op("act", lambda e, hss=hss: e.activation(out=hss, in_=hss, func=AF.Exp, scale=-0.5), reads=[bhss], writes=[bhss])
                P.op("dve", lambda e, t=t, hsq=hsq, hss=hss: e.tensor_tensor(
                    out=hsq[:].rearrange("p (h e) -> p h e", h=4), in0=h_tm[:, t, :, :],
                    in1=hss.unsqueeze(2).to_broadcast([128, 4, 128]), op=ALU.mult),
                    reads=[b_h, bhss], writes=[bhsq])
                P.op("dve", lambda e, hsq=hsq: e.tensor_tensor(out=hsq[:], in0=hsq[:], in1=gml_rep[:], op=ALU.mult),
                     reads=[bhsq, bf("gml_rep")], writes=[bhsq])
                P.op("dve", lambda e, t=t, hsq=hsq, yct=yct: e.tensor_tensor(out=yct[:], in0=hsq[:], in1=o_tm[:, t, :], op=ALU.mult),
                     reads=[bhsq, b_o], writes=[byct])
                pi = ps_next()
                for hd in range(4):
                    P.op("pe", lambda e, pi=pi, hd=hd, yct=yct: e.matmul(
                        psv(pi)[:, hd * 128:(hd + 1) * 128], lhsT=yct[:, hd * 128:(hd + 1) * 128], rhs=ident_bf[:], start=True, stop=True),
                        reads=[byct, bf("ident_bf")], writes=[bps[pi]])
                P.op("dve", lambda e, pi=pi, t=t: e.tensor_copy(
                    out=yc_fm[:, :, t * 128:(t + 1) * 128], in_=psv(pi).rearrange("p (h s) -> p h s", h=4)),
                    reads=[bps[pi]], writes=[b_q])
                if t % 2 == 1 and t // 2 < len(ada_g):
                    u = t // 2
                    ada_mm(ada_g[u][0], ada_g[u][1], psv(pi_ada)[:, 4 * u:4 * u + 4], [bps[pi_ada]])
            for u, (al, ang) in enumerate(ada_g):
                ada_evac(al, ang, psv(pi_ada)[:, 4 * u:4 * u + 4], [bps[pi_ada]])
            ps_reserved.discard(pi_ada)
            if l == 0:
                dump("y_c", yc_fm, [128, 4, T], [b_q], BF16)

            ybr = [(y_a, b_ya), (yb_fm, b_su), (yc_fm, b_q)]
            for cg in range(2):
                for r in range(3):
                    wbv, wbb = w_in_unit(l, C_BR + r * D + cg * 512)
                    wrv, wrb = load_w(w_branch[l, r, :, cg * 512:(cg + 1) * 512].rearrange("(k p) n -> p k n", p=128), 4, 512)
                    yv, yb_ = ybr[r]
                    for n4 in range(4):
                        for h in range(2):
                            hs = slice(h * 512, (h + 1) * 512)
                            pg = ps_next()
                            for kc in range(KC):
                                P.op("pe", lambda e, pg=pg, kc=kc, n4=n4, hs=hs, wbv=wbv: e.matmul(
                                    psv(pg), lhsT=wbv[:, kc, n4 * 128:(n4 + 1) * 128], rhs=u_fm[:, kc, hs],
                                    start=(kc == 0), stop=(kc == KC - 1)), reads=[wbb, bu[h]], writes=[bps[pg]])
                            pb = ps_next()
                            for kc in range(4):
                                P.op("pe", lambda e, pb=pb, kc=kc, n4=n4, hs=hs, wrv=wrv, yv=yv: e.matmul(
                                    psv(pb), lhsT=wrv[:, kc, n4 * 128:(n4 + 1) * 128], rhs=yv[:, kc, hs],
                                    start=(kc == 0), stop=(kc == 3)), reads=[wrb, yb_], writes=[bps[pb]])
                            sg, bsg = ftile[(n4 * 2 + h) % 3], bft[(n4 * 2 + h) % 3]
                            P.op("act", lambda e, pg=pg, sg=sg: e.activation(out=sg[:], in_=psv(pg), func=AF.Sigmoid),
                                 reads=[bps[pg]], writes=[bsg])
                            ba = bacc[n4][h]
                            if r == 0:
                                P.op("dve", lambda e, pb=pb, sg=sg, n4=n4, hs=hs: e.tensor_tensor(
                                    out=acc_fm[:, n4, hs], in0=psv(pb), in1=sg[:], op=ALU.mult),
                                    reads=[bps[pb], bsg], writes=[ba])
                            else:
                                P.op("dve", lambda e, pb=pb, sg=sg: e.tensor_tensor(out=sg[:], in0=psv(pb), in1=sg[:], op=ALU.mult),
                                     reads=[bps[pb], bsg], writes=[bsg])
                                if r == 1:
                                    P.op("dve", lambda e, sg=sg, n4=n4, hs=hs: e.tensor_tensor(
                                        out=acc_fm[:, n4, hs], in0=acc_fm[:, n4, hs], in1=sg[:], op=ALU.add),
                                        reads=[bsg, ba], writes=[ba])
                                else:
                                    P.op("dve", lambda e, sg=sg, n4=n4, hs=hs, cg=cg: e.tensor_tensor(
                                        out=merged[:, cg * 4 + n4, hs], in0=acc_fm[:, n4, hs], in1=sg[:], op=ALU.add),
                                        reads=[bsg, ba], writes=[b_mrg])
            if l == 0:
                dump("merged", merged, [128, 8, T], [b_mrg], BF16)

            for cg in range(2):
                wv, wb_ = load_w(w_out[l, :, cg * 512:(cg + 1) * 512].rearrange("(k p) n -> p k n", p=128), KC, 512)
                for n4 in range(4):
                    n = cg * 4 + n4
                    for h in range(2):
                        hs = slice(h * 512, (h + 1) * 512)
                        pi = ps_next()
                        for kc in range(KC):
                            P.op("pe", lambda e, pi=pi, kc=kc, n4=n4, hs=hs, wv=wv: e.matmul(
                                psv(pi), lhsT=wv[:, kc, n4 * 128:(n4 + 1) * 128], rhs=merged[:, kc, hs],
                                start=(kc == 0), stop=(kc == KC - 1)), reads=[wb_, b_mrg], writes=[bps[pi]])
                        P.op("dve", lambda e, pi=pi, n=n, hs=hs: e.scalar_tensor_tensor(
                            out=x_fm[:, n, hs], in0=psv(pi), scalar=modfm[:, l, 16 + n:17 + n], in1=x_fm[:, n, hs],
                            op0=ALU.mult, op1=ALU.add), reads=[bps[pi], bmod[l][1], bx[n][h]], writes=[bx[n][h]])
            if l == 0:
                dump("xmid", x_fm[:].rearrange("p c t -> p (c t)"), [128, KC * T], [b for bb in bx for b in bb])

            rmsnorm_to_u(l, 1)
            for kg in range(4):
                hb, bhb = hff[kg % 2], bhff[kg % 2]
                for cg in range(2):
                    wv, wb_ = load_w(w_ff1[l, :, kg * 1024 + cg * 512: kg * 1024 + (cg + 1) * 512].rearrange("(k p) n -> p k n", p=128), KC, 512)
                    for n4 in range(4):
                        for h in range(2):
                            hs = slice(h * 512, (h + 1) * 512)
                            pi = ps_next()
                            for kc in range(KC):
                                P.op("pe", lambda e, pi=pi, kc=kc, n4=n4, hs=hs, wv=wv: e.matmul(
                                    psv(pi), lhsT=wv[:, kc, n4 * 128:(n4 + 1) * 128], rhs=u_fm[:, kc, hs],
                                    start=(kc == 0), stop=(kc == KC - 1)), reads=[wb_, bu[h]], writes=[bps[pi]])
                            tb, btb = tmp_b[(n4 * 2 + h) % 2], bf(f"tmpb{(n4 * 2 + h) % 2}")
                            P.op("act", lambda e, pi=pi, tb=tb: e.activation(out=tb[:], in_=psv(pi), func=AF.Relu),
                                 reads=[bps[pi]], writes=[btb])
                            P.op("dve", lambda e, tb=tb, hb=hb, cg=cg, n4=n4, hs=hs: e.tensor_tensor(
                                out=hb[:, cg * 4 + n4, hs], in0=tb[:], in1=tb[:], op=ALU.mult), reads=[btb], writes=[bhb])
                for cg in range(2):
                    wv, wb_ = load_w(w_ff2[l, kg * 1024:(kg + 1) * 1024, cg * 512:(cg + 1) * 512].rearrange("(k p) n -> p k n", p=128), KC, 512)
                    for n4 in range(4):
                        n = cg * 4 + n4
                        for h in range(2):
                            hs = slice(h * 512, (h + 1) * 512)
                            pi = ps_next()
                            for kc in range(KC):
                                P.op("pe", lambda e, pi=pi, kc=kc, n4=n4, hs=hs, wv=wv, hb=hb: e.matmul(
                                    psv(pi), lhsT=wv[:, kc, n4 * 128:(n4 + 1) * 128], rhs=hb[:, kc, hs],
                                    start=(kc == 0), stop=(kc == KC - 1)), reads=[wb_, bhb], writes=[bps[pi]])
                            P.op("dve", lambda e, pi=pi, n=n, hs=hs: e.scalar_tensor_tensor(
                                out=x_fm[:, n, hs], in0=psv(pi), scalar=modfm[:, l, 40 + n:41 + n], in1=x_fm[:, n, hs],
                                op0=ALU.mult, op1=ALU.add), reads=[bps[pi], bmod[l][1], bx[n][h]], writes=[bx[n][h]])
            dump(f"x{l}", x_fm[:].rearrange("p c t -> p (c t)"), [128, KC * T], [b for bb in bx for b in bb])

        for l in range(depth):
            layer(l)

        for h in range(2):
            hs = slice(h * 512, (h + 1) * 512)
            ms_rstd(h)
            for c in range(KC):
                P.op("dve", lambda e, c=c, hs=hs: e.scalar_tensor_tensor(
                    out=x_fm[:, c, hs], in0=x_fm[:, c, hs], scalar=gfin[:, c:c + 1], in1=ftile[2][:], op0=ALU.mult, op1=ALU.mult),
                    reads=[bx[c][h], bft[2], bf("gfin")], writes=[bx[c][h]])
        for t in range(NT):
            ot, bo = io_tm[t % 2], bio[t % 2]
            h = t // 4
            for cg in range(2):
                pi = ps_next()
                for c4 in range(4):
                    c = cg * 4 + c4
                    P.op("pe", lambda e, pi=pi, c4=c4, c=c, t=t: e.matmul(
                        psv(pi)[:, c4 * 128:(c4 + 1) * 128], lhsT=x_fm[:, c, t * 128:(t + 1) * 128], rhs=ident[:], start=True, stop=True),
                        reads=[bx[c][h], bf("ident")], writes=[bps[pi]])
                copy_evac(ot[:, cg * 512:(cg + 1) * 512], psv(pi), [bps[pi]], [bo])
            out_toks.append(P.dma("sp", y_out[t * 128:(t + 1) * 128, :], ot, reads=[bo]))

        P.wait_all("sp", out_toks + dbg_toks)
        P.emit(block)
        n_inst = P.n_inst
    return nc, n_inst


def _centred_weights(n, w):
    t = np.arange(n)
    lo = np.clip(t - w // 2, 0, n)
    hi = np.clip(t + (w - w // 2), 0, n)
    m = np.zeros((n, n), np.float64)
    for i in range(n):
        m[lo[i]:hi[i], i] = 1.0 / (hi[i] - lo[i])
    return m


def _pool_mats(grid):
    out = np.zeros((4, T, T), np.float32)
    for g, w in enumerate((2, 4, 8, 16)):
        if grid:
            mc = _centred_weights(64, w)
            mr = _centred_weights(16, w)
            m = np.kron(mr, mc)
        else:
            m1 = _centred_weights(256, w)
            m = np.zeros((T, T))
            for k in range(4):
                m[k * 256:(k + 1) * 256, k * 256:(k + 1) * 256] = m1
        out[g] = (m - np.eye(T)).astype(np.float32)
    return out.astype(ml_dtypes.bfloat16)


_CONST = {}


def _consts():
    if not _CONST:
        s = np.arange(128)
        mf = (s[:, None] <= s[None, :]).astype(np.float32)
        mb = (s[:, None] >= s[None, :]).astype(np.float32)
        _CONST["masks"] = np.ascontiguousarray(np.stack([mf, mb], axis=1).reshape(128, 256))
        _CONST["ident"] = np.eye(128, dtype=np.float32)
        _CONST["pool_grid"] = _pool_mats(True)
        _CONST["pool_seq"] = _pool_mats(False)
    return _CONST


_PROG = {}


def kernel(x_prompt, x_sample, state_C, state_n, state_m, c, c_ctx, w_ada, b_ada, g_norm1, g_norm2, w_in,
           b_gates, w_pool, pool_scale, g_sgu, w_sgu, b_sgu, g_mlstm, w_branch, w_out, w_ff1, w_ff2, g_final,
           _depth=DEPTH, _dbg=()):
    f = np.float32
    A = lambda a: np.ascontiguousarray(np.asarray(a, dtype=f))
    x_prompt, x_sample = A(x_prompt), A(x_sample)
    state_C, state_n, state_m = A(state_C), A(state_n), A(state_m)
    cst = _consts()

    def fm(vec, nchunk):
        return np.ascontiguousarray(np.asarray(vec, f).reshape(nchunk, 128).T)

    shared = {
        "masks": cst["masks"], "ident": cst["ident"],
        "w_ada": A(w_ada),
        "bada": np.ascontiguousarray(A(b_ada).reshape(DEPTH, 48, 128).transpose(2, 0, 1).reshape(128, DEPTH * 48)),
        "gn": np.ascontiguousarray(np.stack([A(g_norm1).reshape(DEPTH, KC, 128), A(g_norm2).reshape(DEPTH, KC, 128)], axis=1)
                                   .transpose(3, 0, 1, 2).reshape(128, DEPTH * 2 * KC)),
        "w_in": A(w_in), "b_gates": A(b_gates), "w_pool": A(w_pool),
        "pscale": np.ascontiguousarray(A(pool_scale).reshape(DEPTH, 4, 128).transpose(2, 0, 1).reshape(128, DEPTH * 4)),
        "gsgu": np.ascontiguousarray(A(g_sgu).reshape(DEPTH, 4, 128).transpose(2, 0, 1).reshape(128, DEPTH * 4)),
        "w_sguT": np.ascontiguousarray(A(w_sgu).transpose(0, 3, 1, 2).reshape(DEPTH, 128, 512)),
        "b_sgu": A(b_sgu).reshape(DEPTH, 512), "g_mlstm": A(g_mlstm),
        "w_branch": A(w_branch), "w_out": A(w_out), "w_ff1": A(w_ff1), "w_ff2": A(w_ff2),
        "gfin": fm(g_final, KC),
    }
    zeros_c = np.zeros((DEPTH, 4, 128, 8 * 129), f)
    zeros_m = np.zeros((128, DEPTH * 4 * 8), f)
    in_maps = []
    for core in range(8):
        m = dict(shared)
        if core < 2:
            b = core
            m["x"] = x_sample[b]
            m["cond"] = fm(A(c)[b], KC)
            keep = np.zeros((128, 4), f)
            keep[:, 1:] = 1.0
            ci = np.zeros((DEPTH, 4, 128, 8, 129), f)
            ci[:, 0, :, :, 0:128] = state_C[b].transpose(0, 3, 1, 2, 4).reshape(DEPTH, 128, 8, 128)
            ci[:, 0, :, :, 128] = state_n[b].transpose(0, 3, 1, 2).reshape(DEPTH, 128, 8)
            mi = np.zeros((DEPTH, 4, 8), f)
            mi[:, 0, :] = state_m[b].reshape(DEPTH, 8)
            m["keep"] = keep
            m["cinit"] = ci.reshape(DEPTH, 4, 128, 8 * 129)
            m["minit"] = np.ascontiguousarray(np.broadcast_to(mi.reshape(1, -1), (128, DEPTH * 32)))
            m["poolm"] = cst["pool_grid"]
        else:
            g = min(core, 5) - 2
            m["x"] = np.ascontiguousarray(x_prompt[4 * g:4 * g + 4].reshape(T, D))
            m["cond"] = fm(c_ctx, KC)
            m["keep"] = np.zeros((128, 4), f)
            m["cinit"] = zeros_c
            m["minit"] = zeros_m
            m["poolm"] = cst["pool_seq"]
        in_maps.append(m)

    key = (_depth, tuple(_dbg))
    if key not in _PROG:
        _PROG[key] = build_program(_depth, _dbg)
    nc, n_inst = _PROG[key]
    res = run_bass_kernel_spmd(nc, in_maps, core_ids=list(range(8)))
    R = res.results

    y_sample = np.stack([R[0]["y"], R[1]["y"]], axis=0).astype(f)
    y_prompt = np.concatenate([R[cidx]["y"].reshape(4, 256, D) for cidx in range(2, 6)], axis=0).astype(f)
    B = x_prompt.shape[0]
    new_C = np.zeros((B, DEPTH, 2, 4, 128, 128), f)
    new_n = np.zeros((B, DEPTH, 2, 4, 128), f)
    new_m = np.zeros((B, DEPTH, 2, 4), f)
    for cidx in range(2, 6):
        co = R[cidx]["c_out"].reshape(DEPTH, 4, 2, 128, 4, 129)
        mo = R[cidx]["m_out"].reshape(DEPTH, 4, 2, 4)
        for slot in range(4):
            for dr in range(2):
                s = 4 * (cidx - 2) + (slot if dr == 0 else 3 - slot)
                new_C[s, :, dr] = co[:, slot, dr, :, :, 0:128].transpose(0, 2, 1, 3)
                new_n[s, :, dr] = co[:, slot, dr, :, :, 128].transpose(0, 2, 1)
                new_m[s, :, dr] = mo[:, slot, dr]
    if _dbg:
        kernel.dbg = [{k: v for k, v in r.items() if k.startswith("dbg_")} for r in R]
    return (y_prompt, y_sample, new_C, new_n, new_m)
```

```python
import os
from contextlib import ExitStack

import numpy as np
import ml_dtypes
import concourse.bass as bass
import concourse.mybir as mybir
from concourse.bass_utils import run_bass_kernel_spmd

F32 = mybir.dt.float32
BF16 = mybir.dt.bfloat16
ALU = mybir.AluOpType
AF = mybir.ActivationFunctionType
AX = mybir.AxisListType

D = 1024
T = 1024
DEPTH = 4
DIN = 6672
DFF = 4096
NT = 8
KC = 8
EPS = 1e-6
SEM_CAP = 24000
NSLOT = 4
C_XP, C_SU, C_SV, C_Q, C_K, C_V, C_O, C_G, C_BR = 0, 512, 1024, 1536, 2048, 2560, 3072, 3584, 3600


class Buf:
    __slots__ = ("name", "last_write", "reads", "aliases")

    def __init__(self, name):
        self.name = name
        self.last_write = None
        self.reads = []
        self.aliases = []


def alias(*bufs):
    for a in bufs:
        for b in bufs:
            if a is not b and b not in a.aliases:
                a.aliases.append(b)


class Prog:
    ENGS = ("pe", "act", "dve", "pool", "sp")

    def __init__(self, nc, ctx):
        self.nc = nc
        self.ctx = ctx
        self.streams = {e: [] for e in self.ENGS}
        self.cur_sem = {}
        self.cur_cnt = {}
        self.epoch = {}
        for e in self.ENGS:
            self._new_epoch(e, first=True)
        self.waited = {e: {} for e in self.ENGS}
        self.n_inst = 0
        self._dma_sems = {}
        self._dma_rr = {}

    def _new_sem(self, name):
        return self.ctx.enter_context(self.nc.semaphore(name))

    def _new_epoch(self, e, first=False):
        self.epoch[e] = 0 if first else self.epoch[e] + 1
        self.cur_sem[e] = self._new_sem(f"s_{e}_{self.epoch[e]}")
        self.cur_cnt[e] = 0

    def _emit_waits(self, e, deps):
        need = {}
        w = self.waited[e]
        for t in deps:
            if t is None:
                continue
            sem, val, teng, tep = t
            key = id(sem)
            if w.get(key, 0) >= val:
                continue
            if teng is not None and w.get(("ep", teng), -1) > tep:
                continue
            if key not in need or need[key][1] < val:
                need[key] = (sem, val, teng, tep)
        for key, (sem, val, teng, tep) in need.items():
            self.streams[e].append(("wait", sem, val))
            w[key] = val
            if teng is not None:
                w[("ep", teng)] = max(w.get(("ep", teng), -1), tep)

    def _deps(self, reads, writes, extra):
        deps = list(extra)
        for b in reads:
            deps.append(b.last_write)
        for b in writes:
            deps.append(b.last_write)
            deps.extend(b.reads)
            for a in b.aliases:
                deps.append(a.last_write)
                deps.extend(a.reads)
        return deps

    def _commit(self, tok, reads, writes):
        for b in reads:
            b.reads.append(tok)
        for b in writes:
            b.last_write = tok
            b.reads = []

    def op(self, e, fn, reads=(), writes=(), extra=()):
        deps = self._deps(reads, writes, extra)
        if e == "pe":
            deps = [t for t in deps if t is not None and t[2] != "pe"]
        self._emit_waits(e, deps)
        if self.cur_cnt[e] >= SEM_CAP:
            self._new_epoch(e)
        self.cur_cnt[e] += 1
        tok = (self.cur_sem[e], self.cur_cnt[e], e, self.epoch[e])
        self.streams[e].append(("op", fn, self.cur_sem[e]))
        self.n_inst += 1
        self._commit(tok, reads, writes)
        return tok

    def _get_dma_sem(self, q):
        pool = self._dma_sems.setdefault(q, [])
        rr = self._dma_rr.setdefault(q, 0)
        if len(pool) < 20:
            s = [self._new_sem(f"dma_{q}{len(pool)}"), 0]
            pool.append(s)
            return s
        s = pool[rr % len(pool)]
        self._dma_rr[q] = rr + 1
        key = id(s[0])
        if self.waited[q].get(key, 0) < s[1]:
            self.streams[q].append(("wait", s[0], s[1]))
            self.waited[q][key] = s[1]
        return s

    def dma(self, q, out, in_, reads=(), writes=(), extra=()):
        deps = self._deps(reads, writes, extra)
        self._emit_waits(q, deps)
        s = self._get_dma_sem(q)
        s[1] += 16
        tok = (s[0], s[1], None, 0)
        self.streams[q].append(("dma", out, in_, s[0]))
        self.n_inst += 1
        self._commit(tok, reads, writes)
        return tok

    def wait_all(self, e, toks):
        self._emit_waits(e, toks)

    def emit(self, block):
        streams = self.streams

        def run(eng, lst):
            for it in lst:
                if it[0] == "wait":
                    eng.wait_ge(it[1], it[2])
                elif it[0] == "op":
                    it[1](eng).then_inc(it[2], 1)
                else:
                    eng.dma_start(out=it[1], in_=it[2]).then_inc(it[3], 16)

        @block.tensor
        def _(eng):
            run(eng, streams["pe"])

        @block.scalar
        def _(eng):
            run(eng, streams["act"])

        @block.vector
        def _(eng):
            run(eng, streams["dve"])

        @block.gpsimd
        def _(eng):
            run(eng, streams["pool"])

        @block.sync
        def _(eng):
            run(eng, streams["sp"])


def build_program(depth=DEPTH, dbg=()):
    nc = bass.Bass("TRN2", target_bir_lowering=False)

    def din(name, shape, dt=F32):
        return nc.dram_tensor(name, list(shape), dt, kind="ExternalInput").ap()

    def dout(name, shape, dt=F32):
        return nc.dram_tensor(name, list(shape), dt, kind="ExternalOutput").ap()

    x_in = din("x", [T, D])
    cond_in = din("cond", [128, KC])
    keep_in = din("keep", [128, 4])
    minit_in = din("minit", [128, DEPTH * 4 * 8])
    cinit_in = din("cinit", [DEPTH, 4, 128, 8 * 129])
    poolm_in = din("poolm", [4, T, T], BF16)
    masks_in = din("masks", [128, 2 * 128])
    ident_in = din("ident", [128, 128])
    w_ada = din("w_ada", [DEPTH, D, 6 * D])
    bada_in = din("bada", [128, DEPTH * 48])
    gn_in = din("gn", [128, DEPTH * 2 * KC])
    w_in = din("w_in", [DEPTH, D, DIN])
    b_gates = din("b_gates", [DEPTH, 16])
    w_pool = din("w_pool", [DEPTH, 4, 128, 128])
    pscale_in = din("pscale", [128, DEPTH * 4])
    gsgu_in = din("gsgu", [128, DEPTH * 4])
    w_sguT = din("w_sguT", [DEPTH, 128, 4 * 128])
    b_sgu = din("b_sgu", [DEPTH, 4 * 128])
    g_mlstm = din("g_mlstm", [DEPTH, 512])
    w_branch = din("w_branch", [DEPTH, 3, 512, D])
    w_out = din("w_out", [DEPTH, D, D])
    w_ff1 = din("w_ff1", [DEPTH, D, DFF])
    w_ff2 = din("w_ff2", [DEPTH, DFF, D])
    gfin_in = din("gfin", [128, KC])

    y_out = dout("y", [T, D])
    c_out = dout("c_out", [DEPTH, 4, 2, 128, 4 * 129])
    m_out = dout("m_out", [DEPTH, 4, 8])
    dbg_toks = []

    with ExitStack() as ctx:
        P = Prog(nc, ctx)

        def sb(name, shape, dt=F32):
            return ctx.enter_context(nc.sbuf_tensor("s_" + name, list(shape), dt))

        x_fm = sb("x_fm", [128, KC, T])
        u_fm = sb("u_fm", [128, KC, T], BF16)
        wslot = [sb(f"wslot{i}", [128, 4096], BF16) for i in range(NSLOT)]
        arena = sb("arena", [128, 6 * 4096], BF16)
        v_tm = sb("v_tm", [128, NT, 512], BF16)
        vw_ext = sb("vw_ext", [128, NT, 8, 130], BF16)
        p0t = sb("p0t", [128, NT, 512], BF16)
        ident = sb("ident", [128, 128])
        ident_bf = sb("ident_bf", [128, 128], BF16)
        masks = sb("masks", [128, 2, 128])
        ones_mean_bf = sb("ones_mean_bf", [128, 128], BF16)
        ones_f = sb("ones_f", [128, 128])
        cond_sb = sb("cond_sb", [128, KC])
        scond = sb("scond", [128, KC], BF16)
        keep = sb("keep", [128, 4])
        minit = sb("minit", [128, DEPTH, 4, 8])
        gn = sb("gn", [128, DEPTH, 2, KC])
        pscale = sb("pscale", [128, DEPTH, 4])
        gsgu = sb("gsgu", [128, DEPTH, 4])
        gfin = sb("gfin", [128, KC])
        bada = sb("bada", [128, DEPTH, 48])
        modfm = sb("modfm", [128, DEPTH, 48])
        AB = sb("AB", [128, DEPTH, 2, KC])
        wg = sb("wg", [128, KC, 16], BF16)
        bg_rep = sb("bg_rep", [128, 16])
        wpool = sb("wpool", [128, 4, 128], BF16)
        wsgu = sb("wsgu", [128, 4, 128], BF16)
        bsgu_rep = sb("bsgu_rep", [128, 4, 128])
        gml_rep = sb("gml_rep", [128, 512])
        ftile = [sb(f"ftile{i}", [128, 512]) for i in range(3)]
        modrow = ftile[0]
        tmp_b = [sb(f"tmpb{i}", [128, 512], BF16) for i in range(2)]
        ssq = sb("ssq", [128, NT])
        rsv = sb("rsv", [128, NT])
        gp = sb("gp", [128, NT, 16])
        gt0 = sb("gt0", [128, NT, 8])
        G_tm = sb("G_tm", [128, NT, 16])
        b_tm = sb("b_tm", [128, NT, 8])
        Q2 = sb("Q2", [128, 2])
        Dg = sb("Dg", [128, 128])
        Rmax = sb("Rmax", [128, NT, 16])
        Rsum = sb("Rsum", [128, NT, 16])
        mprev = sb("mprev", [128, 2, 9, 4])
        Mlast = sb("Mlast", [128, 2, 8, 4])
        decay = sb("decay", [128, 2, 8, 4])
        wst = sb("wst", [128, NT, 8])
        clampt = sb("clampt", [128, NT, 8])
        Cst = sb("Cst", [128, 8, 129])
        Cd = sb("Cd", [128, 8, 129])
        Cd_bf = [sb(f"Cd_bf{i}", [128, 8, 130], BF16) for i in range(2)]
        Cfin = sb("Cfin", [128, 8, 129])
        mstage = sb("mstage", [128, 4, 8])
        cin = [sb(f"cin{i}", [128, 4, 129]) for i in range(2)]
        dn = sb("dn", [128, 8])
        rd = sb("rd", [128, 8])
        hss2 = sb("hss2", [128, 2, 4])
        yc_tm = tmp_b[0]

        psum = [ctx.enter_context(nc.psum_tensor(f"psum{i}", [128, 1024], F32)) for i in range(4)]
        block = ctx.enter_context(nc.Block())

        def aview(i, shape_str, **kw):
            return arena[:, i * 4096:(i + 1) * 4096].rearrange(shape_str, **kw)

        xp_tm = aview(0, "p (t c) -> p t c", t=NT)
        vn_tm = aview(1, "p (t c) -> p t c", t=NT)
        su_fm = aview(2, "p (g t) -> p g t", g=4)
        q_fm = aview(3, "p (g t) -> p g t", g=4)
        kf_fm = aview(4, "p (g t) -> p g t", g=4)
        o_tm = aview(4, "p (t c) -> p t c", t=NT)
        p0b = aview(4, "p (t c) -> p t c", t=NT)
        k_tm = aview(5, "p (t c) -> p t c", t=NT)
        a01_f32 = arena[:, 0:8192].bitcast(F32)
        h_tm = a01_f32.rearrange("p (t h e) -> p t h e", t=NT, h=4)
        acc_fm = a01_f32.rearrange("p (c t) -> p c t", c=4)
        io_tm = [a01_f32[:, i * 1024:(i + 1) * 1024] for i in range(2)]
        hff = [arena[:, 8192 + i * 8192: 8192 + (i + 1) * 8192].rearrange("p (c t) -> p c t", c=8) for i in range(2)]
        yc_fm = q_fm
        yb_fm = su_fm
        y_a = v_tm[:].rearrange("p t c -> p (t c)").rearrange("p (g t) -> p g t", g=4)
        p0t_flat = p0t[:].rearrange("p t c -> p (t c)")
        d_fm = p0t_flat.rearrange("p (g t) -> p g t", g=4)
        merged = vw_ext[:].rearrange("p t g c -> p (t g c)")[:, 0:8192].rearrange("p (c t) -> p c t", c=8)

        B = {}

        def bf(name):
            if name not in B:
                B[name] = Buf(name)
            return B[name]

        bx = [[bf(f"x{c}_{h}") for h in range(2)] for c in range(KC)]
        bu = [bf(f"u{h}") for h in range(2)]
        bws = [bf(f"ws{i}") for i in range(NSLOT)]
        bps = [bf(f"ps{i}") for i in range(8)]
        bft = [bf(f"ftile{i}") for i in range(3)]
        b_xp, b_vn, b_su, b_q, b_kf, b_kt, b_o = (bf(n) for n in ("xp", "vn", "su", "q", "kf", "kt", "o"))
        b_h = bf("h")
        bacc = [[bf(f"acc{n}_{h}") for h in range(2)] for n in range(4)]
        bio = [bf("io0"), bf("io1")]
        bhff = [bf("hff0"), bf("hff1")]
        alias(b_h, b_xp, b_vn)
        for n in range(4):
            for h in range(2):
                alias(bacc[n][h], b_h)
                alias(bacc[n][h], b_xp)
                alias(bacc[n][h], b_vn)
        for i in range(2):
            alias(bio[i], b_xp)
            alias(bio[i], b_h)
            for n in range(4):
                for h in range(2):
                    alias(bio[i], bacc[n][h])
        alias(bhff[0], b_su, b_q)
        alias(bhff[1], b_kf, b_kt, b_o)
        b_p0t, b_mrg, b_d = bf("p0t"), bf("mrg"), bf("d")
        alias(b_p0t, b_d)
        b_p0b = bf("p0b")
        alias(b_p0b, b_kf)
        alias(b_p0b, b_o)
        alias(b_p0b, bhff[1])
        alias(b_mrg, bf("vw"))
        b_v, b_ya = bf("v"), bf("y_a")
        alias(b_v, b_ya)

        def psv(i):
            return psum[i // 2][:, (i % 2) * 512:(i % 2 + 1) * 512]

        ps_rr = [0]

        ps_reserved = set()

        def ps_next():
            while True:
                i = ps_rr[0] % 8
                ps_rr[0] += 1
                if i not in ps_reserved:
                    return i

        chain_q = []

        def drain(n):
            for _ in range(min(n, len(chain_q))):
                chain_q.pop(0)()

        ws_rr = [0]

        def load_w(view, kc, n):
            i = ws_rr[0] % NSLOT
            ws_rr[0] += 1
            dst = wslot[i][:, 0:kc * n].rearrange("p (k n) -> p k n", k=kc)
            P.dma("pool", dst, view, writes=[bws[i]])
            return dst, bws[i]

        def dump(name, ap, shape, reads, dt=F32):
            if name in dbg:
                o = dout("dbg_" + name, shape, dt)
                dbg_toks.append(P.dma("sp", o, ap, reads=reads))

        out_toks = []
        evac_rr = [0]

        def copy_evac(dst, src, reads, writes, scale=None):
            evac_rr[0] += 1
            if evac_rr[0] % 2 == 0:
                if scale is None:
                    P.op("act", lambda e: e.activation(out=dst, in_=src, func=AF.Copy), reads=reads, writes=writes)
                else:
                    P.op("act", lambda e: e.activation(out=dst, in_=src, func=AF.Copy, scale=scale), reads=reads, writes=writes)
            else:
                if scale is None:
                    P.op("dve", lambda e: e.tensor_copy(out=dst, in_=src), reads=reads, writes=writes)
                else:
                    P.op("dve", lambda e: e.tensor_scalar(out=dst, in0=src, scalar1=scale, scalar2=None, op0=ALU.mult),
                         reads=reads, writes=writes)

        P.dma("sp", ident[:], ident_in, writes=[bf("ident")])
        P.dma("sp", masks[:].rearrange("p a b -> p (a b)"), masks_in, writes=[bf("masks")])
        P.dma("sp", cond_sb[:], cond_in, writes=[bf("cond")])
        P.dma("sp", keep[:], keep_in, writes=[bf("keep")])
        P.dma("sp", minit[:].rearrange("p a b c -> p (a b c)"), minit_in, writes=[bf("minit")])
        P.dma("sp", gn[:].rearrange("p a b c -> p (a b c)"), gn_in, writes=[bf("gn")])
        P.dma("sp", bada[:].rearrange("p a b -> p (a b)"), bada_in, writes=[bf("bada")])
        P.dma("sp", pscale[:].rearrange("p a b -> p (a b)"), pscale_in, writes=[bf("pscale")])
        P.dma("sp", gsgu[:].rearrange("p a b -> p (a b)"), gsgu_in, writes=[bf("gsgu")])
        P.dma("sp", gfin[:], gfin_in, writes=[bf("gfin")])
        P.op("dve", lambda e: e.memset(ones_mean_bf[:], 1.0 / D), writes=[bf("ones_mean_bf")])
        P.op("dve", lambda e: e.memset(ones_f[:], 1.0), writes=[bf("ones_f")])
        P.op("dve", lambda e: e.tensor_copy(out=ident_bf[:], in_=ident[:]), reads=[bf("ident")], writes=[bf("ident_bf")])
        P.op("dve", lambda e: e.memset(mprev[:], 0.0), writes=[bf("mchain")])
        P.op("dve", lambda e: e.memset(Cst[:], 0.0), writes=[bf("Cst0"), bf("Cst1")])
        P.op("dve", lambda e: e.memset(Cfin[:], 0.0), writes=[bf("Cfin0"), bf("Cfin1")])
        P.op("dve", lambda e: e.memset(mstage[:], 0.0), writes=[bf("mstage")])
        for i in range(2):
            P.op("dve", lambda e, i=i: e.memset(Cd_bf[i][:], 0.0), writes=[bf(f"Cdbf{i}_0"), bf(f"Cdbf{i}_1")])
        P.op("dve", lambda e: e.memset(vw_ext[:], 0.0), writes=[bf("vw")])
        P.op("act", lambda e: e.activation(out=scond[:], in_=cond_sb[:], func=AF.Silu), reads=[bf("cond")], writes=[bf("scond")])

        for t in range(NT):
            it = io_tm[t % 2]
            bi = bio[t % 2]
            P.dma("sp", it, x_in[t * 128:(t + 1) * 128, :], writes=[bi])
            for cg in range(2):
                pi = ps_next()
                for c4 in range(4):
                    c = cg * 4 + c4
                    P.op("pe", lambda e, pi=pi, c4=c4, c=c, it=it: e.matmul(
                        psv(pi)[:, c4 * 128:(c4 + 1) * 128], lhsT=it[:, c * 128:(c + 1) * 128], rhs=ident[:],
                        start=True, stop=True), reads=[bi, bf("ident")], writes=[bps[pi]])
                copy_evac(x_fm[:, cg * 4:(cg + 1) * 4, t * 128:(t + 1) * 128], psv(pi).rearrange("p (c t) -> p c t", c=4),
                          [bps[pi]], [bx[c][t // 4] for c in range(cg * 4, cg * 4 + 4)])

        bmod = [[bf(f"mod{l}_{w}") for w in range(2)] for l in range(DEPTH)]
        bAB = [[bf(f"AB{l}_{w}") for w in range(2)] for l in range(DEPTH)]

        def ada_mm(l, ng, pcols, pbufs):
            wv, wb_ = load_w(w_ada[l, :, ng * 512:(ng + 1) * 512].rearrange("(k p) n -> p k n", p=128), KC, 512)
            for n4 in range(4):
                for kc in range(KC):
                    P.op("pe", lambda e, n4=n4, kc=kc: e.matmul(
                        pcols[:, n4:n4 + 1], lhsT=wv[:, kc, n4 * 128:(n4 + 1) * 128], rhs=scond[:, kc:kc + 1],
                        start=(kc == 0), stop=(kc == KC - 1)), reads=[wb_, bf("scond")], writes=pbufs)

        def ada_evac(l, ng, pcols, pbufs):
            piece = 0 if ng < 4 else 1
            P.op("dve", lambda e: e.tensor_tensor(out=modfm[:, l, ng * 4:(ng + 1) * 4], in0=pcols, in1=bada[:, l, ng * 4:(ng + 1) * 4], op=ALU.add),
                 reads=list(pbufs) + [bf("bada")], writes=[bmod[l][piece]])
            if ng == 3 or ng == 9:
                which = 0 if ng == 3 else 1
                off = 8 if which == 0 else 32
                P.op("dve", lambda e: e.scalar_tensor_tensor(
                    out=AB[:, l, which, :], in0=modfm[:, l, off:off + 8], scalar=1.0, in1=gn[:, l, which, :], op0=ALU.add, op1=ALU.mult),
                    reads=[bmod[l][piece], bf("gn")], writes=[bAB[l][which]])

        for ng in range(4):
            pi = ps_next()
            ada_mm(0, ng, psv(pi)[:, 0:4], [bps[pi]])
            ada_evac(0, ng, psv(pi)[:, 0:4], [bps[pi]])
        dump("xfm", x_fm[:].rearrange("p c t -> p (c t)"), [128, KC * T], [b for bb in bx for b in bb])

        def ms_rstd(h):
            hs = slice(h * 512, (h + 1) * 512)
            pi = ps_next()
            for c in range(KC):
                sq, bsq = tmp_b[c % 2], bf(f"tmpb{c % 2}")
                P.op("act", lambda e, sq=sq, c=c: e.activation(out=sq[:], in_=x_fm[:, c, hs], func=AF.Square),
                     reads=[bx[c][h]], writes=[bsq])
                P.op("pe", lambda e, sq=sq, c=c: e.matmul(psv(pi), lhsT=ones_mean_bf[:], rhs=sq[:], start=(c == 0), stop=(c == KC - 1)),
                     reads=[bsq, bf("ones_mean_bf")], writes=[bps[pi]])
            P.op("act", lambda e: e.activation(out=ftile[2][:], in_=psv(pi), func=AF.Ln, bias=EPS), reads=[bps[pi]], writes=[bft[2]])
            P.op("act", lambda e: e.activation(out=ftile[2][:], in_=ftile[2][:], func=AF.Exp, scale=-0.5), reads=[bft[2]], writes=[bft[2]])

        def rmsnorm_to_u(l, which):
            shoff = 0 if which == 0 else 24
            for h in range(2):
                hs = slice(h * 512, (h + 1) * 512)
                ms_rstd(h)
                for c in range(KC):
                    tf, btf = ftile[c % 2], bft[c % 2]
                    P.op("dve", lambda e, tf=tf, c=c, hs=hs: e.tensor_tensor(out=tf[:], in0=x_fm[:, c, hs], in1=ftile[2][:], op=ALU.mult),
                         reads=[bx[c][h], bft[2]], writes=[btf])
                    P.op("act", lambda e, tf=tf, c=c, hs=hs: e.activation(
                        out=u_fm[:, c, hs], in_=tf[:], func=AF.Identity,
                        scale=AB[:, l, which, c:c + 1], bias=modfm[:, l, shoff + c:shoff + c + 1]),
                        reads=[btf, bAB[l][which], bmod[l][which]], writes=[bu[h]])

        def proj_fm(wv, wb_, ncols, evac):
            for nchk in range(ncols // 128):
                for h in range(2):
                    pi = ps_next()
                    for kc in range(KC):
                        P.op("pe", lambda e, pi=pi, kc=kc, nchk=nchk, h=h: e.matmul(
                            psv(pi), lhsT=wv[:, kc, nchk * 128:(nchk + 1) * 128], rhs=u_fm[:, kc, h * 512:(h + 1) * 512],
                            start=(kc == 0), stop=(kc == KC - 1)), reads=[wb_, bu[h]], writes=[bps[pi]])
                    evac(pi, nchk, h)
                    drain(2)

        def proj_tm(wv, wb_, ncols, evac):
            for t in range(NT):
                pi = ps_next()
                for kc in range(KC):
                    P.op("pe", lambda e, pi=pi, kc=kc, t=t: e.matmul(
                        psv(pi)[:, 0:ncols], lhsT=u_fm[:, kc, t * 128:(t + 1) * 128], rhs=wv[:, kc, 0:ncols],
                        start=(kc == 0), stop=(kc == KC - 1)), reads=[wb_, bu[t // 4]], writes=[bps[pi]])
                evac(pi, t)
                drain(2)

        def w_in_unit(l, col, n=512):
            return load_w(w_in[l, :, col:col + n].rearrange("(k p) n -> p k n", p=128), KC, n)

        ksc = 128 ** -0.5

        def layer(l):
            P.dma("pool", wg[:], w_in[l, :, C_G:C_G + 16].rearrange("(k p) n -> p k n", p=128), writes=[bf("wg")])
            P.dma("pool", wpool[:], w_pool[l].rearrange("g c d -> c g d"), writes=[bf("wpool")])
            P.dma("pool", wsgu[:].rearrange("p g q -> p (g q)"), w_sguT[l], writes=[bf("wsgu")])
            P.dma("sp", bg_rep[:], b_gates[l, :].partition_broadcast(128), writes=[bf("bg_rep")])
            P.dma("sp", bsgu_rep[:].rearrange("p g q -> p (g q)"), b_sgu[l, :].partition_broadcast(128), writes=[bf("bsgu_rep")])
            P.dma("sp", gml_rep[:], g_mlstm[l, :].partition_broadcast(128), writes=[bf("gml_rep")])

            rmsnorm_to_u(l, 0)
            if l == 0:
                dump("u0", u_fm[:].rearrange("p c t -> p (c t)"), [128, KC * T], bu, BF16)

            wv, wb_ = w_in_unit(l, C_XP)
            proj_tm(wv, wb_, 512, lambda pi, t: copy_evac(xp_tm[:, t, :], psv(pi), [bps[pi]], [b_xp]))
            pi = ps_next()
            for t in range(NT):
                for kc in range(KC):
                    P.op("pe", lambda e, pi=pi, kc=kc, t=t: e.matmul(
                        psv(pi)[:, t * 16:(t + 1) * 16], lhsT=u_fm[:, kc, t * 128:(t + 1) * 128], rhs=wg[:, kc, :],
                        start=(kc == 0), stop=(kc == KC - 1)), reads=[bf("wg"), bu[t // 4]], writes=[bps[pi]])
            P.op("dve", lambda e, pi=pi: e.tensor_tensor(
                out=gp[:], in0=psv(pi)[:, 0:128].rearrange("p (t g) -> p t g", t=NT),
                in1=bg_rep[:, :].unsqueeze(1).to_broadcast([128, NT, 16]), op=ALU.add),
                reads=[bps[pi], bf("bg_rep")], writes=[bf("gp")])
            if l == 0:
                dump("gp", gp[:].rearrange("p t g -> p (t g)"), [128, 128], [bf("gp")])

            fg = gp[:, :, 8:16]
            P.op("dve", lambda e: e.scalar_tensor_tensor(out=gt0[:], in0=fg, scalar=-1.0, in1=fg, op0=ALU.mult, op1=ALU.max),
                 reads=[bf("gp")], writes=[bf("gt0")])
            P.op("act", lambda e: e.activation(out=gt0[:], in_=gt0[:], func=AF.Exp, scale=-1.0), reads=[bf("gt0")], writes=[bf("gt0")])
            P.op("act", lambda e: e.activation(out=gt0[:], in_=gt0[:], func=AF.Ln, bias=1.0), reads=[bf("gt0")], writes=[bf("gt0")])
            P.op("dve", lambda e: e.scalar_tensor_tensor(out=G_tm[:, :, 8:16], in0=fg, scalar=0.0, in1=gt0[:], op0=ALU.min, op1=ALU.subtract),
                 reads=[bf("gp"), bf("gt0")], writes=[bf("G_tm")])
            pi = ps_next()
            for t in range(NT):
                for dr in range(2):
                    P.op("pe", lambda e, pi=pi, t=t, dr=dr: e.matmul(
                        psv(pi)[:, t * 8 + dr * 4: t * 8 + dr * 4 + 4], lhsT=masks[:, dr, :], rhs=G_tm[:, t, 8 + dr * 4: 12 + dr * 4],
                        start=True, stop=True), reads=[bf("masks"), bf("G_tm")], writes=[bps[pi]])
            P.op("dve", lambda e, pi=pi: e.tensor_copy(out=b_tm[:], in_=psv(pi)[:, 0:64].rearrange("p (t g) -> p t g", t=NT)),
                 reads=[bps[pi]], writes=[bf("b_tm")])
            P.op("dve", lambda e: e.tensor_tensor(out=G_tm[:, :, 0:8], in0=gp[:, :, 0:8], in1=b_tm[:], op=ALU.subtract),
                 reads=[bf("gp"), bf("b_tm")], writes=[bf("G_tm")])
            pi = ps_next()
            P.op("pe", lambda e, pi=pi: e.matmul(psv(pi)[:, 0:128], lhsT=G_tm[:].rearrange("p t g -> p (t g)"), rhs=ident[:],
                                                 start=True, stop=True), reads=[bf("G_tm"), bf("ident")], writes=[bps[pi]])
            P.op("dve", lambda e, pi=pi: e.tensor_reduce(out=Q2[:, 0:1], in_=psv(pi)[:, 0:128], axis=AX.X, op=ALU.max),
                 reads=[bps[pi]], writes=[bf("Q2")])
            P.op("dve", lambda e, pi=pi: e.tensor_reduce(out=Q2[:, 1:2], in_=psv(pi)[:, 0:128], axis=AX.X, op=ALU.add),
                 reads=[bps[pi]], writes=[bf("Q2")])
            for qi, R in ((0, Rmax), (1, Rsum)):
                P.op("dve", lambda e, qi=qi: e.tensor_scalar(out=Dg[:], in0=ident[:], scalar1=Q2[:, qi:qi + 1], scalar2=None, op0=ALU.mult),
                     reads=[bf("ident"), bf("Q2")], writes=[bf("Dg")])
                pi = ps_next()
                P.op("pe", lambda e, pi=pi: e.matmul(psv(pi)[:, 0:128], lhsT=ones_f[:], rhs=Dg[:], start=True, stop=True),
                     reads=[bf("ones_f"), bf("Dg")], writes=[bps[pi]])
                P.op("dve", lambda e, pi=pi, R=R: e.tensor_copy(out=R[:].rearrange("p t g -> p (t g)"), in_=psv(pi)[:, 0:128]),
                     reads=[bps[pi]], writes=[bf("R")])
            bm = bf("mchain")

            def ch_reset(dr, pidx, slot, ds):
                return lambda: P.op("dve", lambda e: e.scalar_tensor_tensor(
                    out=mprev[:, dr, pidx, :], in0=mprev[:, dr, pidx, :], scalar=keep[:, slot:slot + 1],
                    in1=minit[:, l, slot, ds], op0=ALU.mult, op1=ALU.add),
                    reads=[bf("keep"), bf("minit"), bm], writes=[bm])

            def ch_max(dr, pidx, c, ds):
                return lambda: P.op("dve", lambda e: e.tensor_tensor(
                    out=Mlast[:, dr, c, :], in0=mprev[:, dr, pidx, :], in1=Rmax[:, c, ds], op=ALU.max),
                    reads=[bf("R"), bm], writes=[bm])

            def ch_add(dr, nidx, c, dr4):
                return lambda: P.op("dve", lambda e: e.tensor_tensor(
                    out=mprev[:, dr, nidx, :], in0=Mlast[:, dr, c, :], in1=Rsum[:, c, 8 + dr4:12 + dr4], op=ALU.add),
                    reads=[bf("R"), bm], writes=[bm])

            def ch_stage(dr, nidx, slot):
                return lambda: P.op("dve", lambda e: e.tensor_copy(
                    out=mstage[:, slot, dr * 4:dr * 4 + 4], in_=mprev[:, dr, nidx, :]), reads=[bm], writes=[bf("mstage")])

            for j in range(NT):
                for dr in range(2):
                    c = j if dr == 0 else NT - 1 - j
                    pidx = c if dr == 0 else c + 1
                    nidx = c + 1 if dr == 0 else c
                    ds = slice(dr * 4, dr * 4 + 4)
                    if j % 2 == 0:
                        chain_q.append(ch_reset(dr, pidx, j // 2, ds))
                    chain_q.append(ch_max(dr, pidx, c, ds))
                    chain_q.append(ch_add(dr, nidx, c, dr * 4))
                    if j % 2 == 1:
                        chain_q.append(ch_stage(dr, nidx, j // 2))
            wv, wb_ = w_in_unit(l, C_SU)
            proj_fm(wv, wb_, 512, lambda pi, n, h: copy_evac(su_fm[:, n, h * 512:(h + 1) * 512], psv(pi), [bps[pi]], [b_su]))
            wv, wb_ = w_in_unit(l, C_SV)

            def sv_evac(pi, t):
                P.op("act", lambda e: e.activation(out=ftile[0][:], in_=psv(pi), func=AF.Square, accum_out=ssq[:, t:t + 1]),
                     reads=[bps[pi]], writes=[bft[0], bf("ssq")])
                P.op("act", lambda e: e.activation(out=rsv[:, t:t + 1], in_=ssq[:, t:t + 1], func=AF.Ln, scale=1.0 / 512, bias=EPS),
                     reads=[bf("ssq")], writes=[bf("rsv")])
                P.op("act", lambda e: e.activation(out=rsv[:, t:t + 1], in_=rsv[:, t:t + 1], func=AF.Exp, scale=-0.5),
                     reads=[bf("rsv")], writes=[bf("rsv")])
                P.op("dve", lambda e: e.tensor_scalar(out=vn_tm[:, t, :], in0=psv(pi), scalar1=rsv[:, t:t + 1], scalar2=None, op0=ALU.mult),
                     reads=[bps[pi], bf("rsv")], writes=[b_vn])
            proj_tm(wv, wb_, 512, sv_evac)
            wv, wb_ = w_in_unit(l, C_Q)
            proj_fm(wv, wb_, 512, lambda pi, n, h: copy_evac(q_fm[:, n, h * 512:(h + 1) * 512], psv(pi), [bps[pi]], [b_q]))
            wv, wb_ = w_in_unit(l, C_K)
            proj_fm(wv, wb_, 512, lambda pi, n, h: copy_evac(kf_fm[:, n, h * 512:(h + 1) * 512], psv(pi), [bps[pi]], [b_kf], scale=ksc))
            for t in range(NT):
                pi = ps_next()
                for hd in range(4):
                    P.op("pe", lambda e, pi=pi, hd=hd, t=t: e.matmul(
                        psv(pi)[:, hd * 128:(hd + 1) * 128], lhsT=kf_fm[:, hd, t * 128:(t + 1) * 128], rhs=ident_bf[:],
                        start=True, stop=True), reads=[b_kf, bf("ident_bf")], writes=[bps[pi]])
                copy_evac(k_tm[:, t, :], psv(pi), [bps[pi]], [b_kt])
                drain(2)
            wv, wb_ = w_in_unit(l, C_V)
            proj_tm(wv, wb_, 512, lambda pi, t: copy_evac(v_tm[:, t, :], psv(pi), [bps[pi]], [b_v]))
            drain(len(chain_q))
            out_toks.append(P.dma("sp", m_out[l:l + 1].rearrange("a s g -> a (s g)"), mstage[0:1].rearrange("p s g -> p (s g)"),
                                  reads=[bf("mstage")]))
            P.op("dve", lambda e: e.tensor_tensor(out=decay[:, 0, :, :], in0=mprev[:, 0, 0:NT, :], in1=Mlast[:, 0, :, :], op=ALU.subtract),
                 reads=[bm], writes=[bf("decay")])
            P.op("dve", lambda e: e.tensor_tensor(out=decay[:, 1, :, :], in0=mprev[:, 1, 1:NT + 1, :], in1=Mlast[:, 1, :, :], op=ALU.subtract),
                 reads=[bm], writes=[bf("decay")])
            P.op("act", lambda e: e.activation(out=decay[:], in_=decay[:], func=AF.Exp), reads=[bf("decay")], writes=[bf("decay")])
            for dr in range(2):
                ds = slice(dr * 4, dr * 4 + 4)
                P.op("dve", lambda e, dr=dr, ds=ds: e.tensor_tensor(
                    out=wst[:, :, ds], in0=G_tm[:, :, ds], in1=Mlast[:, dr, :, :], op=ALU.subtract),
                    reads=[bf("G_tm"), bm], writes=[bf("wst")])
                P.op("dve", lambda e, dr=dr, ds=ds: e.scalar_tensor_tensor(
                    out=clampt[:, :, ds], in0=b_tm[:, :, ds], scalar=-1.0, in1=Mlast[:, dr, :, :],
                    op0=ALU.mult, op1=ALU.subtract), reads=[bf("b_tm"), bm], writes=[bf("clampt")])
            P.op("act", lambda e: e.activation(out=wst[:], in_=wst[:], func=AF.Exp), reads=[bf("wst")], writes=[bf("wst")])
            P.op("act", lambda e: e.activation(out=clampt[:], in_=clampt[:], func=AF.Exp), reads=[bf("clampt")], writes=[bf("clampt")])
            if l == 0:
                dump("wst", wst[:].rearrange("p t g -> p (t g)"), [128, 64], [bf("wst")])
                dump("clampt", clampt[:].rearrange("p t g -> p (t g)"), [128, 64], [bf("clampt")])
                dump("decay", decay[:].rearrange("p a c g -> p (a c g)"), [128, 64], [bf("decay")])
            for dr in range(2):
                P.op("dve", lambda e, dr=dr: e.tensor_tensor(
                    out=vw_ext[:, :, dr * 4:dr * 4 + 4, 0:128], in0=v_tm[:].rearrange("p t (h e) -> p t h e", h=4),
                    in1=wst[:, :, dr * 4:dr * 4 + 4].unsqueeze(3).to_broadcast([128, NT, 4, 128]), op=ALU.mult),
                    reads=[b_v, bf("wst")], writes=[bf("vw")])
            P.op("dve", lambda e: e.tensor_copy(out=vw_ext[:, :, :, 128], in_=wst[:]), reads=[bf("wst")], writes=[bf("vw")])

            def sgu_tile(t):
                pi = ps_next()
                for g in range(4):
                    P.op("pe", lambda e, g=g: e.matmul(
                        psv(pi)[:, g * 128:(g + 1) * 128], lhsT=vn_tm[:, t, g * 128:(g + 1) * 128], rhs=wsgu[:, g, :],
                        start=True, stop=True), reads=[b_vn, bf("wsgu")], writes=[bps[pi]])
                tf, btf = ftile[t % 2], bft[t % 2]
                for g in range(4):
                    P.op("dve", lambda e, g=g: e.scalar_tensor_tensor(
                        out=tf[:, g * 128:(g + 1) * 128], in0=psv(pi)[:, g * 128:(g + 1) * 128], scalar=gsgu[:, l, g:g + 1],
                        in1=bsgu_rep[:, g, :], op0=ALU.mult, op1=ALU.add),
                        reads=[bps[pi], bf("gsgu"), bf("bsgu_rep")], writes=[btf])
                P.op("dve", lambda e: e.tensor_tensor(
                    out=yb_fm[:, :, t * 128:(t + 1) * 128], in0=su_fm[:, :, t * 128:(t + 1) * 128],
                    in1=tf[:].rearrange("p (g q) -> p g q", g=4), op=ALU.mult),
                    reads=[btf, b_su], writes=[b_su])

            for t in range(NT):
                sgu_tile(t)
            if l == 0:
                dump("y_b", yb_fm, [128, 4, T], [b_su], BF16)
            ada_q = [(l, ng) for ng in range(4, 12)] + ([(l + 1, ng) for ng in range(4)] if l + 1 < depth else [])
            ada_g = ada_q[:4]
            ada_l = ada_q[4:]
            for g in range(4):
                for h in range(2):
                    mv, mb = load_w(poolm_in[g, :, h * 512:(h + 1) * 512].rearrange("(k p) n -> p k n", p=128), KC, 512)
                    pi = ps_next()
                    for sc in range(NT):
                        P.op("pe", lambda e, pi=pi, sc=sc, g=g, mv=mv: e.matmul(
                            psv(pi), lhsT=xp_tm[:, sc, g * 128:(g + 1) * 128], rhs=mv[:, sc, :],
                            start=(sc == 0), stop=(sc == NT - 1)), reads=[b_xp, mb], writes=[bps[pi]])
                    copy_evac(d_fm[:, g, h * 512:(h + 1) * 512], psv(pi), [bps[pi]], [b_d])
            for g in range(4):
                for h in range(2):
                    pi = ps_next()
                    P.op("pe", lambda e, pi=pi, g=g, h=h: e.matmul(
                        psv(pi), lhsT=wpool[:, g, :], rhs=d_fm[:, g, h * 512:(h + 1) * 512], start=True, stop=True),
                        reads=[bf("wpool"), b_d], writes=[bps[pi]])
                    copy_evac(y_a[:, g, h * 512:(h + 1) * 512], psv(pi), [bps[pi], bf("pscale")], [b_ya],
                              scale=pscale[:, l, g:g + 1])
            if l == 0:
                dump("y_a", y_a, [128, 4, T], [b_ya], BF16)


            st_banks = []
            for t in range(NT):
                pi = ps_next()
                st_banks.append(pi)
                for hd in range(4):
                    P.op("pe", lambda e, pi=pi, hd=hd, t=t: e.matmul(
                        psv(pi)[:, hd * 128:(hd + 1) * 128], lhsT=kf_fm[:, hd, t * 128:(t + 1) * 128], rhs=q_fm[:, hd, t * 128:(t + 1) * 128],
                        start=True, stop=True), reads=[b_kf, b_q], writes=[bps[pi]])
                P.op("dve", lambda e, pi=pi, t=t: e.tensor_tensor(
                    out=p0t[:, t, :].rearrange("p (h s) -> p h s", h=4), in0=psv(pi).rearrange("p (h s) -> p h s", h=4),
                    in1=masks[:, 0, :].unsqueeze(1).to_broadcast([128, 4, 128]), op=ALU.mult),
                    reads=[bps[pi], bf("masks")], writes=[b_p0t])
            for t in range(NT):
                pi = st_banks[t]
                P.op("dve", lambda e, pi=pi, t=t: e.tensor_tensor(
                    out=p0b[:, t, :].rearrange("p (h s) -> p h s", h=4), in0=psv(pi).rearrange("p (h s) -> p h s", h=4),
                    in1=masks[:, 1, :].unsqueeze(1).to_broadcast([128, 4, 128]), op=ALU.mult),
                    reads=[bps[pi], bf("masks")], writes=[b_p0b])
            pmat = [(p0t, b_p0t), (p0b, b_p0b)]

            bCs = [bf("Cst0"), bf("Cst1")]
            bCf = [bf("Cfin0"), bf("Cfin1")]
            bCd = [bf("Cd0"), bf("Cd1")]
            bhh = [[bf(f"h{t}_{hd}") for hd in range(4)] for t in range(NT)]
            for t in range(NT):
                for hd in range(4):
                    alias(bhh[t][hd], b_h)
            h_open = P.op("dve", lambda e: e.memset(dn[:, 0:1], 0.0), writes=[b_h, bf("dn0")])

            def cin_load(slot, dr):
                P.dma("sp", cin[dr][:], cinit_in[l, slot].rearrange("p (a b) -> p a b", a=8)[:, dr * 4:dr * 4 + 4, :],
                      writes=[bf(f"cin{dr}")])

            for dr in range(2):
                cin_load(0, dr)

            def prep(j, dr):
                c = j if dr == 0 else NT - 1 - j
                ds = slice(dr * 4, dr * 4 + 4)
                if j % 2 == 0:
                    slot = j // 2
                    ci, bci = cin[dr], bf(f"cin{dr}")
                    P.op("dve", lambda e: e.scalar_tensor_tensor(
                        out=Cst[:, ds, :], in0=Cfin[:, ds, :], scalar=keep[:, slot:slot + 1], in1=ci[:],
                        op0=ALU.mult, op1=ALU.add), reads=[bf("keep"), bci, bCf[dr]], writes=[bCs[dr]])
                    if slot + 1 < 4:
                        cin_load(slot + 1, dr)
                P.op("dve", lambda e: e.tensor_tensor(
                    out=Cd[:, ds, :], in0=Cst[:, ds, :], in1=decay[:, dr, c, :].unsqueeze(2).to_broadcast([128, 4, 129]), op=ALU.mult),
                    reads=[bCs[dr], bf("decay")], writes=[bCd[dr]])
                cb = Cd_bf[j % 2]
                P.op("act", lambda e: e.activation(out=cb[:, ds, 0:129], in_=Cd[:, ds, :], func=AF.Copy),
                     reads=[bCd[dr]], writes=[bf(f"Cdbf{j % 2}_{dr}")])

            for dr in range(2):
                prep(0, dr)
            ada_cols = psum[3][:, 960:964]
            ada_prev = [None]
            for j in range(NT):
                tiles = [j, NT - 1 - j]
                for dr in range(2):
                    t_ = tiles[dr]
                    upv = psum[2 + dr][:].rearrange("p (h c) -> p h c", h=4)
                    bup = [bps[4 + dr * 2], bps[5 + dr * 2]]
                    for hd in range(4):
                        P.op("pe", lambda e, hd=hd, t_=t_, dr=dr, upv=upv: e.matmul(
                            upv[:, hd, 0:129], lhsT=k_tm[:, t_, hd * 128:(hd + 1) * 128], rhs=vw_ext[:, t_, dr * 4 + hd, 0:129],
                            start=True, stop=True), reads=[b_kt, bf("vw")], writes=[bup[hd // 2]])
                for dr in range(2):
                    t_ = tiles[dr]
                    ndv = psum[dr][:].rearrange("p (h c) -> p h c", h=4)
                    bnd = [bps[dr * 2], bps[dr * 2 + 1]]
                    pm, bpm = pmat[dr]
                    cb = Cd_bf[j % 2]
                    for hd in range(4):
                        P.op("pe", lambda e, hd=hd, t_=t_, dr=dr, ndv=ndv, pm=pm: e.matmul(
                            ndv[:, hd, 0:129], lhsT=pm[:, t_, hd * 128:(hd + 1) * 128], rhs=vw_ext[:, t_, dr * 4 + hd, 0:129],
                            start=True, stop=False), reads=[bpm, bf("vw")], writes=[bnd[hd // 2]])
                        P.op("pe", lambda e, hd=hd, t_=t_, dr=dr, ndv=ndv, cb=cb: e.matmul(
                            ndv[:, hd, 0:129], lhsT=q_fm[:, hd, t_ * 128:(t_ + 1) * 128], rhs=cb[:, dr * 4 + hd, 0:129],
                            start=False, stop=True), reads=[b_q, bf(f"Cdbf{j % 2}_{dr}")], writes=[bnd[hd // 2]])
                for dr in range(2):
                    ds = slice(dr * 4, dr * 4 + 4)
                    upv = psum[2 + dr][:].rearrange("p (h c) -> p h c", h=4)
                    bup = [bps[4 + dr * 2], bps[5 + dr * 2]]
                    last = (j % 2 == 1)
                    dst, bdst = (Cfin, bCf[dr]) if last else (Cst, bCs[dr])
                    P.op("dve", lambda e, ds=ds, upv=upv, dst=dst: e.tensor_tensor(
                        out=dst[:, ds, :], in0=Cd[:, ds, :], in1=upv[:, :, 0:129], op=ALU.add),
                        reads=bup + [bCd[dr]], writes=[bdst])
                    if last:
                        slot = j // 2
                        out_toks.append(P.dma("sp", c_out[l, slot, dr].rearrange("p (h c) -> p h c", h=4), Cfin[:, ds, :], reads=[bdst]))
                if ada_l or ada_prev[0] is not None:
                    if ada_prev[0] is not None:
                        ada_evac(ada_prev[0][0], ada_prev[0][1], ada_cols, [bps[7]])
                        ada_prev[0] = None
                    if ada_l:
                        al, ang = ada_l.pop(0)
                        ada_mm(al, ang, ada_cols, [bps[7]])
                        ada_prev[0] = (al, ang)
                if j + 1 < NT:
                    for dr in range(2):
                        prep(j + 1, dr)
                for dr in range(2):
                    t_ = tiles[dr]
                    ds = slice(dr * 4, dr * 4 + 4)
                    ndv = psum[dr][:].rearrange("p (h c) -> p h c", h=4)
                    bnd = [bps[dr * 2], bps[dr * 2 + 1]]
                    bdn, brd = bf(f"dn{dr}"), bf(f"rd{dr}")
                    P.op("dve", lambda e, ds=ds, t_=t_, ndv=ndv: e.tensor_tensor(
                        out=dn[:, ds], in0=ndv[:, :, 128], in1=clampt[:, t_, ds], op=ALU.max),
                        reads=bnd + [bf("clampt")], writes=[bdn])
                    P.op("dve", lambda e, ds=ds, ndv=ndv: e.scalar_tensor_tensor(
                        out=dn[:, ds], in0=ndv[:, :, 128], scalar=-1.0, in1=dn[:, ds], op0=ALU.mult, op1=ALU.max),
                        reads=bnd + [bdn], writes=[bdn])
                    P.op("dve", lambda e, ds=ds: e.reciprocal(out=rd[:, ds], in_=dn[:, ds]), reads=[bdn], writes=[brd])
                    if j < NT // 2:
                        P.op("dve", lambda e, ds=ds, t_=t_, ndv=ndv: e.tensor_tensor(
                            out=h_tm[:, t_, :, :], in0=ndv[:, :, 0:128],
                            in1=rd[:, ds].unsqueeze(2).to_broadcast([128, 4, 128]), op=ALU.mult),
                            reads=bnd + [brd], writes=bhh[t_], extra=[h_open])
                    else:
                        for hd in range(4):
                            P.op("dve", lambda e, hd=hd, dr=dr, t_=t_, ndv=ndv: e.scalar_tensor_tensor(
                                out=h_tm[:, t_, hd, :], in0=ndv[:, hd, 0:128], scalar=rd[:, dr * 4 + hd:dr * 4 + hd + 1],
                                in1=h_tm[:, t_, hd, :], op0=ALU.mult, op1=ALU.add),
                                reads=[bnd[hd // 2], brd, bhh[t_][hd]], writes=[bhh[t_][hd]])
            if ada_prev[0] is not None:
                ada_evac(ada_prev[0][0], ada_prev[0][1], ada_cols, [bps[7]])
            P.op("dve", lambda e: e.memset(dn[:, 0:1], 0.0), reads=[bx_ for a_ in bhh for bx_ in a_], writes=[b_h, bf("dn0")])
            if l == 0:
                dump("h", h_tm.rearrange("p t h e -> p (t h e)"), [128, NT * 512], [b_h])

            wv, wb_ = w_in_unit(l, C_O)
            proj_tm(wv, wb_, 512, lambda pi, t: P.op("act", lambda e: e.activation(out=o_tm[:, t, :], in_=psv(pi), func=AF.Sigmoid),
                                                     reads=[bps[pi]], writes=[b_o]))

            junk = ftile[0]
            bjunk = [bf(f"junk{hd}") for hd in range(4)]
            for hd in range(4):
                alias(bjunk[hd], bft[0])
            pi_ada = ps_next()
            ps_reserved.add(pi_ada)
            yc_pending = [None]
            for t in range(NT):
                hsq, bhsq = ftile[1 + t % 2], bft[1 + t % 2]
                hss, bhss = hss2[:, t % 2, :], bf(f"hss{t % 2}")
                yct, byct = tmp_b[t % 2], bf(f"tmpb{t % 2}")
                for hd in range(4):
                    P.op("act", lambda e, t=t, hd=hd, hss=hss: e.activation(
                        out=junk[:, hd * 128:(hd + 1) * 128], in_=h_tm[:, t, hd, :], func=AF.Square, accum_out=hss[:, hd:hd + 1]),
                        reads=[b_h], writes=[bjunk[hd], bf(f"hss{t % 2}_{hd}")], extra=[bhss.last_write] + list(bhss.reads))
                P.op("act", lambda e, hss=hss: e.activation(out=hss, in_=hss, func=AF.Ln, scale=1.0 / 128, bias=EPS),
                     reads=[bf(f"hss{t % 2}_{hd}") for hd in range(4)], writes=[bhss])
                P.op("act", lambda e, hss=hss: e.activation(out=hss, in_=hss, func=AF.Exp, scale=-0.5), reads=[bhss], writes=[bhss])
                if yc_pending[0] is not None:
                    ppi, pt = yc_pending[0]
                    P.op("act", lambda e, ppi=ppi, pt=pt: e.activation(
                        out=yc_fm[:, :, pt * 128:(pt + 1) * 128], in_=psv(ppi).rearrange("p (h s) -> p h s", h=4), func=AF.Copy),
                        reads=[bps[ppi]], writes=[b_q])
                    yc_pending[0] = None
                P.op("dve", lambda e, t=t, hsq=hsq, hss=hss: e.tensor_tensor(
                    out=hsq[:].rearrange("p (h e) -> p h e", h=4), in0=h_tm[:, t, :, :],
                    in1=hss.unsqueeze(2).to_broadcast([128, 4, 128]), op=ALU.mult),
                    reads=[b_h, bhss], writes=[bhsq])
                P.op("dve", lambda e, hsq=hsq: e.tensor_tensor(out=hsq[:], in0=hsq[:], in1=gml_rep[:], op=ALU.mult),
                     reads=[bhsq, bf("gml_rep")], writes=[bhsq])
                P.op("dve", lambda e, t=t, hsq=hsq, yct=yct: e.tensor_tensor(out=yct[:], in0=hsq[:], in1=o_tm[:, t, :], op=ALU.mult),
                     reads=[bhsq, b_o], writes=[byct])
                pi = ps_next()
                for hd in range(4):
                    P.op("pe", lambda e, pi=pi, hd=hd, yct=yct: e.matmul(
                        psv(pi)[:, hd * 128:(hd + 1) * 128], lhsT=yct[:, hd * 128:(hd + 1) * 128], rhs=ident_bf[:], start=True, stop=True),
                        reads=[byct, bf("ident_bf")], writes=[bps[pi]])
                yc_pending[0] = (pi, t)
                if t % 2 == 1 and t // 2 < len(ada_g):
                    u = t // 2
                    ada_mm(ada_g[u][0], ada_g[u][1], psv(pi_ada)[:, 4 * u:4 * u + 4], [bps[pi_ada]])
            ppi, pt = yc_pending[0]
            P.op("act", lambda e: e.activation(
                out=yc_fm[:, :, pt * 128:(pt + 1) * 128], in_=psv(ppi).rearrange("p (h s) -> p h s", h=4), func=AF.Copy),
                reads=[bps[ppi]], writes=[b_q])
            for u, (al, ang) in enumerate(ada_g):
                ada_evac(al, ang, psv(pi_ada)[:, 4 * u:4 * u + 4], [bps[pi_ada]])
            ps_reserved.discard(pi_ada)
            if l == 0:
                dump("y_c", yc_fm, [128, 4, T], [b_q], BF16)

            ybr = [(y_a, b_ya), (yb_fm, b_su), (yc_fm, b_q)]
            for cg in range(2):
                for r in range(3):
                    wbv, wbb = w_in_unit(l, C_BR + r * D + cg * 512)
                    wrv, wrb = load_w(w_branch[l, r, :, cg * 512:(cg + 1) * 512].rearrange("(k p) n -> p k n", p=128), 4, 512)
                    yv, yb_ = ybr[r]
                    for n4 in range(4):
                        for h in range(2):
                            hs = slice(h * 512, (h + 1) * 512)
                            pg = ps_next()
                            for kc in range(KC):
                                P.op("pe", lambda e, pg=pg, kc=kc, n4=n4, hs=hs, wbv=wbv: e.matmul(
                                    psv(pg), lhsT=wbv[:, kc, n4 * 128:(n4 + 1) * 128], rhs=u_fm[:, kc, hs],
                                    start=(kc == 0), stop=(kc == KC - 1)), reads=[wbb, bu[h]], writes=[bps[pg]])
                            pb = ps_next()
                            for kc in range(4):
                                P.op("pe", lambda e, pb=pb, kc=kc, n4=n4, hs=hs, wrv=wrv, yv=yv: e.matmul(
                                    psv(pb), lhsT=wrv[:, kc, n4 * 128:(n4 + 1) * 128], rhs=yv[:, kc, hs],
                                    start=(kc == 0), stop=(kc == 3)), reads=[wrb, yb_], writes=[bps[pb]])
                            sg, bsg = ftile[(n4 * 2 + h) % 3], bft[(n4 * 2 + h) % 3]
                            P.op("act", lambda e, pg=pg, sg=sg: e.activation(out=sg[:], in_=psv(pg), func=AF.Sigmoid),
                                 reads=[bps[pg]], writes=[bsg])
                            ba = bacc[n4][h]
                            if r == 0:
                                P.op("dve", lambda e, pb=pb, sg=sg, n4=n4, hs=hs: e.tensor_tensor(
                                    out=acc_fm[:, n4, hs], in0=psv(pb), in1=sg[:], op=ALU.mult),
                                    reads=[bps[pb], bsg], writes=[ba])
                            else:
                                P.op("dve", lambda e, pb=pb, sg=sg: e.tensor_tensor(out=sg[:], in0=psv(pb), in1=sg[:], op=ALU.mult),
                                     reads=[bps[pb], bsg], writes=[bsg])
                                if r == 1:
                                    P.op("dve", lambda e, sg=sg, n4=n4, hs=hs: e.tensor_tensor(
                                        out=acc_fm[:, n4, hs], in0=acc_fm[:, n4, hs], in1=sg[:], op=ALU.add),
                                        reads=[bsg, ba], writes=[ba])
                                else:
                                    P.op("dve", lambda e, sg=sg, n4=n4, hs=hs, cg=cg: e.tensor_tensor(
                                        out=merged[:, cg * 4 + n4, hs], in0=acc_fm[:, n4, hs], in1=sg[:], op=ALU.add),
                                        reads=[bsg, ba], writes=[b_mrg])
            if l == 0:
                dump("merged", merged, [128, 8, T], [b_mrg], BF16)

            for cg in range(2):
                wv, wb_ = load_w(w_out[l, :, cg * 512:(cg + 1) * 512].rearrange("(k p) n -> p k n", p=128), KC, 512)
                for n4 in range(4):
                    n = cg * 4 + n4
                    for h in range(2):
                        hs = slice(h * 512, (h + 1) * 512)
                        pi = ps_next()
                        for kc in range(KC):
                            P.op("pe", lambda e, pi=pi, kc=kc, n4=n4, hs=hs, wv=wv: e.matmul(
                                psv(pi), lhsT=wv[:, kc, n4 * 128:(n4 + 1) * 128], rhs=merged[:, kc, hs],
                                start=(kc == 0), stop=(kc == KC - 1)), reads=[wb_, b_mrg], writes=[bps[pi]])
                        P.op("dve", lambda e, pi=pi, n=n, hs=hs: e.scalar_tensor_tensor(
                            out=x_fm[:, n, hs], in0=psv(pi), scalar=modfm[:, l, 16 + n:17 + n], in1=x_fm[:, n, hs],
                            op0=ALU.mult, op1=ALU.add), reads=[bps[pi], bmod[l][1], bx[n][h]], writes=[bx[n][h]])
            if l == 0:
                dump("xmid", x_fm[:].rearrange("p c t -> p (c t)"), [128, KC * T], [b for bb in bx for b in bb])

            rmsnorm_to_u(l, 1)
            for kg in range(4):
                hb, bhb = hff[kg % 2], bhff[kg % 2]
                for cg in range(2):
                    wv, wb_ = load_w(w_ff1[l, :, kg * 1024 + cg * 512: kg * 1024 + (cg + 1) * 512].rearrange("(k p) n -> p k n", p=128), KC, 512)
                    for n4 in range(4):
                        for h in range(2):
                            hs = slice(h * 512, (h + 1) * 512)
                            pi = ps_next()
                            for kc in range(KC):
                                P.op("pe", lambda e, pi=pi, kc=kc, n4=n4, hs=hs, wv=wv: e.matmul(
                                    psv(pi), lhsT=wv[:, kc, n4 * 128:(n4 + 1) * 128], rhs=u_fm[:, kc, hs],
                                    start=(kc == 0), stop=(kc == KC - 1)), reads=[wb_, bu[h]], writes=[bps[pi]])
                            tb, btb = tmp_b[(n4 * 2 + h) % 2], bf(f"tmpb{(n4 * 2 + h) % 2}")
                            P.op("act", lambda e, pi=pi, tb=tb: e.activation(out=tb[:], in_=psv(pi), func=AF.Relu),
                                 reads=[bps[pi]], writes=[btb])
                            P.op("dve", lambda e, tb=tb, hb=hb, cg=cg, n4=n4, hs=hs: e.tensor_tensor(
                                out=hb[:, cg * 4 + n4, hs], in0=tb[:], in1=tb[:], op=ALU.mult), reads=[btb], writes=[bhb])
                for cg in range(2):
                    wv, wb_ = load_w(w_ff2[l, kg * 1024:(kg + 1) * 1024, cg * 512:(cg + 1) * 512].rearrange("(k p) n -> p k n", p=128), KC, 512)
                    for n4 in range(4):
                        n = cg * 4 + n4
                        for h in range(2):
                            hs = slice(h * 512, (h + 1) * 512)
                            pi = ps_next()
                            for kc in range(KC):
                                P.op("pe", lambda e, pi=pi, kc=kc, n4=n4, hs=hs, wv=wv, hb=hb: e.matmul(
                                    psv(pi), lhsT=wv[:, kc, n4 * 128:(n4 + 1) * 128], rhs=hb[:, kc, hs],
                                    start=(kc == 0), stop=(kc == KC - 1)), reads=[wb_, bhb], writes=[bps[pi]])
                            P.op("dve", lambda e, pi=pi, n=n, hs=hs: e.scalar_tensor_tensor(
                                out=x_fm[:, n, hs], in0=psv(pi), scalar=modfm[:, l, 40 + n:41 + n], in1=x_fm[:, n, hs],
                                op0=ALU.mult, op1=ALU.add), reads=[bps[pi], bmod[l][1], bx[n][h]], writes=[bx[n][h]])
            dump(f"x{l}", x_fm[:].rearrange("p c t -> p (c t)"), [128, KC * T], [b for bb in bx for b in bb])

        for l in range(depth):
            layer(l)

        for h in range(2):
            hs = slice(h * 512, (h + 1) * 512)
            ms_rstd(h)
            for c in range(KC):
                P.op("dve", lambda e, c=c, hs=hs: e.scalar_tensor_tensor(
                    out=x_fm[:, c, hs], in0=x_fm[:, c, hs], scalar=gfin[:, c:c + 1], in1=ftile[2][:], op0=ALU.mult, op1=ALU.mult),
                    reads=[bx[c][h], bft[2], bf("gfin")], writes=[bx[c][h]])
        for t in range(NT):
            ot, bo = io_tm[t % 2], bio[t % 2]
            h = t // 4
            for cg in range(2):
                pi = ps_next()
                for c4 in range(4):
                    c = cg * 4 + c4
                    P.op("pe", lambda e, pi=pi, c4=c4, c=c, t=t: e.matmul(
                        psv(pi)[:, c4 * 128:(c4 + 1) * 128], lhsT=x_fm[:, c, t * 128:(t + 1) * 128], rhs=ident[:], start=True, stop=True),
                        reads=[bx[c][h], bf("ident")], writes=[bps[pi]])
                copy_evac(ot[:, cg * 512:(cg + 1) * 512], psv(pi), [bps[pi]], [bo])
            out_toks.append(P.dma("sp", y_out[t * 128:(t + 1) * 128, :], ot, reads=[bo]))

        P.wait_all("sp", out_toks + dbg_toks)
        P.emit(block)
        n_inst = P.n_inst
    return nc, n_inst


def _centred_weights(n, w):
    t = np.arange(n)
    lo = np.clip(t - w // 2, 0, n)
    hi = np.clip(t + (w - w // 2), 0, n)
    m = np.zeros((n, n), np.float64)
    for i in range(n):
        m[lo[i]:hi[i], i] = 1.0 / (hi[i] - lo[i])
    return m


def _pool_mats(grid):
    out = np.zeros((4, T, T), np.float32)
    for g, w in enumerate((2, 4, 8, 16)):
        if grid:
            mc = _centred_weights(64, w)
            mr = _centred_weights(16, w)
            m = np.kron(mr, mc)
        else:
            m1 = _centred_weights(256, w)
            m = np.zeros((T, T))
            for k in range(4):
                m[k * 256:(k + 1) * 256, k * 256:(k + 1) * 256] = m1
        out[g] = (m - np.eye(T)).astype(np.float32)
    return out.astype(ml_dtypes.bfloat16)


_CONST = {}


def _consts():
    if not _CONST:
        s = np.arange(128)
        mf = (s[:, None] <= s[None, :]).astype(np.float32)
        mb = (s[:, None] >= s[None, :]).astype(np.float32)
        _CONST["masks"] = np.ascontiguousarray(np.stack([mf, mb], axis=1).reshape(128, 256))
        _CONST["ident"] = np.eye(128, dtype=np.float32)
        _CONST["pool_grid"] = _pool_mats(True)
        _CONST["pool_seq"] = _pool_mats(False)
    return _CONST


_PROG = {}


def kernel(x_prompt, x_sample, state_C, state_n, state_m, c, c_ctx, w_ada, b_ada, g_norm1, g_norm2, w_in,
           b_gates, w_pool, pool_scale, g_sgu, w_sgu, b_sgu, g_mlstm, w_branch, w_out, w_ff1, w_ff2, g_final,
           _depth=DEPTH, _dbg=()):
    f = np.float32
    A = lambda a: np.ascontiguousarray(np.asarray(a, dtype=f))
    x_prompt, x_sample = A(x_prompt), A(x_sample)
    state_C, state_n, state_m = A(state_C), A(state_n), A(state_m)
    cst = _consts()

    def fm(vec, nchunk):
        return np.ascontiguousarray(np.asarray(vec, f).reshape(nchunk, 128).T)

    shared = {
        "masks": cst["masks"], "ident": cst["ident"],
        "w_ada": A(w_ada),
        "bada": np.ascontiguousarray(A(b_ada).reshape(DEPTH, 48, 128).transpose(2, 0, 1).reshape(128, DEPTH * 48)),
        "gn": np.ascontiguousarray(np.stack([A(g_norm1).reshape(DEPTH, KC, 128), A(g_norm2).reshape(DEPTH, KC, 128)], axis=1)
                                   .transpose(3, 0, 1, 2).reshape(128, DEPTH * 2 * KC)),
        "w_in": A(w_in), "b_gates": A(b_gates), "w_pool": A(w_pool),
        "pscale": np.ascontiguousarray(A(pool_scale).reshape(DEPTH, 4, 128).transpose(2, 0, 1).reshape(128, DEPTH * 4)),
        "gsgu": np.ascontiguousarray(A(g_sgu).reshape(DEPTH, 4, 128).transpose(2, 0, 1).reshape(128, DEPTH * 4)),
        "w_sguT": np.ascontiguousarray(A(w_sgu).transpose(0, 3, 1, 2).reshape(DEPTH, 128, 512)),
        "b_sgu": A(b_sgu).reshape(DEPTH, 512), "g_mlstm": A(g_mlstm),
        "w_branch": A(w_branch), "w_out": A(w_out), "w_ff1": A(w_ff1), "w_ff2": A(w_ff2),
        "gfin": fm(g_final, KC),
    }
    zeros_c = np.zeros((DEPTH, 4, 128, 8 * 129), f)
    zeros_m = np.zeros((128, DEPTH * 4 * 8), f)
    in_maps = []
    for core in range(8):
        m = dict(shared)
        if core < 2:
            b = core
            m["x"] = x_sample[b]
            m["cond"] = fm(A(c)[b], KC)
            keep = np.zeros((128, 4), f)
            keep[:, 1:] = 1.0
            ci = np.zeros((DEPTH, 4, 128, 8, 129), f)
            ci[:, 0, :, :, 0:128] = state_C[b].transpose(0, 3, 1, 2, 4).reshape(DEPTH, 128, 8, 128)
            ci[:, 0, :, :, 128] = state_n[b].transpose(0, 3, 1, 2).reshape(DEPTH, 128, 8)
            mi = np.zeros((DEPTH, 4, 8), f)
            mi[:, 0, :] = state_m[b].reshape(DEPTH, 8)
            m["keep"] = keep
            m["cinit"] = ci.reshape(DEPTH, 4, 128, 8 * 129)
            m["minit"] = np.ascontiguousarray(np.broadcast_to(mi.reshape(1, -1), (128, DEPTH * 32)))
            m["poolm"] = cst["pool_grid"]
        else:
            g = min(core, 5) - 2
            m["x"] = np.ascontiguousarray(x_prompt[4 * g:4 * g + 4].reshape(T, D))
            m["cond"] = fm(c_ctx, KC)
            m["keep"] = np.zeros((128, 4), f)
            m["cinit"] = zeros_c
            m["minit"] = zeros_m
            m["poolm"] = cst["pool_seq"]
        in_maps.append(m)

    key = (_depth, tuple(_dbg))
    if key not in _PROG:
        _PROG[key] = build_program(_depth, _dbg)
    nc, n_inst = _PROG[key]
    res = run_bass_kernel_spmd(nc, in_maps, core_ids=list(range(8)))
    R = res.results

    y_sample = np.stack([R[0]["y"], R[1]["y"]], axis=0).astype(f)
    y_prompt = np.concatenate([R[cidx]["y"].reshape(4, 256, D) for cidx in range(2, 6)], axis=0).astype(f)
    B = x_prompt.shape[0]
    new_C = np.zeros((B, DEPTH, 2, 4, 128, 128), f)
    new_n = np.zeros((B, DEPTH, 2, 4, 128), f)
    new_m = np.zeros((B, DEPTH, 2, 4), f)
    for cidx in range(2, 6):
        co = R[cidx]["c_out"].reshape(DEPTH, 4, 2, 128, 4, 129)
        mo = R[cidx]["m_out"].reshape(DEPTH, 4, 2, 4)
        for slot in range(4):
            for dr in range(2):
                s = 4 * (cidx - 2) + (slot if dr == 0 else 3 - slot)
                new_C[s, :, dr] = co[:, slot, dr, :, :, 0:128].transpose(0, 2, 1, 3)
                new_n[s, :, dr] = co[:, slot, dr, :, :, 128].transpose(0, 2, 1)
                new_m[s, :, dr] = mo[:, slot, dr]
    if _dbg:
        kernel.dbg = [{k: v for k, v in r.items() if k.startswith("dbg_")} for r in R]
    return (y_prompt, y_sample, new_C, new_n, new_m)
```
